# Optimizing a Trainium2 kernel written in Bass

```python
import jax
import jax.numpy as jnp
from jax import lax
import numpy as np

D_MODEL = 1024
BATCH = 8
SEQ = 8192
DEPTH = 1

GRID_W = 64
CTX_LEN = 256
A_HEAD_DIM = 64
A_HEADS = D_MODEL // A_HEAD_DIM
A_WIDTH = A_HEADS * A_HEAD_DIM
A_DECAY_LORA = 64
A_ICLR_LORA = 64
A_CONV = 3
R_QK_DIM = 256
R_HEADS = D_MODEL // R_QK_DIM
R_V_DIM = 2 * R_QK_DIM
R_QK_WIDTH = R_HEADS * R_QK_DIM
R_V_WIDTH = R_HEADS * R_V_DIM
R_CHUNK = 128
ROPE_BASE = 10000.0
NORM_EPS = 1e-6
A_GN_EPS = 64e-5
R_GN_EPS = 1e-5
L2_EPS = 1e-12
IN_SIZES = (3 * A_WIDTH, A_WIDTH, 2 * A_DECAY_LORA, 2 * A_ICLR_LORA,
            R_QK_WIDTH, R_QK_WIDTH, R_V_WIDTH, R_V_WIDTH, D_MODEL, D_MODEL)
N_IN = sum(IN_SIZES)

kernel_name = 'hybrid_rwkv7_retention_prefix_block'


def rms_norm(x, w):
    xf = x.astype(jnp.float32)
    y = xf * lax.rsqrt(jnp.mean(xf * xf, axis=-1, keepdims=True) + NORM_EPS)
    return (y * w).astype(x.dtype)


def head_norm(x, w, b, heads, eps):
    B, L, C = x.shape
    xf = x.astype(jnp.float32).reshape(B, L, heads, C // heads)
    mu = jnp.mean(xf, axis=-1, keepdims=True)
    var = jnp.mean(jnp.square(xf - mu), axis=-1, keepdims=True)
    y = ((xf - mu) * lax.rsqrt(var + eps)).reshape(B, L, C)
    return (y * w + b).astype(x.dtype)


def split_cols(p):
    offs, o = [], 0
    for s in IN_SIZES[:-1]:
        o += s
        offs.append(o)
    return jnp.split(p, offs, axis=-1)


def conv_centred(x, w):
    L = x.shape[1]
    pad = A_CONV // 2
    xp = jnp.pad(x, ((0, 0), (pad, A_CONV - 1 - pad), (0, 0)))
    y = xp[:, 0:L] * w[0]
    for j in range(1, A_CONV):
        y = y + xp[:, j:j + L] * w[j]
    return y


def rope1d(x, pos):
    d = x.shape[-1]
    half = d // 2
    freqs = ROPE_BASE ** (-jnp.arange(half, dtype=jnp.float32) / half)
    ang = pos.astype(jnp.float32)[:, None] * freqs[None, :]
    cos = jnp.cos(ang)[None, :, None, :]
    sin = jnp.sin(ang)[None, :, None, :]
    xf = x.astype(jnp.float32)
    x1, x2 = xf[..., :half], xf[..., half:]
    return jnp.concatenate([x1 * cos - x2 * sin, x1 * sin + x2 * cos], axis=-1)


def rope2d(x, rows, cols):
    half = x.shape[-1] // 2
    return jnp.concatenate([rope1d(x[..., :half], rows), rope1d(x[..., half:], cols)], axis=-1)


def rwkv7_scan(r, w, k, v, kk, a, S0, reverse):
    tm = lambda t: jnp.moveaxis(t.astype(jnp.float32), 1, 0)
    with_out = r is not None
    xs = (tm(w), tm(k), tm(v), tm(kk), tm(kk * a)) + ((tm(r),) if with_out else ())

    def step(S, inp):
        w_t, k_t, v_t, kk_t, b_t = inp[:5]
        sa = jnp.einsum('bhvk,bhk->bhv', S, kk_t)
        S = (S * w_t[:, :, None, :] - sa[..., None] * b_t[:, :, None, :]
             + v_t[..., None] * k_t[:, :, None, :])
        o = jnp.einsum('bhvk,bhk->bhv', S, inp[5]) if with_out else None
        return S, o

    S, o = lax.scan(step, S0, xs, reverse=reverse)
    return S, (jnp.moveaxis(o, 0, 1) if with_out else None)


def rwkv_branch(rkv, g, lo_w, lo_a, S0, with_out,
                conv_w, w_up, w0, a_up, a0, k_k, k_a, r_k, ln_w, ln_b, w_o):
    B, L, _ = rkv.shape
    heads = lambda t: t.reshape(B, L, A_HEADS, A_HEAD_DIM)
    r, k, v = jnp.split(conv_centred(rkv, conv_w), 3, axis=-1)
    kk = heads((k * k_k).astype(jnp.float32))
    kk = kk * lax.rsqrt(jnp.sum(kk * kk, axis=-1, keepdims=True) + L2_EPS)
    lo_w = lo_w.astype(jnp.float32)
    lo_a = lo_a.astype(jnp.float32)
    rh = heads(r) if with_out else None
    vh = heads(v)
    states, outs, bons = [], [], []
    for d, (lw, la) in enumerate(zip(jnp.split(lo_w, 2, axis=-1), jnp.split(lo_a, 2, axis=-1))):
        z = w0[d] + jnp.tanh(lw) @ w_up[d]
        w = jnp.exp(-jnp.exp(-jax.nn.softplus(-z) - 0.5))
        a = jax.nn.sigmoid(a0[d] + la @ a_up[d])
        kd = heads(k * (1.0 + (a - 1.0) * k_a))
        S, o = rwkv7_scan(rh, heads(w), kd, vh, kk, heads(a), S0[d], reverse=(d == 1))
        states.append(S)
        if with_out:
            outs.append(o)
            bons.append(jnp.sum(rh * kd * r_k, axis=-1, keepdims=True) * vh)
    if not with_out:
        return None, states
    o = head_norm((outs[0] + outs[1]).reshape(B, L, A_WIDTH), ln_w, ln_b, A_HEADS, A_GN_EPS)
    o = o + (bons[0] + bons[1]).reshape(B, L, A_WIDTH).astype(o.dtype)
    return ((o * jax.nn.silu(g)) @ w_o).astype(g.dtype), states


def retention_chunked(q, k, v, lg, S0):
    B, H, L, _ = k.shape
    dv = v.shape[-1]
    n = L // R_CHUNK
    chunks = lambda t: jnp.moveaxis(t.reshape(B, H, n, R_CHUNK, t.shape[-1]), 2, 0)
    idx = jnp.arange(R_CHUNK, dtype=jnp.float32)
    diff = idx[:, None] - idx[None, :]
    intra = jnp.exp(jnp.where(diff >= 0, lg[:, None, None] * diff, -jnp.inf))
    q_dec = jnp.exp(lg[:, None] * (idx + 1.0))[None, :, :, None]
    k_dec = jnp.exp(lg[:, None] * (R_CHUNK - 1.0 - idx))[None, :, :, None]
    c_dec = jnp.exp(lg * R_CHUNK)[None, :, None, None]
    with_out = q is not None
    xs = (chunks(k), chunks(v)) + ((chunks(q),) if with_out else ())

    def step(S, inp):
        kc, vc = inp[0], inp[1]
        S_new = S * c_dec + jnp.einsum('bhck,bhcv->bhkv', kc * k_dec, vc)
        if not with_out:
            return S_new, None
        qc = inp[2]
        scores = jnp.einsum('bhik,bhjk->bhij', qc, kc) * intra
        o = (jnp.einsum('bhij,bhjv->bhiv', scores, vc)
             + jnp.einsum('bhik,bhkv->bhiv', qc * q_dec, S))
        return S_new, o

    S, o = lax.scan(step, S0, xs)
    if with_out:
        o = jnp.moveaxis(o, 0, 2).reshape(B, H, L, dv)
    return o, S


def retention_branch(q, k, v, g, pos, S0, with_out, r_decay, ln_w, ln_b, w_o):
    B, L, _ = k.shape

    def heads(t, dh, rotate):
        t = t.reshape(B, L, R_HEADS, dh)
        if rotate and pos is not None:
            t = rope2d(t, pos[0], pos[1])
        return jnp.moveaxis(t.astype(jnp.float32), 2, 1)

    lg = -jnp.exp(r_decay.astype(jnp.float32))
    kh = heads(k, R_QK_DIM, True) * (R_QK_DIM ** -0.5)
    vh = heads(v, R_V_DIM, False)
    qh = heads(q, R_QK_DIM, True) if with_out else None
    flip = lambda t: None if t is None else t[:, :, ::-1]
    o_f, S_f = retention_chunked(qh, kh, vh, lg[0], S0[0])
    o_b, S_b = retention_chunked(flip(qh), flip(kh), flip(vh), lg[1], S0[1])
    if not with_out:
        return None, [S_f, S_b]
    o = jnp.moveaxis(o_f + flip(o_b), 1, 2).reshape(B, L, R_V_WIDTH)
    o = head_norm(o, ln_w, ln_b, R_HEADS, R_GN_EPS)
    return ((o * jax.nn.silu(g)) @ w_o).astype(g.dtype), [S_f, S_b]


def setup_inputs(seed: int = 0) -> dict:
    key = jax.random.key(seed)
    ks = jax.random.split(key, 26)
    f32 = jnp.float32
    nrm = lambda k, shape, s: jax.random.normal(k, shape, f32) * s
    D = D_MODEL
    ramp = -6.0 + 5.0 * jnp.linspace(0.0, 1.0, A_WIDTH, dtype=f32) ** 0.9
    base_decay = jnp.log(-jnp.log(1.0 - 2.0 ** (-5.0 - jnp.arange(R_HEADS, dtype=f32))))
    conv_base = jnp.array([0.25, 0.75, 0.25], f32)[:, None]
    return {
        'x': nrm(ks[0], (BATCH, SEQ, D), 1.0),
        'c': nrm(ks[1], (BATCH, D), 1.0),
        'ctx': nrm(ks[2], (BATCH, CTX_LEN, D), 1.0),
        'c_ctx': nrm(ks[3], (D,), 1.0),
        'norm_w': 1.0 + nrm(ks[4], (DEPTH, D), 0.02),
        'ada_w': nrm(ks[5], (DEPTH, D, 3 * D), D ** -0.5),
        'ada_b': nrm(ks[6], (DEPTH, 3 * D), 0.02),
        'w_in': nrm(ks[7], (DEPTH, D, N_IN), D ** -0.5),
        'a_conv': conv_base + nrm(ks[8], (DEPTH, A_CONV, 3 * A_WIDTH), 0.1),
        'a_w_up': nrm(ks[9], (DEPTH, 2, A_DECAY_LORA, A_WIDTH), 0.1),
        'a_w0': ramp + nrm(ks[10], (DEPTH, 2, A_WIDTH), 0.1),
        'a_a_up': nrm(ks[11], (DEPTH, 2, A_ICLR_LORA, A_WIDTH), 0.5 * A_ICLR_LORA ** -0.5),
        'a_a0': nrm(ks[12], (DEPTH, 2, A_WIDTH), 0.1),
        'a_k_k': 0.85 + nrm(ks[13], (DEPTH, A_WIDTH), 0.05),
        'a_k_a': 1.0 + nrm(ks[14], (DEPTH, A_WIDTH), 0.05),
        'a_r_k': nrm(ks[15], (DEPTH, A_HEADS, A_HEAD_DIM), 0.1),
        'a_ln_w': 1.0 + nrm(ks[16], (DEPTH, A_WIDTH), 0.02),
        'a_ln_b': nrm(ks[17], (DEPTH, A_WIDTH), 0.02),
        'a_w_out': nrm(ks[18], (DEPTH, A_WIDTH, D), A_WIDTH ** -0.5),
        'r_decay': base_decay + nrm(ks[19], (DEPTH, 2, R_HEADS), 0.05),
        'r_ln_w': 1.0 + nrm(ks[20], (DEPTH, R_V_WIDTH), 0.02),
        'r_ln_b': nrm(ks[21], (DEPTH, R_V_WIDTH), 0.02),
        'r_w_out': nrm(ks[22], (DEPTH, R_V_WIDTH, D), R_V_WIDTH ** -0.5),
        'w_out': nrm(ks[23], (DEPTH, D, D), D ** -0.5),
        'final_norm_w': 1.0 + nrm(ks[24], (D,), 0.02),
    }


def reference(x, c, ctx, c_ctx, norm_w, ada_w, ada_b, w_in, a_conv, a_w_up, a_w0, a_a_up, a_a0,
              a_k_k, a_k_a, a_r_k, a_ln_w, a_ln_b, a_w_out, r_decay, r_ln_w, r_ln_b, r_w_out,
              w_out, final_norm_w):
    B, L, _ = x.shape
    ROWS = L // GRID_W
    rows = jnp.repeat(jnp.arange(ROWS), GRID_W)
    cols = jnp.tile(jnp.arange(GRID_W), ROWS)
    Bc = ctx.shape[0]
    Sa0 = [jnp.zeros((Bc, A_HEADS, A_HEAD_DIM, A_HEAD_DIM), jnp.float32)] * 2
    Sr0 = [jnp.zeros((Bc, R_HEADS, R_QK_DIM, R_V_DIM), jnp.float32)] * 2
    for l in range(DEPTH):
        need_ctx = l < DEPTH - 1
        shift, scale, gate = jnp.split(jax.nn.silu(c) @ ada_w[l] + ada_b[l], 3, axis=-1)
        shift_c, scale_c, gate_c = jnp.split(jax.nn.silu(c_ctx) @ ada_w[l] + ada_b[l], 3, axis=-1)
        u = rms_norm(x, norm_w[l]) * (1.0 + scale[:, None, :]) + shift[:, None, :]
        uc = rms_norm(ctx, norm_w[l]) * (1.0 + scale_c) + shift_c
        rkv, ga, lw, la, q, k, v, gr, ma, mb = split_cols(u @ w_in[l])
        rkv_c, ga_c, lw_c, la_c, q_c, k_c, v_c, gr_c, ma_c, mb_c = split_cols(uc @ w_in[l])
        a_args = (a_conv[l], a_w_up[l], a_w0[l], a_a_up[l], a_a0[l], a_k_k[l], a_k_a[l], a_r_k[l],
                  a_ln_w[l], a_ln_b[l], a_w_out[l])
        r_args = (r_decay[l], r_ln_w[l], r_ln_b[l], r_w_out[l])
        ya_c, Sa_c = rwkv_branch(rkv_c, ga_c, lw_c, la_c, Sa0, need_ctx, *a_args)
        yr_c, Sr_c = retention_branch(q_c, k_c, v_c, gr_c, None, Sr0, need_ctx, *r_args)
        ya, _ = rwkv_branch(rkv, ga, lw, la, Sa_c, True, *a_args)
        yr, _ = retention_branch(q, k, v, gr, (rows, cols), Sr_c, True, *r_args)
        merged = jax.nn.sigmoid(ma) * ya + jax.nn.sigmoid(mb) * yr
        x = x + gate[:, None, :] * (merged @ w_out[l])
        if need_ctx:
            merged_c = jax.nn.sigmoid(ma_c) * ya_c + jax.nn.sigmoid(mb_c) * yr_c
            ctx = ctx + gate_c * (merged_c @ w_out[l])
    return rms_norm(x, final_norm_w)
```

```python
from contextlib import ExitStack
import math
import numpy as np
import ml_dtypes
import concourse.bass as bass
import concourse.mybir as mybir
from concourse.bass_utils import run_bass_kernel_spmd

F32 = mybir.dt.float32
BF16 = mybir.dt.bfloat16
ALU = mybir.AluOpType
AF = mybir.ActivationFunctionType
AX = mybir.AxisListType

D = 1024
NCTX = 256
C = 128
N_IN = 12544
KAPPA = math.exp(-0.5)
CO = dict(rkv=0, ga=3072, lw=4096, la=4224, q=4352, k=5376, v=6400, gr=8448, ma=10496, mb=11520)
RO = dict(rkv=0, ga=3072, lora=4096, q=4352, k=5376, gr=6400, ma=8448, mb=9472)
NROW = 10496
PCO = {}
_o = 0
for _n, _w in (("norm_w", 8), ("ada_b", 24), ("conv", 72), ("w0", 16), ("a0", 16), ("k_k", 8), ("k_a", 8),
               ("r_k", 8), ("a_ln_w", 8), ("a_ln_b", 8), ("r_ln_w", 16), ("r_ln_b", 16), ("c", 8), ("c_ctx", 8)):
    PCO[_n] = _o
    _o += _w
NPC = _o


class Buf:
    __slots__ = ("name", "w", "r", "t", "sem")

    def __init__(self, name, t=None, sem=None):
        self.name = name
        self.w = None
        self.r = {}
        self.t = t
        self.sem = sem

    def __getitem__(self, idx):
        return self.t[idx]


class Ring:
    def __init__(self, bufs):
        self.b = bufs
        self.i = 0

    def next(self):
        b = self.b[self.i % len(self.b)]
        self.i += 1
        return b


class Sched:
    CE = ("pe", "act", "dve", "pool")
    QE = ("pe", "act", "dve", "pool", "sp")

    def __init__(self, nc, stack):
        self.nc = nc
        self.q = {e: [] for e in self.QE}
        self.tick = {e: 0 for e in self.CE}
        self.known = {e: {} for e in self.QE}
        self.semh = {}
        for e in self.CE:
            self.semh[e] = stack.enter_context(nc.semaphore("s_" + e))
        self.dma_cnt = {}
        self.n_dma = 0
        self.stack = stack
        self.n_instr = 0
        self.free_sems = {"ld": [], "st": []}
        self.phase_sems = []

    def begin_phase(self):
        self.phase_sems = []

    def end_phase(self):
        for kind, k in self.phase_sems:
            self.free_sems[kind].append(k)
        self.phase_sems = []

    def new_dma_sem(self, kind="ld"):
        if self.free_sems[kind]:
            k = self.free_sems[kind].pop()
            self.phase_sems.append((kind, k))
            return k
        k = "d%d" % self.n_dma
        self.phase_sems.append((kind, k))
        self.n_dma += 1
        self.semh[k] = self.stack.enter_context(self.nc.semaphore("s_" + k))
        self.dma_cnt[k] = 0
        return k

    def _deps(self, reads, writes):
        d = {}
        for b in reads:
            if b.w is not None and d.get(b.w[0], -1) < b.w[1]:
                d[b.w[0]] = b.w[1]
        for b in writes:
            if b.w is not None and d.get(b.w[0], -1) < b.w[1]:
                d[b.w[0]] = b.w[1]
            for k, v in b.r.items():
                if d.get(k, -1) < v:
                    d[k] = v
        return d

    def _waits(self, E, d):
        kn = self.known[E]
        for k, v in d.items():
            if k == E:
                if E == "pe":
                    continue
            if kn.get(k, -1) >= v:
                continue
            kn[k] = v
            sem = self.semh[k]
            self.q[E].append(lambda eng, sem=sem, v=v: eng.wait_ge(sem, v))
            self.n_instr += 1

    def op(self, E, fn, reads=(), writes=(), inc=True):
        self._waits(E, self._deps(reads, writes))
        v = self.tick[E] + 1
        if inc:
            self.tick[E] = v
            sem = self.semh[E]
            self.q[E].append(lambda eng, fn=fn, sem=sem: fn(eng).then_inc(sem, 1))
        else:
            self.q[E].append(lambda eng, fn=fn: fn(eng))
        self.n_instr += 1
        for b in reads:
            b.r[E] = v
        for b in writes:
            b.w = (E, v)
            b.r = {}

    def dma(self, Q, semk, out_ap, in_ap, reads=(), writes=()):
        self._waits(Q, self._deps(reads, writes))
        self.dma_cnt[semk] += 16
        v = self.dma_cnt[semk]
        sem = self.semh[semk]
        self.q[Q].append(lambda eng, o=out_ap, i=in_ap, sem=sem:
                         eng.dma_start(out=o, in_=i).then_inc(sem, 16))
        self.n_instr += 1
        for b in reads:
            b.r[semk] = v
        for b in writes:
            b.w = (semk, v)
            b.r = {}

    def barrier(self):
        for E in self.QE:
            d = {e: self.tick[e] for e in self.CE if self.tick[e] > 0}
            for k, v in self.dma_cnt.items():
                if v > 0:
                    d[k] = v
            kn = self.known[E]
            for k, v in d.items():
                if k == E:
                    continue
                if kn.get(k, -1) >= v:
                    continue
                kn[k] = v
                sem = self.semh[k]
                self.q[E].append(lambda eng, sem=sem, v=v: eng.wait_ge(sem, v))

    def replay(self):
        nc = self.nc
        with nc.Block() as block:
            @block.tensor
            def _(e):
                for f in self.q["pe"]:
                    f(e)

            @block.scalar
            def _(e):
                for f in self.q["act"]:
                    f(e)

            @block.vector
            def _(e):
                for f in self.q["dve"]:
                    f(e)

            @block.gpsimd
            def _(e):
                for f in self.q["pool"]:
                    f(e)

            @block.sync
            def _(e):
                for f in self.q["sp"]:
                    f(e)
        self.q = {e: [] for e in self.QE}


def build(NT, stop_after=99, dbg=False):
    import os as _os0
    LIGHT = bool(_os0.environ.get("KDBG_LIGHT"))
    nc = bass.Bass("TRN2", target_bir_lowering=False)
    NCH = NT // C
    scratch_kind = "ExternalOutput" if dbg else "Internal"

    def din(name, shape, dt=F32):
        return nc.dram_tensor(name, list(shape), dt, kind="ExternalInput").ap()

    def dscr(name, shape, dt=BF16):
        return nc.dram_tensor(name, list(shape), dt, kind=scratch_kind).ap()

    x_d = din("x", [NT, D])
    ctx_d = din("ctx", [NCTX, D])
    pc_d = din("pcols", [128, NPC])
    adaw_d = din("ada_w", [D, 3 * D] if not LIGHT else [8, 8])
    adabg_d = din("ada_b_gate", [D])
    w_in_d = din("w_in", [D, N_IN] if not LIGHT else [8, 8])
    wup_d = din("a_w_up", [128, D])
    aup_d = din("a_a_up", [128, D])
    awo_d = din("a_w_out", [D, D] if not LIGHT else [8, 8])
    rwo_d = din("r_w_out", [2 * D, D] if not LIGHT else [8, 8])
    wo_d = din("w_out", [D, D] if not LIGHT else [8, 8])
    rdec_d = din("r_decay", [8])
    fnw_d = din("final_norm_w", [D])
    masks_d = din("masks", [128, 8, 512], BF16)
    identb_d = din("identb", [128, 128], BF16)
    ident4_d = din("ident4", [128, 512], BF16)
    lmasks_d = din("lmasks", [2, 128, 7, 512], BF16)
    bones_d = din("blockones", [128, 128], BF16)
    iota_d = din("iota", [128, 8])
    cos_d = din("rope_cos", [NT, 256])
    sin_d = din("rope_sin", [NT, 256])
    y_d = nc.dram_tensor("y", [NT, D], F32, kind="ExternalOutput").ap()

    winb_d = dscr("winb", [D, N_IN])
    projT = {"lat": dscr("projT", [NROW, NT]), "ctx": dscr("projTc", [NROW, NCTX])}
    k_tm = {"lat": dscr("k_tm", [NT, D]), "ctx": dscr("k_tmc", [NCTX, D])}
    v_tm = {"lat": dscr("v_tm", [NT, 2 * D]), "ctx": dscr("v_tmc", [NCTX, 2 * D])}
    obwd_a = dscr("obwd_a", [NT, D])
    bonb_d = dscr("bonb", [D, NT])
    obwd_r = dscr("obwd_r", [NT, 2 * D])
    gall = dscr("gall", [3 * D, NT])
    dbg_oa = dscr("dbg_oa", [NT, D], F32) if dbg else None
    dbg_or = dscr("dbg_or", [NT, 2 * D], F32) if dbg else None

    with ExitStack() as st:
        S = Sched(nc, st)

        uid = [0]

        def sb(stack, name, shape, dt, dma=False):
            uid[0] += 1
            t = stack.enter_context(nc.sbuf_tensor("sb%d_%s" % (uid[0], name), list(shape), dt))
            return Buf(name, t, S.new_dma_sem("st" if dma == "st" else "ld") if dma else None)

        def sbring(stack, name, shape, dt, n, dma=False):
            return Ring([sb(stack, "%s%d" % (name, i), shape, dt, dma) for i in range(n)])

        def psb(stack, name, shape, dt):
            uid[0] += 1
            return Buf(name, stack.enter_context(nc.psum_tensor("ps%d_%s" % (uid[0], name), list(shape), dt)))

        def load(buf, out_ap, in_ap, extra_reads=()):
            S.dma("sp", buf.sem, out_ap, in_ap, reads=list(extra_reads), writes=[buf])

        def store(buf, out_ap, in_ap):
            S.dma("pool", buf.sem, out_ap, in_ap, reads=[buf], writes=[])

        def MM(out, lhsT, rhs, start=True, stop=True, skip=False):
            if skip:
                return lambda e: e.matmul(out, lhsT=lhsT, rhs=rhs, start=start, stop=stop, skip_group_check=True)
            return lambda e: e.matmul(out, lhsT=lhsT, rhs=rhs, start=start, stop=stop)

        def TR(out, in_, ident):
            return lambda e: e.transpose(out, in_, ident)

        def ACT(out, in_, func, scale=1.0, bias=0.0, accum=None):
            if accum is None:
                return lambda e: e.activation(out=out, in_=in_, func=func, bias=bias, scale=scale)
            return lambda e: e.activation(out=out, in_=in_, func=func, bias=bias, scale=scale, accum_out=accum)

        def TT(out, a, b, op):
            return lambda e: e.tensor_tensor(out=out, in0=a, in1=b, op=op)

        def TS(out, a, s1, s2, op0, op1):
            if s2 is None:
                return lambda e: e.tensor_scalar(out=out, in0=a, scalar1=s1, scalar2=None, op0=op0)
            return lambda e: e.tensor_scalar(out=out, in0=a, scalar1=s1, scalar2=s2, op0=op0, op1=op1)

        def STT(out, a, s, b, op0, op1):
            return lambda e: e.scalar_tensor_tensor(out=out, in0=a, scalar=s, in1=b, op0=op0, op1=op1)

        def CP(out, in_):
            return lambda e: e.tensor_copy(out=out, in_=in_)

        def MS(out, val):
            return lambda e: e.memset(out, val)

        def bc3(ap2, n):
            return ap2.unsqueeze(2).to_broadcast([128, ap2.shape[1], n])

        pc = sb(st, "pc", [128, NPC], F32, dma=True)
        dcol = sb(st, "dcol", [128, 48], F32)
        modc = sb(st, "modc", [128, 24, 2], F32)
        masks = sb(st, "masks", [128, 8, 512], BF16, dma=True)
        identb = sb(st, "identb", [128, 128], BF16, dma=True)
        ident4 = sb(st, "ident4", [128, 512], BF16, dma=True)
        bones = sb(st, "bones", [128, 128], BF16, dma=True)
        iota = sb(st, "iota", [128, 8], F32, dma=True)
        rtab = sb(st, "rtab", [128, 8, 8], F32)
        wupb = sb(st, "wupb", [128, D], BF16)
        aupb = sb(st, "aupb", [128, D], BF16)
        gate_row = sb(st, "gate_row", [128, D], F32)
        onesf = sb(st, "onesf", [128, 128], F32)
        S.op("pool", MS(onesf[:], 1.0), writes=[onesf])
        U_S, U_I, L_S, L_I, NU_S, NU_I, NL_S, NL_I = range(8)

        def pcol(name, i):
            o = PCO[name] + i
            return pc[:, o:o + 1]

        def phase0():
            with ExitStack() as ph:
                S.begin_phase()
                load(pc, pc[:], pc_d)
                load(masks, masks[:], masks_d)
                load(identb, identb[:], identb_d)
                load(ident4, ident4[:], ident4_d)
                load(bones, bones[:], bones_d)
                load(iota, iota[:], iota_d)
                adabg = sb(ph, "adabg", [128, D], F32, dma=True)
                load(adabg, adabg[:], adabg_d.partition_broadcast(128))
                rdec = sb(ph, "rdec", [128, 8], F32, dma=True)
                load(rdec, rdec[:], rdec_d.partition_broadcast(128))
                adaw = sb(ph, "adaw", [128, 8, 3 * D], F32, dma=True)
                for kc in range(8):
                    if LIGHT:
                        S.op("pool", MS(adaw[:, kc, :], 0.0), writes=[adaw])
                        continue
                    load(adaw, adaw[:, kc, :], adaw_d[kc * 128:(kc + 1) * 128, :])
                lt = sbring(ph, "lt", [128, D], F32, 2, dma=True)
                for src, dst, eng in ((wup_d, wupb, "dve"), (aup_d, aupb, "pool")):
                    t = lt.next()
                    load(t, t[:], src)
                    S.op(eng, CP(dst[:], t[:]), reads=[t], writes=[dst])
                sc = sb(ph, "sc", [128, 8, 2], F32)
                th = sb(ph, "th0", [128, 16], F32)
                cc = pc[:, PCO["c"]:PCO["c"] + 16]
                S.op("act", ACT(th[:], cc, AF.Tanh, scale=0.5), reads=[pc], writes=[th])
                S.op("dve", TS(th[:], th[:], 0.5, 0.5, ALU.mult, ALU.add), reads=[th], writes=[th])
                S.op("dve", TT(sc[:].rearrange("p k t -> p t k"), th[:].rearrange("p (t k) -> p t k", t=2),
                               cc.rearrange("p (t k) -> p t k", t=2), ALU.mult), reads=[th, pc], writes=[sc])
                pmod = psb(ph, "pmod", [128, 24, 2], F32)
                for fb in range(24):
                    for kc in range(8):
                        S.op("pe", MM(pmod[:, fb, :], adaw[:, kc, fb * 128:(fb + 1) * 128], sc[:, kc, :],
                                      start=(kc == 0), stop=(kc == 7)), reads=[adaw, sc], writes=[pmod], inc=(kc == 7))
                S.op("dve", TT(modc[:], pmod[:], bc3(pc[:, PCO["ada_b"]:PCO["ada_b"] + 24], 2), ALU.add),
                     reads=[pmod, pc], writes=[modc])
                for t in range(2):
                    S.op("dve", TS(dcol[:, 24 + 8 * t:32 + 8 * t], modc[:, 8:16, t], 1.0, None, ALU.add, None),
                         reads=[modc], writes=[dcol])
                    S.op("dve", TT(dcol[:, 8 * t:8 * t + 8], dcol[:, 24 + 8 * t:32 + 8 * t],
                                   pc[:, PCO["norm_w"]:PCO["norm_w"] + 8], ALU.mult), reads=[dcol, pc], writes=[dcol])
                S.op("dve", TS(dcol[:, 16:24], pc[:, PCO["k_a"]:PCO["k_a"] + 8], -1.0, 1.0, ALU.mult, ALU.add),
                     reads=[pc], writes=[dcol])
                scb = sb(ph, "scb", [128, 8, 128], F32)
                for kc in range(8):
                    S.op("act", ACT(scb[:, kc, :], onesf[:], AF.Identity, scale=sc[:, kc, 0:1]),
                         reads=[onesf, sc], writes=[scb])
                pg = [psb(ph, "pg%d" % i, [128, 512], F32) for i in range(2)]
                for cb in range(2):
                    for kc in range(8):
                        S.op("pe", MM(pg[cb][:], scb[:, kc, :], adaw[:, kc, 2048 + cb * 512:2048 + (cb + 1) * 512],
                                      start=(kc == 0), stop=(kc == 7)), reads=[scb, adaw], writes=[pg[cb]], inc=(kc == 7))
                    S.op("dve", TT(gate_row[:, cb * 512:(cb + 1) * 512], pg[cb][:], adabg[:, cb * 512:(cb + 1) * 512], ALU.add),
                         reads=[pg[cb], adabg], writes=[gate_row])
                lgt = sb(ph, "lgt", [128, 8], F32)
                S.op("act", ACT(lgt[:], rdec[:], AF.Exp), reads=[rdec], writes=[lgt])
                S.op("dve", TS(lgt[:], lgt[:], -1.0, None, ALU.mult, None), reads=[lgt], writes=[lgt])
                for i, (cf, cbk) in enumerate(((1, 4), (0, 3), (2, 5))):
                    S.op("act", ACT(rtab[:, i, 0:4], lgt[:, 0:4], AF.Exp, scale=iota[:, cf:cf + 1]), reads=[lgt, iota], writes=[rtab])
                    S.op("act", ACT(rtab[:, i, 4:8], lgt[:, 4:8], AF.Exp, scale=iota[:, cbk:cbk + 1]), reads=[lgt, iota], writes=[rtab])
                S.op("act", ACT(rtab[:, 3, :], lgt[:], AF.Exp, scale=128.0), reads=[lgt], writes=[rtab])
                S.op("dve", TS(rtab[:, 0, :], rtab[:, 0, :], 1.0 / 16.0, None, ALU.mult, None), reads=[rtab], writes=[rtab])
                S.op("dve", TS(rtab[:, 2, :], rtab[:, 2, :], 1.0 / 16.0, None, ALU.mult, None), reads=[rtab], writes=[rtab])
                wf = sbring(ph, "wf", [128, 3136], F32, 2, dma=True)
                wb = sbring(ph, "wb", [128, 3136], BF16, 2, dma="st")
                engs = ("dve", "act", "pool", "dve")
                n = 0
                for kc in range(8 if not LIGHT else 0):
                    for cb in range(4):
                        f = wf.next()
                        b = wb.next()
                        load(f, f[:], w_in_d[kc * 128:(kc + 1) * 128, cb * 3136:(cb + 1) * 3136])
                        e = engs[n % 4]
                        n += 1
                        if e == "act":
                            S.op("act", ACT(b[:], f[:], AF.Copy), reads=[f], writes=[b])
                        else:
                            S.op(e, CP(b[:], f[:]), reads=[f], writes=[b])
                        store(b, winb_d[kc * 128:(kc + 1) * 128, cb * 3136:(cb + 1) * 3136], b[:])
                S.barrier()
                S.replay()
                S.end_phase()

        def phase1():
            with ExitStack() as ph:
                S.begin_phase()
                xs = sbring(ph, "xs", [128, D], F32, 4, dma=True)
                xn = sbring(ph, "xn", [128, D], BF16, 4)
                junk = sb(ph, "junk", [128, D], BF16)
                ss = sbring(ph, "ss", [128, 4], F32, 2)
                rstd = sbring(ph, "rstd", [128, 4], F32, 2)
                uT = sbring(ph, "uT", [128, 8, 512], BF16, 2)
                wp = sbring(ph, "wp", [128, 8, 512], BF16, 3, dma=True)
                stg = sbring(ph, "stg", [128, 512], BF16, 4, dma="st")
                cosr = sbring(ph, "cosr", [128, 512], F32, 4, dma=True)
                sinr = sbring(ph, "sinr", [128, 512], F32, 4, dma=True)
                t1r = sbring(ph, "t1r", [128, 512], F32, 2)
                t2r = sbring(ph, "t2r", [128, 512], F32, 2)
                qr = sbring(ph, "qr", [128, D], BF16, 4)
                kr = sbring(ph, "kr", [128, D], BF16, 4, dma="st")
                vr = sbring(ph, "vr", [128, 2 * D], BF16, 4, dma="st")
                qTs = sbring(ph, "qTs", [128, 8, 512], BF16, 1, dma="st")
                kTs = sbring(ph, "kTs", [128, 8, 512], BF16, 1, dma="st")
                pmm = Ring([psb(ph, "pmm%d" % i, [128, 512], F32) for i in range(4)])
                ptr = Ring([psb(ph, "ptr%d" % i, [128, 1024], BF16) for i in range(3)])
                winb_v = winb_d.rearrange("(kc p) n -> p kc n", p=128)
                evac_n = [0]

                def evac(out_ap, in_ap, reads, writes):
                    evac_n[0] += 1
                    if evac_n[0] % 2:
                        S.op("act", ACT(out_ap, in_ap, AF.Copy), reads=reads, writes=writes)
                    else:
                        S.op("dve", CP(out_ap, in_ap), reads=reads, writes=writes)

                tiles = [("ctx", 0, NCTX)] + [("lat", t0, 512) for t0 in range(0, NT, 512)]
                for (src, tok0, tw) in tiles:
                    nsub = tw // 128
                    xsrc = ctx_d if src == "ctx" else x_d
                    t = 1 if src == "ctx" else 0
                    s_ = ss.next()
                    r_ = rstd.next()
                    xt = []
                    for s in range(nsub):
                        xb = xs.next()
                        load(xb, xb[:], xsrc[tok0 + s * 128:tok0 + (s + 1) * 128, :])
                        S.op("act", ACT(junk[:], xb[:], AF.Square, accum=s_[:, s:s + 1]), reads=[xb], writes=[junk, s_])
                        xt.append(xb)
                    S.op("act", ACT(r_[:, 0:nsub], s_[:, 0:nsub], AF.Sqrt, scale=1.0 / D, bias=1e-6), reads=[s_], writes=[r_])
                    S.op("dve", lambda e, r_=r_, nsub=nsub: e.reciprocal(out=r_[:, 0:nsub], in_=r_[:, 0:nsub]), reads=[r_], writes=[r_])
                    xns = []
                    for s in range(nsub):
                        xnb = xn.next()
                        S.op("pool" if s % 2 else "dve", TS(xnb[:], xt[s][:], r_[:, s:s + 1], None, ALU.mult, None),
                             reads=[xt[s], r_], writes=[xnb])
                        xns.append(xnb)
                    u = uT.next()
                    for jp in range(4):
                        pt = ptr.next()
                        for jj in range(2):
                            j = jp * 2 + jj
                            for s in range(nsub):
                                S.op("pe", TR(pt[:, jj * 512 + s * 128:jj * 512 + (s + 1) * 128], xns[s][:, j * 128:(j + 1) * 128], identb[:]),
                                     reads=[xns[s], identb], writes=[pt], inc=(s == nsub - 1))
                            S.op("act", ACT(u[:, j, 0:tw], pt[:, jj * 512:jj * 512 + tw], AF.Identity,
                                            scale=dcol[:, 8 * t + j:8 * t + j + 1], bias=modc[:, j, t:t + 1]),
                                 reads=[pt, dcol, modc], writes=[u])
                    pieces = []
                    for i in range(6):
                        pieces.append(("fm", CO["rkv"] + i * 512, 512, RO["rkv"] + i * 512))
                    pieces.append(("fm", CO["lw"], 256, RO["lora"]))
                    pieces += [("k", CO["k"] + i * 512, 512, i) for i in range(2)]
                    pieces += [("v", CO["v"] + i * 512, 512, i) for i in range(4)]
                    if src == "lat":
                        pieces += [("q", CO["q"] + i * 512, 512, i) for i in range(2)]
                        pieces += [("fm", CO["ga"] + i * 512, 512, RO["ga"] + i * 512) for i in range(2)]
                        pieces += [("fm", CO["gr"] + i * 512, 512, RO["gr"] + i * 512) for i in range(4)]
                        pieces += [("fm", CO["ma"] + i * 512, 512, RO["ma"] + i * 512) for i in range(2)]
                        pieces += [("fm", CO["mb"] + i * 512, 512, RO["mb"] + i * 512) for i in range(2)]
                    tm_pieces = [p for p in pieces if p[0] != "fm"]
                    fm_pieces = [p for p in pieces if p[0] == "fm"]
                    for (kind, c0, ncol, r0) in fm_pieces:
                        w = wp.next()
                        load(w, w[:, :, 0:ncol], winb_v[:, :, c0:c0 + ncol])
                        for bi in range(ncol // 128):
                            pm = pmm.next()
                            for kc in range(8):
                                S.op("pe", MM(pm[:, 0:tw], w[:, kc, bi * 128:(bi + 1) * 128], u[:, kc, 0:tw],
                                              start=(kc == 0), stop=(kc == 7)), reads=[w, u], writes=[pm], inc=(kc == 7))
                            sg = stg.next()
                            evac(sg[:, 0:tw], pm[:, 0:tw], [pm], [sg])
                            store(sg, projT[src][r0 + bi * 128:r0 + (bi + 1) * 128, tok0:tok0 + tw], sg[:, 0:tw])
                    qT = qTs.next()
                    kT = kTs.next()
                    subs = []
                    for s in range(nsub):
                        tk0 = tok0 + s * 128
                        cs = cosr.next()
                        sn = sinr.next()
                        if src == "lat":
                            for hh in range(2):
                                load(cs, cs[:, hh * 256:(hh + 1) * 256], cos_d[tk0:tk0 + 128, :])
                                load(sn, sn[:, hh * 256:(hh + 1) * 256], sin_d[tk0:tk0 + 128, :])
                        subs.append((cs, sn, qr.next() if src == "lat" else None, kr.next(), vr.next()))
                    for (kind, c0, ncol, idx) in tm_pieces:
                        w = wp.next()
                        load(w, w[:], winb_v[:, :, c0:c0 + 512])
                        for s in range(nsub):
                            cs, sn, qb, kb, vb = subs[s]
                            pm = pmm.next()
                            for kc in range(8):
                                S.op("pe", MM(pm[:], u[:, kc, s * 128:(s + 1) * 128], w[:, kc, :],
                                              start=(kc == 0), stop=(kc == 7)), reads=[w, u], writes=[pm], inc=(kc == 7))
                            if kind == "v":
                                evac(vb[:, idx * 512:(idx + 1) * 512], pm[:], [pm], [vb])
                            elif src == "ctx":
                                evac(kb[:, idx * 512:(idx + 1) * 512], pm[:], [pm], [kb])
                            else:
                                dst = qb if kind == "q" else kb
                                t1 = t1r.next()
                                t2 = t2r.next()
                                S.op("dve", TT(t1[:], pm[:], cs[:], ALU.mult), reads=[pm, cs], writes=[t1])
                                p4 = pm[:].rearrange("p (g h i) -> p g h i", g=4, h=2)
                                s4 = sn[:].rearrange("p (g h i) -> p g h i", g=4, h=2)
                                t4 = t2[:].rearrange("p (g h i) -> p g h i", g=4, h=2)
                                S.op("dve", TT(t4[:, :, 0, :], p4[:, :, 1, :], s4[:, :, 0, :], ALU.mult), reads=[pm, sn], writes=[t2])
                                S.op("dve", TT(t4[:, :, 1, :], p4[:, :, 0, :], s4[:, :, 1, :], ALU.mult), reads=[pm, sn], writes=[t2])
                                S.op("pool", TT(dst[:, idx * 512:(idx + 1) * 512], t1[:], t2[:], ALU.add), reads=[t1, t2], writes=[dst])
                    for s in range(nsub):
                        tk0 = tok0 + s * 128
                        cs, sn, qb, kb, vb = subs[s]
                        store(vb, v_tm[src][tk0:tk0 + 128, :], vb[:])
                        store(kb, k_tm[src][tk0:tk0 + 128, :], kb[:])
                        for (srcb, dstT) in ((qb, qT), (kb, kT)):
                            if srcb is None:
                                continue
                            pt = ptr.next()
                            for j in range(8):
                                S.op("pe", TR(pt[:, j * 128:(j + 1) * 128], srcb[:, j * 128:(j + 1) * 128], identb[:]),
                                     reads=[srcb, identb], writes=[pt], inc=(j == 7))
                            evac(dstT[:, :, s * 128:(s + 1) * 128], pt[:].rearrange("p (j t) -> p j t", j=8), [pt], [dstT])
                    if src == "lat":
                        store(qT, projT[src][RO["q"]:RO["q"] + D, tok0:tok0 + tw].rearrange("(j p) t -> p j t", p=128), qT[:, :, 0:tw])
                    store(kT, projT[src][RO["k"]:RO["k"] + D, tok0:tok0 + tw].rearrange("(j p) t -> p j t", p=128), kT[:, :, 0:tw])
                S.barrier()
                S.replay()
                S.end_phase()

        def rwkv_sweep(d, interleave=True):
            fwd = (d == 0)
            pb = 64 * d
            if fwd:
                mQ, mP, mAk, mRb, mRk = NU_S, NL_S, U_S, NU_I, U_I
            else:
                mQ, mP, mAk, mRb, mRk = NL_S, NU_S, L_S, NL_I, L_I
            with ExitStack() as ph:
                S.begin_phase()
                cdiag = sb(ph, "cdiag", [128, 72, 128], BF16)
                rkvraw = sbring(ph, "rkvraw", [128, 24, 130], BF16, 1, dma=True)
                lora = sbring(ph, "lora", [128, 2, 128], BF16, 2, dma=True)
                thr = sbring(ph, "thr", [128, 128], BF16, 2)
                if fwd:
                    gaTr = sbring(ph, "gaTr", [128, 8, 128], BF16, 2, dma=True)
                    oblr = sbring(ph, "oblr", [128, D], BF16, 2, dma=True)
                    bblr = sbring(ph, "bblr", [128, 8, 128], BF16, 2, dma=True)
                    bonfr = sbring(ph, "bonfr", [128, 8, 128], BF16, 2)
                    oa = sb(ph, "oa", [128, D], F32)
                    sqt = sb(ph, "sqt", [128, D], F32)
                    ynb = sb(ph, "ynb", [128, D], BF16)
                    GTr = sbring(ph, "GTr", [128, 8, 128], BF16, 1, dma="st")
                    stt_ = {n: sb(ph, "st_" + n, [128, 16], F32) for n in ("s1", "s2", "mean", "msq", "var", "rstd", "nb")}
                else:
                    ostr = sbring(ph, "ostr", [128, D], BF16, 2, dma="st")
                    bstr = sbring(ph, "bstr", [128, 8, 128], BF16, 2, dma="st")
                T = {n: sb(ph, "t_" + n, [128, 512], F32) for n in
                     ("rS", "kS", "kk", "rs", "sg", "a", "prefix", "Dm", "PT", "b", "kd", "PM", "DMm") + (() if fwd else ("DT",))}
                Er = sbring(ph, "Er", [128, 512], F32, 2)
                vSr = sbring(ph, "vSr", [128, 512], BF16, 2)
                sq = sb(ph, "sq", [128, 512], BF16)
                pr = sb(ph, "pr", [128, 512], BF16)
                WCr = sbring(ph, "WCr", [128, 8], F32, 2)
                outs = {n: [sbring(ph, "%s%d_" % (n, hf), [128, 512], BF16, 2) for hf in range(2)]
                        for n in ("ATe", "ATo", "ATse", "ATso", "BT", "KT", "RTe", "RTo", "RTse", "RTso", "KH", "BH")}
                for n in ("ATe", "ATo", "ATse", "ATso", "RTe", "RTo", "RTse", "RTso"):
                    for hf in range(2):
                        for b_ in outs[n][hf].b:
                            S.op("pool", MS(b_[:], 0.0), writes=[b_])
                Vtmr = sbring(ph, "Vtmr", [128, D], BF16, 2)
                Khtmr = sbring(ph, "Khtmr", [128, D], BF16, 2)
                nBhtmr = sbring(ph, "nBhtmr", [128, D], BF16, 2)
                Uallr = sbring(ph, "Uallr", [128, D], BF16, 1)
                Tst = sb(ph, "Tst", [128, 8, 64], F32)
                Tbf = sb(ph, "Tbf", [128, 8, 64], BF16)
                dbgsem = S.new_dma_sem("st") if dbg else None
                LM = sb(ph, "LM", [128, 7, 512], BF16, dma=True)
                load(LM, LM[:], lmasks_d[d])
                slots = []
                for sl in range(2):
                    slots.append(dict(
                        nN=sb(ph, "cnN%d" % sl, [128, 512], BF16), nB=sbring(ph, "cnB%d_" % sl, [128, 512], BF16, 2),
                        IT=sbring(ph, "cIT%d_" % sl, [128, 512], BF16, 2),
                        M=sbring(ph, "cM%d_" % sl, [128, 512], BF16, 2), MT=sbring(ph, "cMT%d_" % sl, [128, 512], BF16, 2),
                        Aak=sb(ph, "cAak%d" % sl, [128, 512], BF16), nArb=sb(ph, "cnArb%d" % sl, [128, 512], BF16),
                        Ark=sb(ph, "cArk%d" % sl, [128, 512], BF16), Xbf=sb(ph, "cX%d" % sl, [128, 256], BF16)))
                pprep = Ring([psb(ph, "pprep%d" % i, [128, 512], F32) for i in range(2)])
                ptr = psb(ph, "ptrA", [128, 1024], BF16)
                pxo = Ring([psb(ph, "pxo%d" % i, [128, 512], F32) for i in range(2)])
                pch = Ring([psb(ph, "pch%d" % i, [128, 512], F32) for i in range(3)])

                for i in range(72):
                    S.op("pool" if i % 2 else "dve", TS(cdiag[:, i, :], identb[:], pcol("conv", i), None, ALU.mult, None),
                         reads=[identb, pc], writes=[cdiag])
                S.op("pool", MS(Tst[:], 0.0), writes=[Tst])
                S.op("pool", MS(Tbf[:], 0.0), writes=[Tbf])

                def v4(ap):
                    return ap.rearrange("p (i t) -> p i t", i=4)

                def prep(src, c, with_out, ctxd):
                    ntok = NCTX if src == "ctx" else NT
                    t0 = c * 128
                    rk = rkvraw.next()
                    lo, hi = max(t0 - 1, 0), min(t0 + 129, ntok)
                    a0 = lo - (t0 - 1)
                    for part in range(3):
                        load(rk, rk[:, 8 * part:8 * part + 8, a0:a0 + hi - lo],
                             projT[src][1024 * part:1024 * (part + 1), lo:hi].rearrange("(c p) t -> p c t", p=128))
                    if t0 == 0:
                        S.op("dve", MS(rk[:, :, 0:1], 0.0), writes=[rk])
                    if t0 + 128 == ntok:
                        S.op("dve", MS(rk[:, :, 129:130], 0.0), writes=[rk])
                    lr = lora.next()
                    load(lr, lr[:], projT[src][RO["lora"]:RO["lora"] + 256, t0:t0 + 128].rearrange("(c p) t -> p c t", p=128))
                    if fwd and with_out:
                        ga = gaTr.next()
                        load(ga, ga[:], projT[src][RO["ga"]:RO["ga"] + D, t0:t0 + 128].rearrange("(c p) t -> p c t", p=128))
                        obl = oblr.next()
                        load(obl, obl[:], obwd_a[t0:t0 + 128, :])
                        bbl = bblr.next()
                        load(bbl, bbl[:], bonb_d[:, t0:t0 + 128].rearrange("(c p) t -> p c t", p=128))
                        ctxd.update(ga=ga, obl=obl, bbl=bbl, bonf=bonfr.next())
                    if (not fwd) and with_out:
                        ctxd["bst"] = bstr.next()
                    thb = thr.next()
                    S.op("act", ACT(thb[:], lr[:, 0, :], AF.Tanh), reads=[lr], writes=[thb])
                    WCt = WCr.next()
                    ctxd["WC"] = WCt
                    ctxd["vS"] = []
                    yield
                    for hf in range(2):
                        fcs = [4 * hf + i for i in range(4)]
                        o = {n: outs[n][hf].next() for n in outs}
                        ctxd.setdefault("o", []).append(o)
                        vS = vSr.next()
                        ctxd["vS"].append(vS)
                        rS, kS, kk, rs, sg, a_, prefix, Dm, PT, b_, kd = (T[n] for n in
                            ("rS", "kS", "kk", "rs", "sg", "a", "prefix", "Dm", "PT", "b", "kd"))
                        DT = T.get("DT")
                        for (nm, base) in (("r", 0), ("k", 8), ("v", 16)):
                            if nm == "r" and not with_out:
                                continue
                            pcv = pprep.next()
                            for i, fc in enumerate(fcs):
                                blk = base + fc
                                for j in range(3):
                                    S.op("pe", MM(pcv[:, i * 128:(i + 1) * 128], cdiag[:, j * 24 + blk, :], rk[:, blk, j:j + 128],
                                                  start=(j == 0), stop=(j == 2)), reads=[cdiag, rk], writes=[pcv], inc=(j == 2 and i == 3))
                            if nm == "r":
                                S.op("act", ACT(rS[:], pcv[:], AF.Copy), reads=[pcv], writes=[rS])
                            elif nm == "k":
                                S.op("dve", CP(kS[:], pcv[:]), reads=[pcv], writes=[kS])
                            else:
                                S.op("act", ACT(vS[:], pcv[:], AF.Copy), reads=[pcv], writes=[vS])
                        yield
                        S.op("dve", TT(v4(kk[:]), v4(kS[:]), bc3(pc[:, PCO["k_k"] + 4 * hf:PCO["k_k"] + 4 * hf + 4], 128), ALU.mult),
                             reads=[kS, pc], writes=[kk])
                        S.op("act", ACT(sq[:], kk[:], AF.Square), reads=[kk], writes=[sq])
                        pss = pprep.next()
                        for i in range(4):
                            S.op("pe", MM(pss[:, i * 128:(i + 1) * 128], bones[:], sq[:, i * 128:(i + 1) * 128]), reads=[bones, sq], writes=[pss], inc=(i == 3))
                        S.op("act", ACT(rs[:], pss[:], AF.Sqrt, bias=1e-12), reads=[pss], writes=[rs])
                        S.op("dve", lambda e, rs=rs: e.reciprocal(out=rs[:], in_=rs[:]), reads=[rs], writes=[rs])
                        S.op("pool", TT(kk[:], kk[:], rs[:], ALU.mult), reads=[kk, rs], writes=[kk])
                        yield
                        pz = pprep.next()
                        for i, fc in enumerate(fcs):
                            S.op("pe", MM(pz[:, i * 128:(i + 1) * 128], wupb[pb:pb + 64, fc * 128:(fc + 1) * 128], thb[pb:pb + 64, :]),
                                 reads=[wupb, thb], writes=[pz], inc=(i == 3))
                        paz = pprep.next()
                        for i, fc in enumerate(fcs):
                            S.op("pe", MM(paz[:, i * 128:(i + 1) * 128], aupb[pb:pb + 64, fc * 128:(fc + 1) * 128], lr[pb:pb + 64, 1, :]),
                                 reads=[aupb, lr], writes=[paz], inc=(i == 3))
                        w0o = PCO["w0"] + d * 8 + 4 * hf
                        a0o = PCO["a0"] + d * 8 + 4 * hf
                        S.op("dve", TT(v4(sg[:]), v4(pz[:]), bc3(pc[:, w0o:w0o + 4], 128), ALU.add), reads=[pz, pc], writes=[sg])
                        S.op("act", ACT(sg[:], sg[:], AF.Tanh, scale=0.5), reads=[sg], writes=[sg])
                        S.op("dve", TS(sg[:], sg[:], 0.5, 0.5, ALU.mult, ALU.add), reads=[sg], writes=[sg])
                        S.op("dve", TT(v4(a_[:]), v4(paz[:]), bc3(pc[:, a0o:a0o + 4], 128), ALU.add), reads=[paz, pc], writes=[a_])
                        S.op("act", ACT(a_[:], a_[:], AF.Tanh, scale=0.5), reads=[a_], writes=[a_])
                        S.op("pool", TS(a_[:], a_[:], 0.5, 0.5, ALU.mult, ALU.add), reads=[a_], writes=[a_])
                        yield
                        for i in range(4):
                            S.op("dve", lambda e, i=i: e.tensor_tensor_scan(out=prefix[:, i * 128:(i + 1) * 128], data0=onesf[:], data1=sg[:, i * 128:(i + 1) * 128],
                                                                             initial=0.0, op0=ALU.mult, op1=ALU.add), reads=[onesf, sg], writes=[prefix])
                        S.op("dve", TT(Dm[:], sg[:], prefix[:], ALU.subtract), reads=[sg, prefix], writes=[Dm])
                        totb = v4(prefix[:])[:, :, 127:128].to_broadcast([128, 4, 128])
                        S.op("dve", TT(v4(PT[:]), v4(prefix[:]), totb, ALU.subtract), reads=[prefix], writes=[PT])
                        if not fwd:
                            S.op("dve", TT(v4(DT[:]), v4(Dm[:]), totb, ALU.add), reads=[Dm, prefix], writes=[DT])
                        S.op("act", ACT(WCt[:, 4 * hf:4 * hf + 4], v4(prefix[:])[:, :, 127], AF.Exp, scale=-KAPPA), reads=[prefix], writes=[WCt])
                        S.op("dve", TT(b_[:], kk[:], a_[:], ALU.mult), reads=[kk, a_], writes=[b_])
                        S.op("dve", TT(v4(kd[:]), v4(a_[:]), bc3(pc[:, PCO["k_a"] + 4 * hf:PCO["k_a"] + 4 * hf + 4], 128), ALU.mult), reads=[a_, pc], writes=[kd])
                        S.op("dve", TT(v4(kd[:]), v4(kd[:]), bc3(dcol[:, 16 + 4 * hf:20 + 4 * hf], 128), ALU.add), reads=[kd, dcol], writes=[kd])
                        S.op("pool", TT(kd[:], kS[:], kd[:], ALU.mult), reads=[kS, kd], writes=[kd])
                        yield
                        PM, DMm = T["PM"], T["DMm"]
                        if fwd:
                            midb = v4(prefix[:])[:, :, 63:64].to_broadcast([128, 4, 128])
                            S.op("dve", TT(v4(PM[:]), v4(prefix[:]), midb, ALU.subtract), reads=[prefix], writes=[PM])
                            S.op("dve", TT(v4(DMm[:]), v4(Dm[:]), midb, ALU.add), reads=[Dm, prefix], writes=[DMm])
                            e1, e3, e4 = (prefix, -KAPPA), (Dm, KAPPA), (PT, KAPPA)
                        else:
                            midb = v4(DT[:])[:, :, 64:65].to_broadcast([128, 4, 128])
                            S.op("dve", TT(v4(PM[:]), v4(DT[:]), midb, ALU.subtract), reads=[DT], writes=[PM])
                            S.op("dve", TT(v4(DMm[:]), v4(PT[:]), midb, ALU.add), reads=[PT, DT], writes=[DMm])
                            e1, e3, e4 = (DT, -KAPPA), (PT, KAPPA), (Dm, KAPPA)
                        if with_out:
                            E = Er.next()
                            S.op("act", ACT(E[:], e1[0][:], AF.Exp, scale=e1[1]), reads=[e1[0]], writes=[E])
                            S.op("pool", TT(o["RTe"][0:64, :], rS[0:64, :], E[0:64, :], ALU.mult), reads=[rS, E], writes=[o["RTe"]])
                            S.op("dve", TT(o["RTo"][64:128, :], rS[64:128, :], E[64:128, :], ALU.mult), reads=[rS, E], writes=[o["RTo"]])
                            E = Er.next()
                            S.op("act", ACT(E[:], PM[:], AF.Exp, scale=-KAPPA), reads=[PM], writes=[E])
                            S.op("pool", TT(o["RTse"][0:64, :], rS[0:64, :], E[0:64, :], ALU.mult), reads=[rS, E], writes=[o["RTse"]])
                            S.op("dve", TT(o["RTso"][64:128, :], rS[64:128, :], E[64:128, :], ALU.mult), reads=[rS, E], writes=[o["RTso"]])
                        E = Er.next()
                        S.op("act", ACT(E[:], PM[:], AF.Exp, scale=KAPPA), reads=[PM], writes=[E])
                        S.op("pool", TT(o["BT"][:], b_[:], E[:], ALU.mult), reads=[b_, E], writes=[o["BT"]])
                        S.op("pool", TT(o["KT"][:], kd[:], E[:], ALU.mult), reads=[kd, E], writes=[o["KT"]])
                        E = Er.next()
                        S.op("act", ACT(E[:], e3[0][:], AF.Exp, scale=e3[1]), reads=[e3[0]], writes=[E])
                        S.op("pool", TT(o["ATe"][0:64, :], kk[0:64, :], E[0:64, :], ALU.mult), reads=[kk, E], writes=[o["ATe"]])
                        S.op("dve", TT(o["ATo"][64:128, :], kk[64:128, :], E[64:128, :], ALU.mult), reads=[kk, E], writes=[o["ATo"]])
                        E = Er.next()
                        S.op("act", ACT(E[:], DMm[:], AF.Exp, scale=KAPPA), reads=[DMm], writes=[E])
                        S.op("pool", TT(o["ATse"][0:64, :], kk[0:64, :], E[0:64, :], ALU.mult), reads=[kk, E], writes=[o["ATse"]])
                        S.op("dve", TT(o["ATso"][64:128, :], kk[64:128, :], E[64:128, :], ALU.mult), reads=[kk, E], writes=[o["ATso"]])
                        E = Er.next()
                        S.op("act", ACT(E[:], e4[0][:], AF.Exp, scale=e4[1]), reads=[e4[0]], writes=[E])
                        S.op("pool", TT(o["KH"][:], kd[:], E[:], ALU.mult), reads=[kd, E], writes=[o["KH"]])
                        S.op("pool", TT(o["BH"][:], b_[:], E[:], ALU.mult), reads=[b_, E], writes=[o["BH"]])
                        yield
                        if with_out:
                            S.op("dve", TT(v4(rs[:]), v4(rS[:]), bc3(pc[:, PCO["r_k"] + 4 * hf:PCO["r_k"] + 4 * hf + 4], 128), ALU.mult), reads=[rS, pc], writes=[rs])
                            S.op("pool", TT(pr[:], rs[:], kd[:], ALU.mult), reads=[rs, kd], writes=[pr])
                            pbs = pprep.next()
                            for i in range(4):
                                S.op("pe", MM(pbs[:, i * 128:(i + 1) * 128], bones[:], pr[:, i * 128:(i + 1) * 128]), reads=[bones, pr], writes=[pbs], inc=(i == 3))
                            if fwd:
                                S.op("dve", TT(ctxd["bonf"][:, 4 * hf:4 * hf + 4, :], v4(pbs[:]), v4(vS[:]), ALU.mult), reads=[pbs, vS], writes=[ctxd["bonf"]])
                            else:
                                S.op("dve", TT(ctxd["bst"][:, 4 * hf:4 * hf + 4, :], v4(pbs[:]), v4(vS[:]), ALU.mult), reads=[pbs, vS], writes=[ctxd["bst"]])
                            yield
                    Vtm, Khtm, nBhtm = Vtmr.next(), Khtmr.next(), nBhtmr.next()
                    ctxd.update(Vtm=Vtm, Khtm=Khtm, nBhtm=nBhtm)
                    for which in range(3):
                        for fc in range(8):
                            hf, i = fc // 4, fc % 4
                            srcb = ctxd["vS"][hf] if which == 0 else ctxd["o"][hf]["KH" if which == 1 else "BH"]
                            S.op("pe", TR(ptr[:, fc * 128:(fc + 1) * 128], srcb[:, i * 128:(i + 1) * 128], identb[:]),
                                 reads=[srcb, identb], writes=[ptr], inc=(fc == 7))
                        if which == 0:
                            S.op("act", ACT(Vtm[:], ptr[:], AF.Copy), reads=[ptr], writes=[Vtm])
                        elif which == 1:
                            S.op("dve", CP(Khtm[:], ptr[:]), reads=[ptr], writes=[Khtm])
                        else:
                            S.op("act", ACT(nBhtm[:], ptr[:], AF.Copy, scale=-1.0), reads=[ptr], writes=[nBhtm])
                        yield
                    if (not fwd) and with_out:
                        store(ctxd["bst"], bonb_d[:, t0:t0 + 128].rearrange("(c p) t -> p c t", p=128), ctxd["bst"][:])

                def chain_group(g, sl, with_out, ctxd, Uall):
                    Vtm, Khtm, nBhtm = ctxd["Vtm"], ctxd["Khtm"], ctxd["nBhtm"]
                    heads = []
                    for hl in range(4):
                        fc = 2 * g + hl // 2
                        hf, i, p0 = fc // 4, fc % 4, 64 * (hl % 2)
                        o = ctxd["o"][hf]
                        sfx = "e" if p0 == 0 else "o"
                        cs_ = slice(i * 128, (i + 1) * 128)
                        heads.append(dict(fc=fc, p0=p0, h=4 * g + hl, o=o, sfx=sfx,
                                          at=o["AT" + sfx][:, cs_], ats=o["ATs" + sfx][:, cs_], rt=o["RT" + sfx][:, cs_], rts=o["RTs" + sfx][:, cs_],
                                          bt=o["BT"][:, cs_], kt=o["KT"][:, cs_]))

                    def prod(lname, rname, mask, dst, eng="dve"):
                        pp = pch.next()
                        for hl, H in enumerate(heads):
                            S.op("pe", MM(pp[:, hl * 128:(hl + 1) * 128], H[lname], H[rname]),
                                 reads=[H["o"]["ATs" + H["sfx"]], H["o"]["BT"], H["o"]["KT"]] + ([H["o"]["RTs" + H["sfx"]]] if with_out else []),
                                 writes=[pp], inc=(hl == 3))
                        S.op("dve", TT(dst[:], pp[:], masks[:, mask, :], ALU.mult), reads=[pp, masks], writes=[dst])

                    nN = sl["nN"]
                    prod("ats", "bt", mP, nN)
                    prod("kt", "ats", mAk, sl["Aak"])
                    if with_out:
                        prod("bt", "rts", mRb, sl["nArb"])
                        prod("kt", "rts", mRk, sl["Ark"])
                    yield
                    px = pxo.next()
                    for hl, H in enumerate(heads):
                        S.op("pe", MM(px[:, hl * 64:(hl + 1) * 64], H["at"], Tbf[:, H["fc"], :], start=True, stop=False),
                             reads=[H["o"]["AT" + H["sfx"]], Tbf], writes=[px], inc=False)
                        S.op("pe", MM(px[:, hl * 64:(hl + 1) * 64], sl["Aak"][:, hl * 128:(hl + 1) * 128], Vtm[:, H["h"] * 64:(H["h"] + 1) * 64], start=False, stop=True),
                             reads=[sl["Aak"], Vtm], writes=[px], inc=(hl == 3))
                    Xb = sl["Xbf"]
                    S.op("act", ACT(Xb[:], px[:, 0:256], AF.Copy), reads=[px], writes=[Xb])
                    yield
                    M, MT = ident4, ident4
                    for lev in range(7):
                        nB = sl["nB"].next()
                        S.op("pool", TT(nB[:], nN[:], LM[:, lev, :], ALU.mult), reads=[nN, LM], writes=[nB])
                        pb_ = pch.next()
                        for hl in range(4):
                            cs_ = slice(hl * 128, (hl + 1) * 128)
                            S.op("pe", MM(pb_[:, cs_], identb[:], identb[:], start=True, stop=False), reads=[identb], writes=[pb_], inc=False)
                            S.op("pe", MM(pb_[:, cs_], nB[:, cs_], MT[:, cs_], start=False, stop=True), reads=[nB, MT], writes=[pb_], inc=(hl == 3))
                        IT = sl["IT"].next()
                        S.op("act", ACT(IT[:], pb_[:], AF.Copy), reads=[pb_], writes=[IT])
                        pmt = pch.next()
                        for hl in range(4):
                            cs_ = slice(hl * 128, (hl + 1) * 128)
                            S.op("pe", MM(pmt[:, cs_], M[:, cs_], IT[:, cs_]), reads=[M, IT], writes=[pmt], inc=(hl == 3))
                        if lev < 6:
                            pm_ = pch.next()
                            for hl in range(4):
                                cs_ = slice(hl * 128, (hl + 1) * 128)
                                S.op("pe", MM(pm_[:, cs_], IT[:, cs_], M[:, cs_]), reads=[M, IT], writes=[pm_], inc=(hl == 3))
                        MTn = sl["MT"].next()
                        S.op("dve", CP(MTn[:], pmt[:]), reads=[pmt], writes=[MTn])
                        if lev < 6:
                            Mn = sl["M"].next()
                            S.op("act", ACT(Mn[:], pm_[:], AF.Copy), reads=[pm_], writes=[Mn])
                            M = Mn
                        MT = MTn
                        yield
                    pu = pxo.next()
                    for hl in range(4):
                        S.op("pe", MM(pu[:, hl * 64:(hl + 1) * 64], MT[:, hl * 128:(hl + 1) * 128], Xb[:, hl * 64:(hl + 1) * 64]),
                             reads=[MT, Xb], writes=[pu], inc=(hl == 3))
                    S.op("act", ACT(Uall[:, g * 256:(g + 1) * 256], pu[:, 0:256], AF.Copy), reads=[pu], writes=[Uall])
                    yield
                    if with_out:
                        po = pxo.next()
                        for hl, H in enumerate(heads):
                            hc = slice(H["h"] * 64, (H["h"] + 1) * 64)
                            oc = po[:, hl * 64:(hl + 1) * 64]
                            S.op("pe", MM(oc, H["rt"], Tbf[:, H["fc"], :], start=True, stop=False),
                                 reads=[H["o"]["RT" + H["sfx"]], Tbf], writes=[po], inc=False)
                            S.op("pe", MM(oc, sl["nArb"][:, hl * 128:(hl + 1) * 128], Uall[:, hc], start=False, stop=False),
                                 reads=[sl["nArb"], Uall], writes=[po], inc=False)
                            S.op("pe", MM(oc, sl["Ark"][:, hl * 128:(hl + 1) * 128], Vtm[:, hc], start=False, stop=True),
                                 reads=[sl["Ark"], Vtm], writes=[po], inc=(hl == 3))
                        if fwd:
                            S.op("dve", TT(oa[:, g * 256:(g + 1) * 256], po[:, 0:256], ctxd["obl"][:, g * 256:(g + 1) * 256], ALU.add),
                                 reads=[po, ctxd["obl"]], writes=[oa])
                        else:
                            S.op("act", ACT(ctxd["ost"][:, g * 256:(g + 1) * 256], po[:, 0:256], AF.Copy), reads=[po], writes=[ctxd["ost"]])
                    yield

                def rr(gens):
                    gens = list(gens)
                    while gens:
                        for gq in list(gens):
                            try:
                                next(gq)
                            except StopIteration:
                                gens.remove(gq)
                        yield

                def chain(src, c, with_out, ctxd):
                    t0 = c * 128
                    Uall = Uallr.next()
                    if (not fwd) and with_out:
                        ctxd["ost"] = ostr.next()
                    for pair in ((0, 1), (2, 3)):
                        yield from rr([chain_group(pair[0], slots[0], with_out, ctxd, Uall),
                                       chain_group(pair[1], slots[1], with_out, ctxd, Uall)])
                    pS = pprep.next()
                    Vtm, Khtm, nBhtm, WCt = ctxd["Vtm"], ctxd["Khtm"], ctxd["nBhtm"], ctxd["WC"]
                    for fc in range(8):
                        for hb in range(2):
                            h = 2 * fc + hb
                            hc = slice(h * 64, (h + 1) * 64)
                            oc = pS[64 * hb:64 * hb + 64, fc * 64:(fc + 1) * 64]
                            S.op("pe", MM(oc, Khtm[:, hc], Vtm[:, hc], start=True, stop=False), reads=[Khtm, Vtm], writes=[pS], inc=False)
                            S.op("pe", MM(oc, nBhtm[:, hc], Uall[:, hc], start=False, stop=True), reads=[nBhtm, Uall], writes=[pS],
                                 inc=(fc == 7 and hb == 1))
                    for fc in range(8):
                        S.op("dve", STT(Tst[:, fc, :], Tst[:, fc, :], WCt[:, fc:fc + 1], pS[:, fc * 64:(fc + 1) * 64], ALU.mult, ALU.add),
                             reads=[Tst, WCt, pS], writes=[Tst])
                    S.op("act", ACT(Tbf[:], Tst[:], AF.Copy), reads=[Tst], writes=[Tbf])
                    yield
                    if (not fwd) and with_out:
                        store(ctxd["ost"], obwd_a[t0:t0 + 128, :], ctxd["ost"][:])
                    if fwd and with_out:
                        st_ = stt_
                        if dbg:
                            S.dma("pool", dbgsem, dbg_oa[t0:t0 + 128, :], oa[:], reads=[oa], writes=[])
                        o3 = oa[:].rearrange("p (h v) -> p h v", h=16)
                        S.op("dve", lambda e: e.tensor_reduce(out=st_["s1"][:], in_=o3, axis=AX.X, op=ALU.add), reads=[oa], writes=[st_["s1"]])
                        S.op("act", ACT(sqt[:], oa[:], AF.Square), reads=[oa], writes=[sqt])
                        S.op("dve", lambda e: e.tensor_reduce(out=st_["s2"][:], in_=sqt[:].rearrange("p (h v) -> p h v", h=16), axis=AX.X, op=ALU.add),
                             reads=[sqt], writes=[st_["s2"]])
                        S.op("pool", TS(st_["mean"][:], st_["s1"][:], 1.0 / 64, None, ALU.mult, None), reads=[st_["s1"]], writes=[st_["mean"]])
                        S.op("pool", TT(st_["msq"][:], st_["mean"][:], st_["mean"][:], ALU.mult), reads=[st_["mean"]], writes=[st_["msq"]])
                        S.op("dve", STT(st_["var"][:], st_["s2"][:], 1.0 / 64, st_["msq"][:], ALU.mult, ALU.subtract), reads=[st_["s2"], st_["msq"]], writes=[st_["var"]])
                        S.op("act", ACT(st_["rstd"][:], st_["var"][:], AF.Sqrt, bias=64e-5), reads=[st_["var"]], writes=[st_["rstd"]])
                        S.op("dve", lambda e: e.reciprocal(out=st_["rstd"][:], in_=st_["rstd"][:]), reads=[st_["rstd"]], writes=[st_["rstd"]])
                        S.op("dve", STT(st_["nb"][:], st_["mean"][:], -1.0, st_["rstd"][:], ALU.mult, ALU.mult), reads=[st_["mean"], st_["rstd"]], writes=[st_["nb"]])
                        yield
                        s3 = sqt[:].rearrange("p (h v) -> p h v", h=16)
                        S.op("dve", TT(s3, o3, bc3(st_["rstd"][:], 64), ALU.mult), reads=[oa, st_["rstd"]], writes=[sqt])
                        S.op("dve", TT(ynb[:].rearrange("p (h v) -> p h v", h=16), s3, bc3(st_["nb"][:], 64), ALU.add), reads=[sqt, st_["nb"]], writes=[ynb])
                        for fc in range(8):
                            S.op("pe", TR(ptr[:, fc * 128:(fc + 1) * 128], ynb[:, fc * 128:(fc + 1) * 128], identb[:]), reads=[ynb, identb], writes=[ptr], inc=(fc == 7))
                        for fc in range(8):
                            S.op("act", ACT(sqt[:, fc * 128:(fc + 1) * 128], ptr[:, fc * 128:(fc + 1) * 128], AF.Identity, scale=pcol("a_ln_w", fc), bias=pcol("a_ln_b", fc)),
                                 reads=[ptr, pc], writes=[sqt])
                        yield
                        ga, bbl = ctxd["ga"], ctxd["bbl"]
                        g3 = sqt[:].rearrange("p (c t) -> p c t", c=8)
                        sg3 = oa[:].rearrange("p (c t) -> p c t", c=8)
                        S.op("pool", TT(g3, g3, ctxd["bonf"][:], ALU.add), reads=[sqt, ctxd["bonf"]], writes=[sqt])
                        S.op("pool", TT(g3, g3, bbl[:], ALU.add), reads=[sqt, bbl], writes=[sqt])
                        S.op("act", ACT(sg3, ga[:], AF.Tanh, scale=0.5), reads=[ga], writes=[oa])
                        S.op("pool", TS(oa[:], oa[:], 0.5, 0.5, ALU.mult, ALU.add), reads=[oa], writes=[oa])
                        S.op("pool", TT(sg3, sg3, ga[:], ALU.mult), reads=[oa, ga], writes=[oa])
                        GT = GTr.next()
                        S.op("dve", TT(GT[:], g3, sg3, ALU.mult), reads=[sqt, oa], writes=[GT])
                        store(GT, gall[0:D, t0:t0 + 128].rearrange("(c p) t -> p c t", p=128), GT[:])
                        yield

                order = [("ctx", 0, False), ("ctx", 1, False)] + [("lat", c, True) for c in range(NCH)]
                if not fwd:
                    order = [("ctx", 1, False), ("ctx", 0, False)] + [("lat", c, True) for c in range(NCH - 1, -1, -1)]
                ctxs = [dict() for _ in order]
                import os as _os
                budget = [int(_os.environ.get("KDBG_STEPS", "-1"))]

                def run(gq):
                    for _ in gq:
                        if budget[0] >= 0:
                            budget[0] -= 1
                            if budget[0] < 0:
                                return False
                    return True
                ok = budget[0] != 0 and run(prep(*order[0], ctxs[0]))
                for n in range(len(order)):
                    if not ok:
                        break
                    gens = [chain(*order[n], ctxs[n])]
                    if n + 1 < len(order):
                        gens.append(prep(*order[n + 1], ctxs[n + 1]))
                    if interleave and budget[0] < 0:
                        ok = run(rr(gens))
                    else:
                        for gq in gens:
                            ok = ok and run(gq)
                    ctxs[n].clear()
                S.barrier()
                S.replay()
                S.end_phase()

        def ret_sweep(d):
            fwd = (d == 0)
            mk = U_I if fwd else L_I
            with ExitStack() as ph:
                S.begin_phase()
                qTr = sbring(ph, "qTr", [128, 8, 128], BF16, 2, dma=True)
                kTr = sbring(ph, "kTr", [128, 8, 128], BF16, 2, dma=True)
                ktmr = sbring(ph, "ktmr", [128, D], BF16, 2, dma=True)
                vtmr = sbring(ph, "vtmr", [128, 2 * D], BF16, 2, dma=True)
                Mk = sb(ph, "Mk", [128, 512], F32)
                kdf = sb(ph, "kdf", [128, D], F32)
                Sst = [sb(ph, "Sst%d" % h, [128, 2, 512], F32) for h in range(4)]
                Sbf = [sb(ph, "Sbf%d" % h, [128, 2, 512], BF16) for h in range(4)]
                STr = sbring(ph, "STr", [128, 512], BF16, 2)
                Kdr = sbring(ph, "Kdr", [128, D], BF16, 2)
                if fwd:
                    grTr = sbring(ph, "grTr", [128, 16, 128], BF16, 2, dma=True)
                    oblr = sbring(ph, "roblr", [128, 2 * D], BF16, 2, dma=True)
                    orr = sb(ph, "orr", [128, 2 * D], F32)
                    sqr = sb(ph, "sqr", [128, 2 * D], F32)
                    ynr = sb(ph, "ynr", [128, 2 * D], BF16)
                    GrTr = sbring(ph, "GrTr", [128, 16, 128], BF16, 2, dma="st")
                    st_ = {n: sb(ph, "rst_" + n, [128, 4], F32) for n in ("s1", "s2", "mean", "msq", "var", "rstd", "nb")}
                    dbgsem = S.new_dma_sem("st") if dbg else None
                else:
                    ostr = sbring(ph, "rostr", [128, 2 * D], BF16, 2, dma="st")
                pst = Ring([psb(ph, "pst%d" % i, [128, 512], F32) for i in range(2)])
                pov = Ring([psb(ph, "pov%d" % i, [128, 512], F32) for i in range(2)])
                pss = Ring([psb(ph, "pss%d" % i, [128, 512], F32) for i in range(2)])
                ptr2 = [psb(ph, "ptrR%d" % i, [128, 1024], BF16) for i in range(2)]
                for h in range(4):
                    S.op("dve", TS(Mk[:, h * 128:(h + 1) * 128], masks[:, mk, 0:128], rtab[:, 0, 4 * d + h:4 * d + h + 1], None, ALU.mult, None),
                         reads=[masks, rtab], writes=[Mk])
                    for q2 in range(2):
                        S.op("act", ACT(kdf[:, h * 256 + q2 * 128:h * 256 + (q2 + 1) * 128], onesf[:], AF.Identity, scale=rtab[:, 2, 4 * d + h:4 * d + h + 1]),
                             reads=[onesf, rtab], writes=[kdf])
                    S.op("pool", MS(Sst[h][:], 0.0), writes=[Sst[h]])
                    S.op("pool", MS(Sbf[h][:], 0.0), writes=[Sbf[h]])

                def chunk(src, c, with_out):
                    t0 = c * 128
                    kT = kTr.next()
                    load(kT, kT[:], projT[src][RO["k"]:RO["k"] + D, t0:t0 + 128].rearrange("(c p) t -> p c t", p=128))
                    ktm = ktmr.next()
                    load(ktm, ktm[:], k_tm[src][t0:t0 + 128, :])
                    vtm = vtmr.next()
                    load(vtm, vtm[:], v_tm[src][t0:t0 + 128, :])
                    if with_out:
                        qT = qTr.next()
                        load(qT, qT[:], projT[src][RO["q"]:RO["q"] + D, t0:t0 + 128].rearrange("(c p) t -> p c t", p=128))
                        if fwd:
                            grT = grTr.next()
                            for part in range(2):
                                load(grT, grT[:, 8 * part:8 * part + 8, :],
                                     projT[src][RO["gr"] + D * part:RO["gr"] + D * (part + 1), t0:t0 + 128].rearrange("(c p) t -> p c t", p=128))
                            obl = oblr.next()
                            load(obl, obl[:], obwd_r[t0:t0 + 128, :])
                        else:
                            ost = ostr.next()
                        pS = pst.next()
                        for h in range(4):
                            for kc in range(2):
                                S.op("pe", MM(pS[:, h * 128:(h + 1) * 128], kT[:, 2 * h + kc, :], qT[:, 2 * h + kc, :], start=(kc == 0), stop=(kc == 1)),
                                     reads=[kT, qT], writes=[pS], inc=(h == 3 and kc == 1))
                        STm = STr.next()
                        S.op("dve", TT(STm[:], pS[:], Mk[:], ALU.mult), reads=[pS, Mk], writes=[STm])
                        for h in range(4):
                            po_ = pov.next()
                            S.op("pe", MM(po_[:], STm[:, h * 128:(h + 1) * 128], vtm[:, h * 512:(h + 1) * 512], start=True, stop=False),
                                 reads=[STm, vtm], writes=[po_], inc=False)
                            for kc in range(2):
                                S.op("pe", MM(po_[:], qT[:, 2 * h + kc, :], Sbf[h][:, kc, :], start=False, stop=(kc == 1)),
                                     reads=[qT, Sbf[h]], writes=[po_], inc=(kc == 1))
                            qd = rtab[:, 1, 4 * d + h:4 * d + h + 1]
                            if fwd:
                                S.op("dve", STT(orr[:, h * 512:(h + 1) * 512], po_[:], qd, obl[:, h * 512:(h + 1) * 512], ALU.mult, ALU.add),
                                     reads=[po_, rtab, obl], writes=[orr])
                            else:
                                S.op("act", ACT(ost[:, h * 512:(h + 1) * 512], po_[:], AF.Identity, scale=qd), reads=[po_, rtab], writes=[ost])
                        if not fwd:
                            store(ost, obwd_r[t0:t0 + 128, :], ost[:])
                    Kd = Kdr.next()
                    S.op("pool", TT(Kd[:], ktm[:], kdf[:], ALU.mult), reads=[ktm, kdf], writes=[Kd])
                    for h in range(4):
                        for kc in range(2):
                            ps_ = pss.next()
                            S.op("pe", MM(ps_[:], Kd[:, h * 256 + kc * 128:h * 256 + (kc + 1) * 128], vtm[:, h * 512:(h + 1) * 512]),
                                 reads=[Kd, vtm], writes=[ps_])
                            S.op("dve", STT(Sst[h][:, kc, :], Sst[h][:, kc, :], rtab[:, 3, 4 * d + h:4 * d + h + 1], ps_[:], ALU.mult, ALU.add),
                                 reads=[Sst[h], rtab, ps_], writes=[Sst[h]])
                        S.op("act", ACT(Sbf[h][:], Sst[h][:], AF.Copy), reads=[Sst[h]], writes=[Sbf[h]])
                    if fwd and with_out:
                        if dbg:
                            S.dma("pool", dbgsem, dbg_or[t0:t0 + 128, :], orr[:], reads=[orr], writes=[])
                        o3 = orr[:].rearrange("p (h v) -> p h v", h=4)
                        s3 = sqr[:].rearrange("p (h v) -> p h v", h=4)
                        S.op("dve", lambda e: e.tensor_reduce(out=st_["s1"][:], in_=o3, axis=AX.X, op=ALU.add), reads=[orr], writes=[st_["s1"]])
                        S.op("act", ACT(sqr[:], orr[:], AF.Square), reads=[orr], writes=[sqr])
                        S.op("dve", lambda e: e.tensor_reduce(out=st_["s2"][:], in_=s3, axis=AX.X, op=ALU.add), reads=[sqr], writes=[st_["s2"]])
                        S.op("pool", TS(st_["mean"][:], st_["s1"][:], 1.0 / 512, None, ALU.mult, None), reads=[st_["s1"]], writes=[st_["mean"]])
                        S.op("pool", TT(st_["msq"][:], st_["mean"][:], st_["mean"][:], ALU.mult), reads=[st_["mean"]], writes=[st_["msq"]])
                        S.op("dve", STT(st_["var"][:], st_["s2"][:], 1.0 / 512, st_["msq"][:], ALU.mult, ALU.subtract), reads=[st_["s2"], st_["msq"]], writes=[st_["var"]])
                        S.op("act", ACT(st_["rstd"][:], st_["var"][:], AF.Sqrt, bias=1e-5), reads=[st_["var"]], writes=[st_["rstd"]])
                        S.op("dve", lambda e: e.reciprocal(out=st_["rstd"][:], in_=st_["rstd"][:]), reads=[st_["rstd"]], writes=[st_["rstd"]])
                        S.op("dve", STT(st_["nb"][:], st_["mean"][:], -1.0, st_["rstd"][:], ALU.mult, ALU.mult), reads=[st_["mean"], st_["rstd"]], writes=[st_["nb"]])
                        S.op("dve", TT(s3, o3, bc3(st_["rstd"][:], 512), ALU.mult), reads=[orr, st_["rstd"]], writes=[sqr])
                        S.op("dve", TT(ynr[:].rearrange("p (h v) -> p h v", h=4), s3, bc3(st_["nb"][:], 512), ALU.add), reads=[sqr, st_["nb"]], writes=[ynr])
                        for fc in range(16):
                            pt = ptr2[fc // 8]
                            S.op("pe", TR(pt[:, (fc % 8) * 128:(fc % 8 + 1) * 128], ynr[:, fc * 128:(fc + 1) * 128], identb[:]), reads=[ynr, identb], writes=[pt],
                                 inc=(fc % 8 == 7))
                        for fc in range(16):
                            pt = ptr2[fc // 8]
                            S.op("act", ACT(sqr[:, fc * 128:(fc + 1) * 128], pt[:, (fc % 8) * 128:(fc % 8 + 1) * 128], AF.Identity,
                                            scale=pcol("r_ln_w", fc), bias=pcol("r_ln_b", fc)), reads=[pt, pc], writes=[sqr])
                        g3 = sqr[:].rearrange("p (c t) -> p c t", c=16)
                        sg3 = orr[:].rearrange("p (c t) -> p c t", c=16)
                        S.op("act", ACT(sg3, grT[:], AF.Tanh, scale=0.5), reads=[grT], writes=[orr])
                        S.op("pool", TS(orr[:], orr[:], 0.5, 0.5, ALU.mult, ALU.add), reads=[orr], writes=[orr])
                        S.op("pool", TT(sg3, sg3, grT[:], ALU.mult), reads=[orr, grT], writes=[orr])
                        GrT = GrTr.next()
                        S.op("dve", TT(GrT[:], g3, sg3, ALU.mult), reads=[sqr, orr], writes=[GrT])
                        for part in range(2):
                            store(GrT, gall[D * (1 + part):D * (2 + part), t0:t0 + 128].rearrange("(c p) t -> p c t", p=128), GrT[:, 8 * part:8 * part + 8, :])

                order = [("ctx", 0, False), ("ctx", 1, False)] + [("lat", c, True) for c in range(NCH)]
                if not fwd:
                    order = [("ctx", 1, False), ("ctx", 0, False)] + [("lat", c, True) for c in range(NCH - 1, -1, -1)]
                for o_ in order:
                    chunk(*o_)
                S.barrier()
                S.replay()
                S.end_phase()

        def phase6():
            TW = 256
            with ExitStack() as ph:
                S.begin_phase()
                fnw_row = sb(ph, "fnw_row", [128, D], F32, dma=True)
                load(fnw_row, fnw_row[:], fnw_d.partition_broadcast(128))
                awo = sb(ph, "awo", [128, 8, D], BF16)
                rwo = sb(ph, "rwo", [128, 16, D], BF16)
                wo = sb(ph, "wo", [128, 8, D], BF16)
                wf = sbring(ph, "wf6", [128, D], F32, 3, dma=True)
                n = 0
                for (src_d, dst, nk) in ((awo_d, awo, 8), (rwo_d, rwo, 16), (wo_d, wo, 8)):
                    for kc in range(nk):
                        f = wf.next()
                        load(f, f[:], src_d[kc * 128:(kc + 1) * 128, :])
                        e = ("dve", "act", "pool")[n % 3]
                        n += 1
                        if e == "act":
                            S.op("act", ACT(dst[:, kc, :], f[:], AF.Copy), reads=[f], writes=[dst])
                        else:
                            S.op(e, CP(dst[:, kc, :], f[:]), reads=[f], writes=[dst])
                gTr = sbring(ph, "gTr6", [128, 24, TW], BF16, 2, dma=True)
                mTr = sbring(ph, "mTr6", [128, 16, TW], BF16, 2, dma=True)
                xr = sbring(ph, "xr6", [128, D], F32, 3, dma=True)
                sm = sb(ph, "sm6", [128, 16, TW], F32)
                m1r = sbring(ph, "m1r", [128, TW], F32, 2)
                m2r = sbring(ph, "m2r", [128, TW], F32, 2)
                mgr = sbring(ph, "mgr", [128, 8, TW], BF16, 2)
                yor = sbring(ph, "yor", [128, D], F32, 2, dma="st")
                junk = sb(ph, "junk6", [128, D], BF16)
                ssr = sbring(ph, "ss6", [128, 1], F32, 2)
                pya = Ring([psb(ph, "pya%d" % i, [128, 512], F32) for i in range(2)])
                pyr = Ring([psb(ph, "pyr%d" % i, [128, 512], F32) for i in range(2)])
                pout = Ring([psb(ph, "pout%d" % i, [128, 512], F32) for i in range(3)])
                for t0 in range(0, NT, TW):
                    gT = gTr.next()
                    for part in range(3):
                        load(gT, gT[:, 8 * part:8 * part + 8, :], gall[D * part:D * (part + 1), t0:t0 + TW].rearrange("(c p) t -> p c t", p=128))
                    mT = mTr.next()
                    for part in range(2):
                        load(mT, mT[:, 8 * part:8 * part + 8, :],
                             projT["lat"][RO["ma"] + D * part:RO["ma"] + D * (part + 1), t0:t0 + TW].rearrange("(c p) t -> p c t", p=128))
                    S.op("act", ACT(sm[:], mT[:], AF.Tanh, scale=0.5), reads=[mT], writes=[sm])
                    S.op("pool", TS(sm[:], sm[:], 0.5, 0.5, ALU.mult, ALU.add), reads=[sm], writes=[sm])
                    mg = mgr.next()
                    for fo in range(8):
                        pa = pya.next()
                        for fc in range(8):
                            S.op("pe", MM(pa[:, 0:TW], awo[:, fc, fo * 128:(fo + 1) * 128], gT[:, fc, :], start=(fc == 0), stop=(fc == 7)),
                                 reads=[awo, gT], writes=[pa], inc=(fc == 7))
                        pr_ = pyr.next()
                        for fc in range(16):
                            S.op("pe", MM(pr_[:, 0:TW], rwo[:, fc, fo * 128:(fo + 1) * 128], gT[:, 8 + fc, :], start=(fc == 0), stop=(fc == 15)),
                                 reads=[rwo, gT], writes=[pr_], inc=(fc == 15))
                        m1 = m1r.next()
                        m2 = m2r.next()
                        S.op("dve", TT(m1[:], pa[:, 0:TW], sm[:, fo, :], ALU.mult), reads=[pa, sm], writes=[m1])
                        S.op("dve", TT(m2[:], pr_[:, 0:TW], sm[:, 8 + fo, :], ALU.mult), reads=[pr_, sm], writes=[m2])
                        S.op("pool", TT(mg[:, fo, :], m1[:], m2[:], ALU.add), reads=[m1, m2], writes=[mg])
                    for s_ in range(TW // 128):
                        tk = t0 + s_ * 128
                        xb = xr.next()
                        load(xb, xb[:], x_d[tk:tk + 128, :])
                        yo = yor.next()
                        for cb in range(2):
                            po_ = pout.next()
                            for fc in range(8):
                                S.op("pe", MM(po_[:], mg[:, fc, s_ * 128:(s_ + 1) * 128], wo[:, fc, cb * 512:(cb + 1) * 512], start=(fc == 0), stop=(fc == 7)),
                                     reads=[mg, wo], writes=[po_], inc=(fc == 7))
                            S.op("dve", TT(yo[:, cb * 512:(cb + 1) * 512], po_[:], gate_row[:, cb * 512:(cb + 1) * 512], ALU.mult),
                                 reads=[po_, gate_row], writes=[yo])
                        S.op("pool", TT(yo[:], yo[:], xb[:], ALU.add), reads=[yo, xb], writes=[yo])
                        ss = ssr.next()
                        S.op("act", ACT(junk[:], yo[:], AF.Square, accum=ss[:]), reads=[yo], writes=[junk, ss])
                        S.op("act", ACT(ss[:], ss[:], AF.Sqrt, scale=1.0 / D, bias=1e-6), reads=[ss], writes=[ss])
                        S.op("dve", lambda e, ss=ss: e.reciprocal(out=ss[:], in_=ss[:]), reads=[ss], writes=[ss])
                        S.op("dve", STT(yo[:], yo[:], ss[:], fnw_row[:], ALU.mult, ALU.mult), reads=[yo, ss, fnw_row], writes=[yo])
                        store(yo, y_d[tk:tk + 128, :], yo[:])
                S.barrier()
                S.replay()
                S.end_phase()

        import os as _os2
        phase0()
        if stop_after >= 1 and not _os2.environ.get("KDBG_SKIP_P1"):
            phase1()
        if stop_after >= 2:
            rwkv_sweep(1)
        if stop_after >= 3:
            rwkv_sweep(0)
        if stop_after >= 4:
            ret_sweep(1)
        if stop_after >= 5:
            ret_sweep(0)
        if stop_after >= 6:
            phase6()
        if stop_after < 6:
            with ExitStack() as ph:
                S.begin_phase()
                z = sb(ph, "zz", [128, D], F32, dma="st")
                S.op("pool", MS(z[:], 0.0), writes=[z])
                for i in range(NT // 128):
                    store(z, y_d[i * 128:(i + 1) * 128, :], z[:])
                S.barrier()
                S.replay()
                S.end_phase()
    return nc


def _cols(v):
    v = np.asarray(v, np.float32).reshape(-1)
    return np.ascontiguousarray(v.reshape(-1, 128).T)


def host_consts(NT):
    idx = np.arange(128)
    p = idx[:, None]
    f = idx[None, :]
    base = [(f > p), (f >= p), (f < p), (f <= p)]
    m = np.zeros((128, 8, 512), np.float32)
    for i, b in enumerate(base):
        m[:, i, :] = np.tile(b.astype(np.float32), (1, 4))
        m[:, 4 + i, :] = -m[:, i, :]
    bones = np.zeros((128, 128), np.float32)
    bones[:64, :64] = 1
    bones[64:, 64:] = 1
    j = idx.astype(np.float32)
    iota = np.stack([j + 1, -(j + 1), 127 - j, 128 - j, -(128 - j), j, 0 * j, 0 * j], 1).astype(np.float32)
    t = np.arange(NT)
    rows = (t // 64).astype(np.float64)
    cols = (t % 64).astype(np.float64)
    fr = 10000.0 ** (-np.arange(64, dtype=np.float64) / 64)
    cos = np.zeros((NT, 2, 2, 64), np.float32)
    sin = np.zeros((NT, 2, 2, 64), np.float32)
    for ty, pos in enumerate((rows, cols)):
        ang = (pos.astype(np.float32)[:, None] * fr.astype(np.float32)[None, :]).astype(np.float32)
        cos[:, ty, 0] = np.cos(ang)
        cos[:, ty, 1] = np.cos(ang)
        sin[:, ty, 0] = -np.sin(ang)
        sin[:, ty, 1] = np.sin(ang)
    lm = np.zeros((2, 128, 7, 512), np.float32)
    for j in range(7):
        bsz = 2 ** j
        q = ((p // (2 * bsz) == f // (2 * bsz)) & (p % (2 * bsz) >= bsz) & (f % (2 * bsz) < bsz)).astype(np.float32)
        lm[0, :, j, :] = np.tile(q, (1, 4))
        lm[1, :, j, :] = np.tile(q.T, (1, 4))
    return dict(masks=m.astype(ml_dtypes.bfloat16), identb=np.eye(128).astype(ml_dtypes.bfloat16),
                lmasks=lm.astype(ml_dtypes.bfloat16), ident4=np.tile(np.eye(128), (1, 4)).astype(ml_dtypes.bfloat16),
                blockones=bones.astype(ml_dtypes.bfloat16), iota=iota,
                rope_cos=cos.reshape(NT, 256), rope_sin=sin.reshape(NT, 256))


def make_in_maps(inputs, NT):
    f32 = lambda a: np.ascontiguousarray(np.asarray(a, np.float32))
    x = f32(inputs["x"])[:, :NT]
    B = x.shape[0]
    hc = host_consts(NT)
    shared = dict(
        ada_w=f32(inputs["ada_w"][0]), ada_b_gate=f32(inputs["ada_b"][0][2048:3072]),
        w_in=f32(inputs["w_in"][0]), a_w_up=f32(inputs["a_w_up"][0]).reshape(128, D),
        a_a_up=f32(inputs["a_a_up"][0]).reshape(128, D), a_w_out=f32(inputs["a_w_out"][0]),
        r_w_out=f32(inputs["r_w_out"][0]), w_out=f32(inputs["w_out"][0]),
        r_decay=f32(inputs["r_decay"][0]).reshape(8), final_norm_w=f32(inputs["final_norm_w"]), **hc)
    conv = f32(inputs["a_conv"][0])
    pcs = [_cols(inputs["norm_w"][0]), _cols(inputs["ada_b"][0])]
    pcs += [_cols(conv[j]) for j in range(3)]
    pcs += [_cols(inputs["a_w0"][0][d]) for d in range(2)]
    pcs += [_cols(inputs["a_a0"][0][d]) for d in range(2)]
    pcs += [_cols(inputs["a_k_k"][0]), _cols(inputs["a_k_a"][0]), _cols(inputs["a_r_k"][0]),
            _cols(inputs["a_ln_w"][0]), _cols(inputs["a_ln_b"][0]), _cols(inputs["r_ln_w"][0]), _cols(inputs["r_ln_b"][0])]
    maps = []
    for b in range(B):
        pcb = np.concatenate(pcs + [_cols(inputs["c"][b]), _cols(inputs["c_ctx"])], axis=1)
        assert pcb.shape == (128, NPC), pcb.shape
        m = dict(shared)
        m.update(x=np.ascontiguousarray(x[b]), ctx=f32(inputs["ctx"][b]), pcols=np.ascontiguousarray(pcb))
        maps.append(m)
    return maps


_NC_CACHE = {}


def kernel(**inputs):
    NT = inputs["x"].shape[1]
    if NT not in _NC_CACHE:
        _NC_CACHE[NT] = build(NT)
    nc = _NC_CACHE[NT]
    maps = make_in_maps(inputs, NT)
    res = run_bass_kernel_spmd(nc, maps, core_ids=list(range(len(maps))))
    return np.stack([np.asarray(r["y"], np.float32) for r in res.results], axis=0)
```

```python
from contextlib import ExitStack
import math
import numpy as np
import ml_dtypes
import concourse.bass as bass
import concourse.mybir as mybir
from concourse.bass_utils import run_bass_kernel_spmd

F32 = mybir.dt.float32
BF16 = mybir.dt.bfloat16
ALU = mybir.AluOpType
AF = mybir.ActivationFunctionType
AX = mybir.AxisListType

D = 1024
NCTX = 256
C = 128
N_IN = 12544
KAPPA = math.exp(-0.5)
CO = dict(rkv=0, ga=3072, lw=4096, la=4224, q=4352, k=5376, v=6400, gr=8448, ma=10496, mb=11520)
RO = dict(rkv=0, ga=3072, lora=4096, q=4352, k=5376, gr=6400, ma=8448, mb=9472)
NROW = 10496
PCO = {}
_o = 0
for _n, _w in (("norm_w", 8), ("ada_b", 24), ("conv", 72), ("w0", 16), ("a0", 16), ("k_k", 8), ("k_a", 8),
               ("r_k", 8), ("a_ln_w", 8), ("a_ln_b", 8), ("r_ln_w", 16), ("r_ln_b", 16), ("c", 8), ("c_ctx", 8)):
    PCO[_n] = _o
    _o += _w
NPC = _o


class Buf:
    __slots__ = ("name", "w", "r", "t", "sem")

    def __init__(self, name, t=None, sem=None):
        self.name = name
        self.w = None
        self.r = {}
        self.t = t
        self.sem = sem

    def __getitem__(self, idx):
        return self.t[idx]


class Ring:
    def __init__(self, bufs):
        self.b = bufs
        self.i = 0

    def next(self):
        b = self.b[self.i % len(self.b)]
        self.i += 1
        return b


class Sched:
    CE = ("pe", "act", "dve", "pool")
    QE = ("pe", "act", "dve", "pool", "sp")

    def __init__(self, nc, stack):
        self.nc = nc
        self.q = {e: [] for e in self.QE}
        self.tick = {e: 0 for e in self.CE}
        self.known = {e: {} for e in self.QE}
        self.semh = {}
        for e in self.CE:
            self.semh[e] = stack.enter_context(nc.semaphore("s_" + e))
        self.dma_cnt = {}
        self.n_dma = 0
        self.stack = stack
        self.n_instr = 0
        self.free_sems = {"ld": [], "st": []}
        self.phase_sems = []

    def begin_phase(self):
        self.phase_sems = []

    def end_phase(self):
        for kind, k in self.phase_sems:
            self.free_sems[kind].append(k)
        self.phase_sems = []

    def new_dma_sem(self, kind="ld"):
        if self.free_sems[kind]:
            k = self.free_sems[kind].pop()
            self.phase_sems.append((kind, k))
            return k
        k = "d%d" % self.n_dma
        self.phase_sems.append((kind, k))
        self.n_dma += 1
        self.semh[k] = self.stack.enter_context(self.nc.semaphore("s_" + k))
        self.dma_cnt[k] = 0
        return k

    def _deps(self, reads, writes):
        d = {}
        for b in reads:
            if b.w is not None and d.get(b.w[0], -1) < b.w[1]:
                d[b.w[0]] = b.w[1]
        for b in writes:
            if b.w is not None and d.get(b.w[0], -1) < b.w[1]:
                d[b.w[0]] = b.w[1]
            for k, v in b.r.items():
                if d.get(k, -1) < v:
                    d[k] = v
        return d

    def _waits(self, E, d):
        kn = self.known[E]
        for k, v in d.items():
            if k == E:
                if E == "pe":
                    continue
            if kn.get(k, -1) >= v:
                continue
            kn[k] = v
            sem = self.semh[k]
            self.q[E].append(lambda eng, sem=sem, v=v: eng.wait_ge(sem, v))
            self.n_instr += 1

    def op(self, E, fn, reads=(), writes=(), inc=True):
        self._waits(E, self._deps(reads, writes))
        v = self.tick[E] + 1
        if inc:
            self.tick[E] = v
            sem = self.semh[E]
            self.q[E].append(lambda eng, fn=fn, sem=sem: fn(eng).then_inc(sem, 1))
        else:
            self.q[E].append(lambda eng, fn=fn: fn(eng))
        self.n_instr += 1
        for b in reads:
            b.r[E] = v
        for b in writes:
            b.w = (E, v)
            b.r = {}

    def dma(self, Q, semk, out_ap, in_ap, reads=(), writes=()):
        self._waits(Q, self._deps(reads, writes))
        self.dma_cnt[semk] += 16
        v = self.dma_cnt[semk]
        sem = self.semh[semk]
        self.q[Q].append(lambda eng, o=out_ap, i=in_ap, sem=sem:
                         eng.dma_start(out=o, in_=i).then_inc(sem, 16))
        self.n_instr += 1
        for b in reads:
            b.r[semk] = v
        for b in writes:
            b.w = (semk, v)
            b.r = {}

    def barrier(self):
        for E in self.QE:
            d = {e: self.tick[e] for e in self.CE if self.tick[e] > 0}
            for k, v in self.dma_cnt.items():
                if v > 0:
                    d[k] = v
            kn = self.known[E]
            for k, v in d.items():
                if k == E:
                    continue
                if kn.get(k, -1) >= v:
                    continue
                kn[k] = v
                sem = self.semh[k]
                self.q[E].append(lambda eng, sem=sem, v=v: eng.wait_ge(sem, v))

    def replay(self):
        nc = self.nc
        with nc.Block() as block:
            @block.tensor
            def _(e):
                for f in self.q["pe"]:
                    f(e)

            @block.scalar
            def _(e):
                for f in self.q["act"]:
                    f(e)

            @block.vector
            def _(e):
                for f in self.q["dve"]:
                    f(e)

            @block.gpsimd
            def _(e):
                for f in self.q["pool"]:
                    f(e)

            @block.sync
            def _(e):
                for f in self.q["sp"]:
                    f(e)
        self.q = {e: [] for e in self.QE}


def build(NT, stop_after=99, dbg=False):
    import os as _os0
    LIGHT = bool(_os0.environ.get("KDBG_LIGHT"))
    nc = bass.Bass("TRN2", target_bir_lowering=False)
    NCH = NT // C
    scratch_kind = "ExternalOutput" if dbg else "Internal"

    def din(name, shape, dt=F32):
        return nc.dram_tensor(name, list(shape), dt, kind="ExternalInput").ap()

    def dscr(name, shape, dt=BF16):
        return nc.dram_tensor(name, list(shape), dt, kind=scratch_kind).ap()

    x_d = din("x", [NT, D])
    ctx_d = din("ctx", [NCTX, D])
    pc_d = din("pcols", [128, NPC])
    adaw_d = din("ada_w", [D, 3 * D] if not LIGHT else [8, 8])
    adabg_d = din("ada_b_gate", [D])
    w_in_d = din("w_in", [D, N_IN] if not LIGHT else [8, 8])
    wup_d = din("a_w_up", [128, D])
    aup_d = din("a_a_up", [128, D])
    awo_d = din("a_w_out", [D, D] if not LIGHT else [8, 8])
    rwo_d = din("r_w_out", [2 * D, D] if not LIGHT else [8, 8])
    wo_d = din("w_out", [D, D] if not LIGHT else [8, 8])
    rdec_d = din("r_decay", [8])
    fnw_d = din("final_norm_w", [D])
    masks_d = din("masks", [128, 8, 512], BF16)
    identb_d = din("identb", [128, 128], BF16)
    ident4_d = din("ident4", [128, 512], BF16)
    lmasks_d = din("lmasks", [2, 128, 7, 512], BF16)
    bones_d = din("blockones", [128, 128], BF16)
    iota_d = din("iota", [128, 8])
    cos_d = din("rope_cos", [NT, 256])
    sin_d = din("rope_sin", [NT, 256])
    y_d = nc.dram_tensor("y", [NT, D], F32, kind="ExternalOutput").ap()

    winb_d = dscr("winb", [D, N_IN])
    projT = {"lat": dscr("projT", [NROW, NT]), "ctx": dscr("projTc", [NROW, NCTX])}
    k_tm = {"lat": dscr("k_tm", [NT, D]), "ctx": dscr("k_tmc", [NCTX, D])}
    v_tm = {"lat": dscr("v_tm", [NT, 2 * D]), "ctx": dscr("v_tmc", [NCTX, 2 * D])}
    obwd_a = dscr("obwd_a", [NT, D])
    bonb_d = dscr("bonb", [D, NT])
    obwd_r = dscr("obwd_r", [NT, 2 * D])
    gall = dscr("gall", [3 * D, NT])
    dbg_oa = dscr("dbg_oa", [NT, D], F32) if dbg else None
    dbg_or = dscr("dbg_or", [NT, 2 * D], F32) if dbg else None

    with ExitStack() as st:
        S = Sched(nc, st)

        uid = [0]

        def sb(stack, name, shape, dt, dma=False):
            uid[0] += 1
            t = stack.enter_context(nc.sbuf_tensor("sb%d_%s" % (uid[0], name), list(shape), dt))
            return Buf(name, t, S.new_dma_sem("st" if dma == "st" else "ld") if dma else None)

        def sbring(stack, name, shape, dt, n, dma=False):
            return Ring([sb(stack, "%s%d" % (name, i), shape, dt, dma) for i in range(n)])

        def psb(stack, name, shape, dt):
            uid[0] += 1
            return Buf(name, stack.enter_context(nc.psum_tensor("ps%d_%s" % (uid[0], name), list(shape), dt)))

        def load(buf, out_ap, in_ap, extra_reads=()):
            S.dma("sp", buf.sem, out_ap, in_ap, reads=list(extra_reads), writes=[buf])

        def store(buf, out_ap, in_ap):
            S.dma("pool", buf.sem, out_ap, in_ap, reads=[buf], writes=[])

        def MM(out, lhsT, rhs, start=True, stop=True, skip=False):
            if skip:
                return lambda e: e.matmul(out, lhsT=lhsT, rhs=rhs, start=start, stop=stop, skip_group_check=True)
            return lambda e: e.matmul(out, lhsT=lhsT, rhs=rhs, start=start, stop=stop)

        def TR(out, in_, ident):
            return lambda e: e.transpose(out, in_, ident)

        def ACT(out, in_, func, scale=1.0, bias=0.0, accum=None):
            if accum is None:
                return lambda e: e.activation(out=out, in_=in_, func=func, bias=bias, scale=scale)
            return lambda e: e.activation(out=out, in_=in_, func=func, bias=bias, scale=scale, accum_out=accum)

        def TT(out, a, b, op):
            return lambda e: e.tensor_tensor(out=out, in0=a, in1=b, op=op)

        def TS(out, a, s1, s2, op0, op1):
            if s2 is None:
                return lambda e: e.tensor_scalar(out=out, in0=a, scalar1=s1, scalar2=None, op0=op0)
            return lambda e: e.tensor_scalar(out=out, in0=a, scalar1=s1, scalar2=s2, op0=op0, op1=op1)

        def STT(out, a, s, b, op0, op1):
            return lambda e: e.scalar_tensor_tensor(out=out, in0=a, scalar=s, in1=b, op0=op0, op1=op1)

        def CP(out, in_):
            return lambda e: e.tensor_copy(out=out, in_=in_)

        def MS(out, val):
            return lambda e: e.memset(out, val)

        def bc3(ap2, n):
            return ap2.unsqueeze(2).to_broadcast([128, ap2.shape[1], n])

        pc = sb(st, "pc", [128, NPC], F32, dma=True)
        dcol = sb(st, "dcol", [128, 48], F32)
        modc = sb(st, "modc", [128, 24, 2], F32)
        masks = sb(st, "masks", [128, 8, 512], BF16, dma=True)
        identb = sb(st, "identb", [128, 128], BF16, dma=True)
        ident4 = sb(st, "ident4", [128, 512], BF16, dma=True)
        bones = sb(st, "bones", [128, 128], BF16, dma=True)
        iota = sb(st, "iota", [128, 8], F32, dma=True)
        rtab = sb(st, "rtab", [128, 8, 8], F32)
        wupb = sb(st, "wupb", [128, D], BF16)
        aupb = sb(st, "aupb", [128, D], BF16)
        gate_row = sb(st, "gate_row", [128, D], F32)
        onesf = sb(st, "onesf", [128, 128], F32)
        S.op("pool", MS(onesf[:], 1.0), writes=[onesf])
        U_S, U_I, L_S, L_I, NU_S, NU_I, NL_S, NL_I = range(8)

        def pcol(name, i):
            o = PCO[name] + i
            return pc[:, o:o + 1]

        def phase0():
            with ExitStack() as ph:
                S.begin_phase()
                load(pc, pc[:], pc_d)
                load(masks, masks[:], masks_d)
                load(identb, identb[:], identb_d)
                load(ident4, ident4[:], ident4_d)
                load(bones, bones[:], bones_d)
                load(iota, iota[:], iota_d)
                adabg = sb(ph, "adabg", [128, D], F32, dma=True)
                load(adabg, adabg[:], adabg_d.partition_broadcast(128))
                rdec = sb(ph, "rdec", [128, 8], F32, dma=True)
                load(rdec, rdec[:], rdec_d.partition_broadcast(128))
                adaw = sb(ph, "adaw", [128, 8, 3 * D], F32, dma=True)
                for kc in range(8):
                    if LIGHT:
                        S.op("pool", MS(adaw[:, kc, :], 0.0), writes=[adaw])
                        continue
                    load(adaw, adaw[:, kc, :], adaw_d[kc * 128:(kc + 1) * 128, :])
                lt = sbring(ph, "lt", [128, D], F32, 2, dma=True)
                for src, dst, eng in ((wup_d, wupb, "dve"), (aup_d, aupb, "pool")):
                    t = lt.next()
                    load(t, t[:], src)
                    S.op(eng, CP(dst[:], t[:]), reads=[t], writes=[dst])
                sc = sb(ph, "sc", [128, 8, 2], F32)
                th = sb(ph, "th0", [128, 16], F32)
                cc = pc[:, PCO["c"]:PCO["c"] + 16]
                S.op("act", ACT(th[:], cc, AF.Tanh, scale=0.5), reads=[pc], writes=[th])
                S.op("dve", TS(th[:], th[:], 0.5, 0.5, ALU.mult, ALU.add), reads=[th], writes=[th])
                S.op("dve", TT(sc[:].rearrange("p k t -> p t k"), th[:].rearrange("p (t k) -> p t k", t=2),
                               cc.rearrange("p (t k) -> p t k", t=2), ALU.mult), reads=[th, pc], writes=[sc])
                pmod = psb(ph, "pmod", [128, 24, 2], F32)
                for fb in range(24):
                    for kc in range(8):
                        S.op("pe", MM(pmod[:, fb, :], adaw[:, kc, fb * 128:(fb + 1) * 128], sc[:, kc, :],
                                      start=(kc == 0), stop=(kc == 7)), reads=[adaw, sc], writes=[pmod], inc=(kc == 7))
                S.op("dve", TT(modc[:], pmod[:], bc3(pc[:, PCO["ada_b"]:PCO["ada_b"] + 24], 2), ALU.add),
                     reads=[pmod, pc], writes=[modc])
                for t in range(2):
                    S.op("dve", TS(dcol[:, 24 + 8 * t:32 + 8 * t], modc[:, 8:16, t], 1.0, None, ALU.add, None),
                         reads=[modc], writes=[dcol])
                    S.op("dve", TT(dcol[:, 8 * t:8 * t + 8], dcol[:, 24 + 8 * t:32 + 8 * t],
                                   pc[:, PCO["norm_w"]:PCO["norm_w"] + 8], ALU.mult), reads=[dcol, pc], writes=[dcol])
                S.op("dve", TS(dcol[:, 16:24], pc[:, PCO["k_a"]:PCO["k_a"] + 8], -1.0, 1.0, ALU.mult, ALU.add),
                     reads=[pc], writes=[dcol])
                scb = sb(ph, "scb", [128, 8, 128], F32)
                for kc in range(8):
                    S.op("act", ACT(scb[:, kc, :], onesf[:], AF.Identity, scale=sc[:, kc, 0:1]),
                         reads=[onesf, sc], writes=[scb])
                pg = [psb(ph, "pg%d" % i, [128, 512], F32) for i in range(2)]
                for cb in range(2):
                    for kc in range(8):
                        S.op("pe", MM(pg[cb][:], scb[:, kc, :], adaw[:, kc, 2048 + cb * 512:2048 + (cb + 1) * 512],
                                      start=(kc == 0), stop=(kc == 7)), reads=[scb, adaw], writes=[pg[cb]], inc=(kc == 7))
                    S.op("dve", TT(gate_row[:, cb * 512:(cb + 1) * 512], pg[cb][:], adabg[:, cb * 512:(cb + 1) * 512], ALU.add),
                         reads=[pg[cb], adabg], writes=[gate_row])
                lgt = sb(ph, "lgt", [128, 8], F32)
                S.op("act", ACT(lgt[:], rdec[:], AF.Exp), reads=[rdec], writes=[lgt])
                S.op("dve", TS(lgt[:], lgt[:], -1.0, None, ALU.mult, None), reads=[lgt], writes=[lgt])
                for i, (cf, cbk) in enumerate(((1, 4), (0, 3), (2, 5))):
                    S.op("act", ACT(rtab[:, i, 0:4], lgt[:, 0:4], AF.Exp, scale=iota[:, cf:cf + 1]), reads=[lgt, iota], writes=[rtab])
                    S.op("act", ACT(rtab[:, i, 4:8], lgt[:, 4:8], AF.Exp, scale=iota[:, cbk:cbk + 1]), reads=[lgt, iota], writes=[rtab])
                S.op("act", ACT(rtab[:, 3, :], lgt[:], AF.Exp, scale=128.0), reads=[lgt], writes=[rtab])
                S.op("dve", TS(rtab[:, 0, :], rtab[:, 0, :], 1.0 / 16.0, None, ALU.mult, None), reads=[rtab], writes=[rtab])
                S.op("dve", TS(rtab[:, 2, :], rtab[:, 2, :], 1.0 / 16.0, None, ALU.mult, None), reads=[rtab], writes=[rtab])
                wf = sbring(ph, "wf", [128, 3136], F32, 2, dma=True)
                wb = sbring(ph, "wb", [128, 3136], BF16, 2, dma="st")
                engs = ("dve", "act", "pool", "dve")
                n = 0
                for kc in range(8 if not LIGHT else 0):
                    for cb in range(4):
                        f = wf.next()
                        b = wb.next()
                        load(f, f[:], w_in_d[kc * 128:(kc + 1) * 128, cb * 3136:(cb + 1) * 3136])
                        e = engs[n % 4]
                        n += 1
                        if e == "act":
                            S.op("act", ACT(b[:], f[:], AF.Copy), reads=[f], writes=[b])
                        else:
                            S.op(e, CP(b[:], f[:]), reads=[f], writes=[b])
                        store(b, winb_d[kc * 128:(kc + 1) * 128, cb * 3136:(cb + 1) * 3136], b[:])
                S.barrier()
                S.replay()
                S.end_phase()

        def phase1():
            with ExitStack() as ph:
                S.begin_phase()
                xs = sbring(ph, "xs", [128, D], F32, 4, dma=True)
                xn = sbring(ph, "xn", [128, D], BF16, 4)
                junk = sb(ph, "junk", [128, D], BF16)
                ss = sbring(ph, "ss", [128, 4], F32, 2)
                rstd = sbring(ph, "rstd", [128, 4], F32, 2)
                uT = sbring(ph, "uT", [128, 8, 512], BF16, 2)
                wp = sbring(ph, "wp", [128, 8, 512], BF16, 3, dma=True)
                stg = sbring(ph, "stg", [128, 512], BF16, 4, dma="st")
                cosr = sbring(ph, "cosr", [128, 512], F32, 4, dma=True)
                sinr = sbring(ph, "sinr", [128, 512], F32, 4, dma=True)
                t1r = sbring(ph, "t1r", [128, 512], F32, 2)
                t2r = sbring(ph, "t2r", [128, 512], F32, 2)
                qr = sbring(ph, "qr", [128, D], BF16, 4)
                kr = sbring(ph, "kr", [128, D], BF16, 4, dma="st")
                vr = sbring(ph, "vr", [128, 2 * D], BF16, 4, dma="st")
                qTs = sbring(ph, "qTs", [128, 8, 512], BF16, 1, dma="st")
                kTs = sbring(ph, "kTs", [128, 8, 512], BF16, 1, dma="st")
                pmm = Ring([psb(ph, "pmm%d" % i, [128, 512], F32) for i in range(4)])
                ptr = Ring([psb(ph, "ptr%d" % i, [128, 1024], BF16) for i in range(3)])
                winb_v = winb_d.rearrange("(kc p) n -> p kc n", p=128)
                evac_n = [0]

                def evac(out_ap, in_ap, reads, writes):
                    evac_n[0] += 1
                    if evac_n[0] % 2:
                        S.op("act", ACT(out_ap, in_ap, AF.Copy), reads=reads, writes=writes)
                    else:
                        S.op("dve", CP(out_ap, in_ap), reads=reads, writes=writes)

                tiles = [("ctx", 0, NCTX)] + [("lat", t0, 512) for t0 in range(0, NT, 512)]
                for (src, tok0, tw) in tiles:
                    nsub = tw // 128
                    xsrc = ctx_d if src == "ctx" else x_d
                    t = 1 if src == "ctx" else 0
                    s_ = ss.next()
                    r_ = rstd.next()
                    xt = []
                    for s in range(nsub):
                        xb = xs.next()
                        load(xb, xb[:], xsrc[tok0 + s * 128:tok0 + (s + 1) * 128, :])
                        S.op("act", ACT(junk[:], xb[:], AF.Square, accum=s_[:, s:s + 1]), reads=[xb], writes=[junk, s_])
                        xt.append(xb)
                    S.op("act", ACT(r_[:, 0:nsub], s_[:, 0:nsub], AF.Sqrt, scale=1.0 / D, bias=1e-6), reads=[s_], writes=[r_])
                    S.op("dve", lambda e, r_=r_, nsub=nsub: e.reciprocal(out=r_[:, 0:nsub], in_=r_[:, 0:nsub]), reads=[r_], writes=[r_])
                    xns = []
                    for s in range(nsub):
                        xnb = xn.next()
                        S.op("pool" if s % 2 else "dve", TS(xnb[:], xt[s][:], r_[:, s:s + 1], None, ALU.mult, None),
                             reads=[xt[s], r_], writes=[xnb])
                        xns.append(xnb)
                    u = uT.next()
                    for jp in range(4):
                        pt = ptr.next()
                        for jj in range(2):
                            j = jp * 2 + jj
                            for s in range(nsub):
                                S.op("pe", TR(pt[:, jj * 512 + s * 128:jj * 512 + (s + 1) * 128], xns[s][:, j * 128:(j + 1) * 128], identb[:]),
                                     reads=[xns[s], identb], writes=[pt], inc=(s == nsub - 1))
                            S.op("act", ACT(u[:, j, 0:tw], pt[:, jj * 512:jj * 512 + tw], AF.Identity,
                                            scale=dcol[:, 8 * t + j:8 * t + j + 1], bias=modc[:, j, t:t + 1]),
                                 reads=[pt, dcol, modc], writes=[u])
                    pieces = []
                    for i in range(6):
                        pieces.append(("fm", CO["rkv"] + i * 512, 512, RO["rkv"] + i * 512))
                    pieces.append(("fm", CO["lw"], 256, RO["lora"]))
                    pieces += [("k", CO["k"] + i * 512, 512, i) for i in range(2)]
                    pieces += [("v", CO["v"] + i * 512, 512, i) for i in range(4)]
                    if src == "lat":
                        pieces += [("q", CO["q"] + i * 512, 512, i) for i in range(2)]
                        pieces += [("fm", CO["ga"] + i * 512, 512, RO["ga"] + i * 512) for i in range(2)]
                        pieces += [("fm", CO["gr"] + i * 512, 512, RO["gr"] + i * 512) for i in range(4)]
                        pieces += [("fm", CO["ma"] + i * 512, 512, RO["ma"] + i * 512) for i in range(2)]
                        pieces += [("fm", CO["mb"] + i * 512, 512, RO["mb"] + i * 512) for i in range(2)]
                    tm_pieces = [p for p in pieces if p[0] != "fm"]
                    fm_pieces = [p for p in pieces if p[0] == "fm"]
                    for (kind, c0, ncol, r0) in fm_pieces:
                        w = wp.next()
                        load(w, w[:, :, 0:ncol], winb_v[:, :, c0:c0 + ncol])
                        for bi in range(ncol // 128):
                            pm = pmm.next()
                            for kc in range(8):
                                S.op("pe", MM(pm[:, 0:tw], w[:, kc, bi * 128:(bi + 1) * 128], u[:, kc, 0:tw],
                                              start=(kc == 0), stop=(kc == 7)), reads=[w, u], writes=[pm], inc=(kc == 7))
                            sg = stg.next()
                            evac(sg[:, 0:tw], pm[:, 0:tw], [pm], [sg])
                            store(sg, projT[src][r0 + bi * 128:r0 + (bi + 1) * 128, tok0:tok0 + tw], sg[:, 0:tw])
                    qT = qTs.next()
                    kT = kTs.next()
                    subs = []
                    for s in range(nsub):
                        tk0 = tok0 + s * 128
                        cs = cosr.next()
                        sn = sinr.next()
                        if src == "lat":
                            for hh in range(2):
                                load(cs, cs[:, hh * 256:(hh + 1) * 256], cos_d[tk0:tk0 + 128, :])
                                load(sn, sn[:, hh * 256:(hh + 1) * 256], sin_d[tk0:tk0 + 128, :])
                        subs.append((cs, sn, qr.next() if src == "lat" else None, kr.next(), vr.next()))
                    for (kind, c0, ncol, idx) in tm_pieces:
                        w = wp.next()
                        load(w, w[:], winb_v[:, :, c0:c0 + 512])
                        for s in range(nsub):
                            cs, sn, qb, kb, vb = subs[s]
                            pm = pmm.next()
                            for kc in range(8):
                                S.op("pe", MM(pm[:], u[:, kc, s * 128:(s + 1) * 128], w[:, kc, :],
                                              start=(kc == 0), stop=(kc == 7)), reads=[w, u], writes=[pm], inc=(kc == 7))
                            if kind == "v":
                                evac(vb[:, idx * 512:(idx + 1) * 512], pm[:], [pm], [vb])
                            elif src == "ctx":
                                evac(kb[:, idx * 512:(idx + 1) * 512], pm[:], [pm], [kb])
                            else:
                                dst = qb if kind == "q" else kb
                                t1 = t1r.next()
                                t2 = t2r.next()
                                S.op("dve", TT(t1[:], pm[:], cs[:], ALU.mult), reads=[pm, cs], writes=[t1])
                                p4 = pm[:].rearrange("p (g h i) -> p g h i", g=4, h=2)
                                s4 = sn[:].rearrange("p (g h i) -> p g h i", g=4, h=2)
                                t4 = t2[:].rearrange("p (g h i) -> p g h i", g=4, h=2)
                                S.op("dve", TT(t4[:, :, 0, :], p4[:, :, 1, :], s4[:, :, 0, :], ALU.mult), reads=[pm, sn], writes=[t2])
                                S.op("dve", TT(t4[:, :, 1, :], p4[:, :, 0, :], s4[:, :, 1, :], ALU.mult), reads=[pm, sn], writes=[t2])
                                S.op("pool", TT(dst[:, idx * 512:(idx + 1) * 512], t1[:], t2[:], ALU.add), reads=[t1, t2], writes=[dst])
                    for s in range(nsub):
                        tk0 = tok0 + s * 128
                        cs, sn, qb, kb, vb = subs[s]
                        store(vb, v_tm[src][tk0:tk0 + 128, :], vb[:])
                        store(kb, k_tm[src][tk0:tk0 + 128, :], kb[:])
                        for (srcb, dstT) in ((qb, qT), (kb, kT)):
                            if srcb is None:
                                continue
                            pt = ptr.next()
                            for j in range(8):
                                S.op("pe", TR(pt[:, j * 128:(j + 1) * 128], srcb[:, j * 128:(j + 1) * 128], identb[:]),
                                     reads=[srcb, identb], writes=[pt], inc=(j == 7))
                            evac(dstT[:, :, s * 128:(s + 1) * 128], pt[:].rearrange("p (j t) -> p j t", j=8), [pt], [dstT])
                    if src == "lat":
                        store(qT, projT[src][RO["q"]:RO["q"] + D, tok0:tok0 + tw].rearrange("(j p) t -> p j t", p=128), qT[:, :, 0:tw])
                    store(kT, projT[src][RO["k"]:RO["k"] + D, tok0:tok0 + tw].rearrange("(j p) t -> p j t", p=128), kT[:, :, 0:tw])
                S.barrier()
                S.replay()
                S.end_phase()

        def rwkv_sweep(d, interleave=True):
            fwd = (d == 0)
            pb = 64 * d
            if fwd:
                mQ, mP, mAk, mRb, mRk = NU_S, NL_S, U_S, NU_I, U_I
            else:
                mQ, mP, mAk, mRb, mRk = NL_S, NU_S, L_S, NL_I, L_I
            with ExitStack() as ph:
                S.begin_phase()
                cdiag = sb(ph, "cdiag", [128, 72, 128], BF16)
                rkvraw = sbring(ph, "rkvraw", [128, 24, 130], BF16, 1, dma=True)
                lora = sbring(ph, "lora", [128, 2, 128], BF16, 2, dma=True)
                thr = sbring(ph, "thr", [128, 128], BF16, 2)
                if fwd:
                    gaTr = sbring(ph, "gaTr", [128, 8, 128], BF16, 2, dma=True)
                    oblr = sbring(ph, "oblr", [128, D], BF16, 2, dma=True)
                    bblr = sbring(ph, "bblr", [128, 8, 128], BF16, 2, dma=True)
                    bonfr = sbring(ph, "bonfr", [128, 8, 128], BF16, 2)
                    oa = sb(ph, "oa", [128, D], F32)
                    sqt = sb(ph, "sqt", [128, D], F32)
                    ynb = sb(ph, "ynb", [128, D], BF16)
                    GTr = sbring(ph, "GTr", [128, 8, 128], BF16, 1, dma="st")
                    stt_ = {n: sb(ph, "st_" + n, [128, 16], F32) for n in ("s1", "s2", "mean", "msq", "var", "rstd", "nb")}
                else:
                    ostr = sbring(ph, "ostr", [128, D], BF16, 1, dma="st")
                    bstr = sbring(ph, "bstr", [128, 8, 128], BF16, 1, dma="st")
                T = {n: sb(ph, "t_" + n, [128, 512], F32) for n in
                     ("rS", "kS", "kk", "rs", "sg", "a", "prefix", "Dm", "PT", "b", "kd", "PM", "DMm") + (() if fwd else ("DT",))}
                Er = sbring(ph, "Er", [128, 512], F32, 2)
                vSr = sbring(ph, "vSr", [128, 512], BF16, 2)
                sq = sb(ph, "sq", [128, 512], BF16)
                pr = sb(ph, "pr", [128, 512], BF16)
                WCr = sbring(ph, "WCr", [128, 8], F32, 2)
                outs = {n: [sbring(ph, "%s%d_" % (n, hf), [128, 512], BF16, 2) for hf in range(2)]
                        for n in ("ATe", "ATo", "ATse", "ATso", "BT", "KT", "RTe", "RTo", "RTse", "RTso", "KH", "BH")}
                for n in ("ATe", "ATo", "ATse", "ATso", "RTe", "RTo", "RTse", "RTso"):
                    for hf in range(2):
                        for b_ in outs[n][hf].b:
                            S.op("pool", MS(b_[:], 0.0), writes=[b_])
                Vtmr = sbring(ph, "Vtmr", [128, D], BF16, 2)
                Khtmr = sbring(ph, "Khtmr", [128, D], BF16, 2)
                nBhtmr = sbring(ph, "nBhtmr", [128, D], BF16, 2)
                Uallr = sbring(ph, "Uallr", [128, D], BF16, 1)
                Tst = sb(ph, "Tst", [128, 8, 64], F32)
                Tbf = sb(ph, "Tbf", [128, 8, 64], BF16)
                dbgsem = S.new_dma_sem("st") if dbg else None
                LM = sb(ph, "LM", [128, 7, 512], BF16, dma=True)
                load(LM, LM[:], lmasks_d[d])
                NSL = 2 if fwd else 4
                slots = []
                for sl in range(NSL):
                    slots.append(dict(
                        nN=sb(ph, "cnN%d" % sl, [128, 512], BF16), nB=sbring(ph, "cnB%d_" % sl, [128, 512], BF16, 2),
                        IT=sbring(ph, "cIT%d_" % sl, [128, 512], BF16, 2),
                        M=sbring(ph, "cM%d_" % sl, [128, 512], BF16, 2), MT=sbring(ph, "cMT%d_" % sl, [128, 512], BF16, 2),
                        Aak=sb(ph, "cAak%d" % sl, [128, 512], BF16), nArb=sb(ph, "cnArb%d" % sl, [128, 512], BF16),
                        Ark=sb(ph, "cArk%d" % sl, [128, 512], BF16), Xbf=sb(ph, "cX%d" % sl, [128, 256], BF16)))
                pprep = Ring([psb(ph, "pprep%d" % i, [128, 512], F32) for i in range(2)])
                ptr = psb(ph, "ptrA", [128, 1024], BF16)
                pxo = Ring([psb(ph, "pxo%d" % i, [128, 512], F32) for i in range(2)])
                pch = Ring([psb(ph, "pch%d" % i, [128, 512], F32) for i in range(3)])

                for i in range(72):
                    S.op("pool" if i % 2 else "dve", TS(cdiag[:, i, :], identb[:], pcol("conv", i), None, ALU.mult, None),
                         reads=[identb, pc], writes=[cdiag])
                S.op("pool", MS(Tst[:], 0.0), writes=[Tst])
                S.op("pool", MS(Tbf[:], 0.0), writes=[Tbf])

                def v4(ap):
                    return ap.rearrange("p (i t) -> p i t", i=4)

                def prep(src, c, with_out, ctxd):
                    ntok = NCTX if src == "ctx" else NT
                    t0 = c * 128
                    rk = rkvraw.next()
                    lo, hi = max(t0 - 1, 0), min(t0 + 129, ntok)
                    a0 = lo - (t0 - 1)
                    for part in range(3):
                        load(rk, rk[:, 8 * part:8 * part + 8, a0:a0 + hi - lo],
                             projT[src][1024 * part:1024 * (part + 1), lo:hi].rearrange("(c p) t -> p c t", p=128))
                    if t0 == 0:
                        S.op("dve", MS(rk[:, :, 0:1], 0.0), writes=[rk])
                    if t0 + 128 == ntok:
                        S.op("dve", MS(rk[:, :, 129:130], 0.0), writes=[rk])
                    lr = lora.next()
                    load(lr, lr[:], projT[src][RO["lora"]:RO["lora"] + 256, t0:t0 + 128].rearrange("(c p) t -> p c t", p=128))
                    if fwd and with_out:
                        ga = gaTr.next()
                        load(ga, ga[:], projT[src][RO["ga"]:RO["ga"] + D, t0:t0 + 128].rearrange("(c p) t -> p c t", p=128))
                        obl = oblr.next()
                        load(obl, obl[:], obwd_a[t0:t0 + 128, :])
                        bbl = bblr.next()
                        load(bbl, bbl[:], bonb_d[:, t0:t0 + 128].rearrange("(c p) t -> p c t", p=128))
                        ctxd.update(ga=ga, obl=obl, bbl=bbl, bonf=bonfr.next())
                    if (not fwd) and with_out:
                        ctxd["bst"] = bstr.next()
                    thb = thr.next()
                    S.op("act", ACT(thb[:], lr[:, 0, :], AF.Tanh), reads=[lr], writes=[thb])
                    WCt = WCr.next()
                    ctxd["WC"] = WCt
                    ctxd["vS"] = []
                    yield
                    for hf in range(2):
                        fcs = [4 * hf + i for i in range(4)]
                        o = {n: outs[n][hf].next() for n in outs}
                        ctxd.setdefault("o", []).append(o)
                        vS = vSr.next()
                        ctxd["vS"].append(vS)
                        rS, kS, kk, rs, sg, a_, prefix, Dm, PT, b_, kd = (T[n] for n in
                            ("rS", "kS", "kk", "rs", "sg", "a", "prefix", "Dm", "PT", "b", "kd"))
                        DT = T.get("DT")
                        for (nm, base) in (("r", 0), ("k", 8), ("v", 16)):
                            if nm == "r" and not with_out:
                                continue
                            pcv = pprep.next()
                            for i, fc in enumerate(fcs):
                                blk = base + fc
                                for j in range(3):
                                    S.op("pe", MM(pcv[:, i * 128:(i + 1) * 128], cdiag[:, j * 24 + blk, :], rk[:, blk, j:j + 128],
                                                  start=(j == 0), stop=(j == 2)), reads=[cdiag, rk], writes=[pcv], inc=(j == 2 and i == 3))
                            if nm == "r":
                                S.op("act", ACT(rS[:], pcv[:], AF.Copy), reads=[pcv], writes=[rS])
                            elif nm == "k":
                                S.op("dve", CP(kS[:], pcv[:]), reads=[pcv], writes=[kS])
                            else:
                                S.op("act", ACT(vS[:], pcv[:], AF.Copy), reads=[pcv], writes=[vS])
                        yield
                        S.op("dve", TT(v4(kk[:]), v4(kS[:]), bc3(pc[:, PCO["k_k"] + 4 * hf:PCO["k_k"] + 4 * hf + 4], 128), ALU.mult),
                             reads=[kS, pc], writes=[kk])
                        S.op("act", ACT(sq[:], kk[:], AF.Square), reads=[kk], writes=[sq])
                        pss = pprep.next()
                        for i in range(4):
                            S.op("pe", MM(pss[:, i * 128:(i + 1) * 128], bones[:], sq[:, i * 128:(i + 1) * 128]), reads=[bones, sq], writes=[pss], inc=(i == 3))
                        S.op("act", ACT(rs[:], pss[:], AF.Sqrt, bias=1e-12), reads=[pss], writes=[rs])
                        S.op("dve", lambda e, rs=rs: e.reciprocal(out=rs[:], in_=rs[:]), reads=[rs], writes=[rs])
                        S.op("pool", TT(kk[:], kk[:], rs[:], ALU.mult), reads=[kk, rs], writes=[kk])
                        yield
                        pz = pprep.next()
                        for i, fc in enumerate(fcs):
                            S.op("pe", MM(pz[:, i * 128:(i + 1) * 128], wupb[pb:pb + 64, fc * 128:(fc + 1) * 128], thb[pb:pb + 64, :]),
                                 reads=[wupb, thb], writes=[pz], inc=(i == 3))
                        paz = pprep.next()
                        for i, fc in enumerate(fcs):
                            S.op("pe", MM(paz[:, i * 128:(i + 1) * 128], aupb[pb:pb + 64, fc * 128:(fc + 1) * 128], lr[pb:pb + 64, 1, :]),
                                 reads=[aupb, lr], writes=[paz], inc=(i == 3))
                        w0o = PCO["w0"] + d * 8 + 4 * hf
                        a0o = PCO["a0"] + d * 8 + 4 * hf
                        S.op("dve", TT(v4(sg[:]), v4(pz[:]), bc3(pc[:, w0o:w0o + 4], 128), ALU.add), reads=[pz, pc], writes=[sg])
                        S.op("act", ACT(sg[:], sg[:], AF.Tanh, scale=0.5), reads=[sg], writes=[sg])
                        S.op("dve", TS(sg[:], sg[:], 0.5, 0.5, ALU.mult, ALU.add), reads=[sg], writes=[sg])
                        S.op("dve", TT(v4(a_[:]), v4(paz[:]), bc3(pc[:, a0o:a0o + 4], 128), ALU.add), reads=[paz, pc], writes=[a_])
                        S.op("act", ACT(a_[:], a_[:], AF.Tanh, scale=0.5), reads=[a_], writes=[a_])
                        S.op("pool", TS(a_[:], a_[:], 0.5, 0.5, ALU.mult, ALU.add), reads=[a_], writes=[a_])
                        yield
                        for i in range(4):
                            S.op("dve", lambda e, i=i: e.tensor_tensor_scan(out=prefix[:, i * 128:(i + 1) * 128], data0=onesf[:], data1=sg[:, i * 128:(i + 1) * 128],
                                                                             initial=0.0, op0=ALU.mult, op1=ALU.add), reads=[onesf, sg], writes=[prefix])
                        S.op("dve", TT(Dm[:], sg[:], prefix[:], ALU.subtract), reads=[sg, prefix], writes=[Dm])
                        totb = v4(prefix[:])[:, :, 127:128].to_broadcast([128, 4, 128])
                        S.op("dve", TT(v4(PT[:]), v4(prefix[:]), totb, ALU.subtract), reads=[prefix], writes=[PT])
                        if not fwd:
                            S.op("dve", TT(v4(DT[:]), v4(Dm[:]), totb, ALU.add), reads=[Dm, prefix], writes=[DT])
                        S.op("act", ACT(WCt[:, 4 * hf:4 * hf + 4], v4(prefix[:])[:, :, 127], AF.Exp, scale=-KAPPA), reads=[prefix], writes=[WCt])
                        S.op("dve", TT(b_[:], kk[:], a_[:], ALU.mult), reads=[kk, a_], writes=[b_])
                        S.op("dve", TT(v4(kd[:]), v4(a_[:]), bc3(pc[:, PCO["k_a"] + 4 * hf:PCO["k_a"] + 4 * hf + 4], 128), ALU.mult), reads=[a_, pc], writes=[kd])
                        S.op("dve", TT(v4(kd[:]), v4(kd[:]), bc3(dcol[:, 16 + 4 * hf:20 + 4 * hf], 128), ALU.add), reads=[kd, dcol], writes=[kd])
                        S.op("pool", TT(kd[:], kS[:], kd[:], ALU.mult), reads=[kS, kd], writes=[kd])
                        yield
                        PM, DMm = T["PM"], T["DMm"]
                        if fwd:
                            midb = v4(prefix[:])[:, :, 63:64].to_broadcast([128, 4, 128])
                            S.op("dve", TT(v4(PM[:]), v4(prefix[:]), midb, ALU.subtract), reads=[prefix], writes=[PM])
                            S.op("dve", TT(v4(DMm[:]), v4(Dm[:]), midb, ALU.add), reads=[Dm, prefix], writes=[DMm])
                            e1, e3, e4 = (prefix, -KAPPA), (Dm, KAPPA), (PT, KAPPA)
                        else:
                            midb = v4(DT[:])[:, :, 64:65].to_broadcast([128, 4, 128])
                            S.op("dve", TT(v4(PM[:]), v4(DT[:]), midb, ALU.subtract), reads=[DT], writes=[PM])
                            S.op("dve", TT(v4(DMm[:]), v4(PT[:]), midb, ALU.add), reads=[PT, DT], writes=[DMm])
                            e1, e3, e4 = (DT, -KAPPA), (PT, KAPPA), (Dm, KAPPA)
                        if with_out:
                            E = Er.next()
                            S.op("act", ACT(E[:], e1[0][:], AF.Exp, scale=e1[1]), reads=[e1[0]], writes=[E])
                            S.op("pool", TT(o["RTe"][0:64, :], rS[0:64, :], E[0:64, :], ALU.mult), reads=[rS, E], writes=[o["RTe"]])
                            S.op("dve", TT(o["RTo"][64:128, :], rS[64:128, :], E[64:128, :], ALU.mult), reads=[rS, E], writes=[o["RTo"]])
                            E = Er.next()
                            S.op("act", ACT(E[:], PM[:], AF.Exp, scale=-KAPPA), reads=[PM], writes=[E])
                            S.op("pool", TT(o["RTse"][0:64, :], rS[0:64, :], E[0:64, :], ALU.mult), reads=[rS, E], writes=[o["RTse"]])
                            S.op("dve", TT(o["RTso"][64:128, :], rS[64:128, :], E[64:128, :], ALU.mult), reads=[rS, E], writes=[o["RTso"]])
                        E = Er.next()
                        S.op("act", ACT(E[:], PM[:], AF.Exp, scale=KAPPA), reads=[PM], writes=[E])
                        S.op("pool", TT(o["BT"][:], b_[:], E[:], ALU.mult), reads=[b_, E], writes=[o["BT"]])
                        S.op("pool", TT(o["KT"][:], kd[:], E[:], ALU.mult), reads=[kd, E], writes=[o["KT"]])
                        E = Er.next()
                        S.op("act", ACT(E[:], e3[0][:], AF.Exp, scale=e3[1]), reads=[e3[0]], writes=[E])
                        S.op("pool", TT(o["ATe"][0:64, :], kk[0:64, :], E[0:64, :], ALU.mult), reads=[kk, E], writes=[o["ATe"]])
                        S.op("dve", TT(o["ATo"][64:128, :], kk[64:128, :], E[64:128, :], ALU.mult), reads=[kk, E], writes=[o["ATo"]])
                        E = Er.next()
                        S.op("act", ACT(E[:], DMm[:], AF.Exp, scale=KAPPA), reads=[DMm], writes=[E])
                        S.op("pool", TT(o["ATse"][0:64, :], kk[0:64, :], E[0:64, :], ALU.mult), reads=[kk, E], writes=[o["ATse"]])
                        S.op("dve", TT(o["ATso"][64:128, :], kk[64:128, :], E[64:128, :], ALU.mult), reads=[kk, E], writes=[o["ATso"]])
                        E = Er.next()
                        S.op("act", ACT(E[:], e4[0][:], AF.Exp, scale=e4[1]), reads=[e4[0]], writes=[E])
                        S.op("pool", TT(o["KH"][:], kd[:], E[:], ALU.mult), reads=[kd, E], writes=[o["KH"]])
                        S.op("pool", TT(o["BH"][:], b_[:], E[:], ALU.mult), reads=[b_, E], writes=[o["BH"]])
                        yield
                        if with_out:
                            S.op("dve", TT(v4(rs[:]), v4(rS[:]), bc3(pc[:, PCO["r_k"] + 4 * hf:PCO["r_k"] + 4 * hf + 4], 128), ALU.mult), reads=[rS, pc], writes=[rs])
                            S.op("pool", TT(pr[:], rs[:], kd[:], ALU.mult), reads=[rs, kd], writes=[pr])
                            pbs = pprep.next()
                            for i in range(4):
                                S.op("pe", MM(pbs[:, i * 128:(i + 1) * 128], bones[:], pr[:, i * 128:(i + 1) * 128]), reads=[bones, pr], writes=[pbs], inc=(i == 3))
                            if fwd:
                                S.op("dve", TT(ctxd["bonf"][:, 4 * hf:4 * hf + 4, :], v4(pbs[:]), v4(vS[:]), ALU.mult), reads=[pbs, vS], writes=[ctxd["bonf"]])
                            else:
                                S.op("dve", TT(ctxd["bst"][:, 4 * hf:4 * hf + 4, :], v4(pbs[:]), v4(vS[:]), ALU.mult), reads=[pbs, vS], writes=[ctxd["bst"]])
                            yield
                    Vtm, Khtm, nBhtm = Vtmr.next(), Khtmr.next(), nBhtmr.next()
                    ctxd.update(Vtm=Vtm, Khtm=Khtm, nBhtm=nBhtm)
                    for which in range(3):
                        for fc in range(8):
                            hf, i = fc // 4, fc % 4
                            srcb = ctxd["vS"][hf] if which == 0 else ctxd["o"][hf]["KH" if which == 1 else "BH"]
                            S.op("pe", TR(ptr[:, fc * 128:(fc + 1) * 128], srcb[:, i * 128:(i + 1) * 128], identb[:]),
                                 reads=[srcb, identb], writes=[ptr], inc=(fc == 7))
                        if which == 0:
                            S.op("act", ACT(Vtm[:], ptr[:], AF.Copy), reads=[ptr], writes=[Vtm])
                        elif which == 1:
                            S.op("dve", CP(Khtm[:], ptr[:]), reads=[ptr], writes=[Khtm])
                        else:
                            S.op("act", ACT(nBhtm[:], ptr[:], AF.Copy, scale=-1.0), reads=[ptr], writes=[nBhtm])
                        yield
                    if (not fwd) and with_out:
                        store(ctxd["bst"], bonb_d[:, t0:t0 + 128].rearrange("(c p) t -> p c t", p=128), ctxd["bst"][:])

                def chain_group(g, sl, with_out, ctxd, Uall):
                    Vtm, Khtm, nBhtm = ctxd["Vtm"], ctxd["Khtm"], ctxd["nBhtm"]
                    heads = []
                    for hl in range(4):
                        fc = 2 * g + hl // 2
                        hf, i, p0 = fc // 4, fc % 4, 64 * (hl % 2)
                        o = ctxd["o"][hf]
                        sfx = "e" if p0 == 0 else "o"
                        cs_ = slice(i * 128, (i + 1) * 128)
                        heads.append(dict(fc=fc, p0=p0, h=4 * g + hl, o=o, sfx=sfx,
                                          at=o["AT" + sfx][:, cs_], ats=o["ATs" + sfx][:, cs_], rt=o["RT" + sfx][:, cs_], rts=o["RTs" + sfx][:, cs_],
                                          bt=o["BT"][:, cs_], kt=o["KT"][:, cs_]))

                    def prod(lname, rname, mask, dst, eng="dve"):
                        pp = pch.next()
                        for hl, H in enumerate(heads):
                            S.op("pe", MM(pp[:, hl * 128:(hl + 1) * 128], H[lname], H[rname]),
                                 reads=[H["o"]["ATs" + H["sfx"]], H["o"]["BT"], H["o"]["KT"]] + ([H["o"]["RTs" + H["sfx"]]] if with_out else []),
                                 writes=[pp], inc=(hl == 3))
                        S.op("dve", TT(dst[:], pp[:], masks[:, mask, :], ALU.mult), reads=[pp, masks], writes=[dst])

                    nN = sl["nN"]
                    prod("ats", "bt", mP, nN)
                    prod("kt", "ats", mAk, sl["Aak"])
                    if with_out:
                        prod("bt", "rts", mRb, sl["nArb"])
                        prod("kt", "rts", mRk, sl["Ark"])
                    yield
                    px = pxo.next()
                    for hl, H in enumerate(heads):
                        S.op("pe", MM(px[:, hl * 64:(hl + 1) * 64], H["at"], Tbf[:, H["fc"], :], start=True, stop=False),
                             reads=[H["o"]["AT" + H["sfx"]], Tbf], writes=[px], inc=False)
                        S.op("pe", MM(px[:, hl * 64:(hl + 1) * 64], sl["Aak"][:, hl * 128:(hl + 1) * 128], Vtm[:, H["h"] * 64:(H["h"] + 1) * 64], start=False, stop=True),
                             reads=[sl["Aak"], Vtm], writes=[px], inc=(hl == 3))
                    Xb = sl["Xbf"]
                    S.op("act", ACT(Xb[:], px[:, 0:256], AF.Copy), reads=[px], writes=[Xb])
                    yield
                    M, MT = ident4, ident4
                    for lev in range(7):
                        nB = sl["nB"].next()
                        S.op("pool", TT(nB[:], nN[:], LM[:, lev, :], ALU.mult), reads=[nN, LM], writes=[nB])
                        pb_ = pch.next()
                        for hl in range(4):
                            cs_ = slice(hl * 128, (hl + 1) * 128)
                            S.op("pe", MM(pb_[:, cs_], identb[:], identb[:], start=True, stop=False), reads=[identb], writes=[pb_], inc=False)
                            S.op("pe", MM(pb_[:, cs_], nB[:, cs_], MT[:, cs_], start=False, stop=True), reads=[nB, MT], writes=[pb_], inc=(hl == 3))
                        IT = sl["IT"].next()
                        S.op("act", ACT(IT[:], pb_[:], AF.Copy), reads=[pb_], writes=[IT])
                        pmt = pch.next()
                        for hl in range(4):
                            cs_ = slice(hl * 128, (hl + 1) * 128)
                            S.op("pe", MM(pmt[:, cs_], M[:, cs_], IT[:, cs_]), reads=[M, IT], writes=[pmt], inc=(hl == 3))
                        if lev < 6:
                            pm_ = pch.next()
                            for hl in range(4):
                                cs_ = slice(hl * 128, (hl + 1) * 128)
                                S.op("pe", MM(pm_[:, cs_], IT[:, cs_], M[:, cs_]), reads=[M, IT], writes=[pm_], inc=(hl == 3))
                        MTn = sl["MT"].next()
                        S.op("dve", CP(MTn[:], pmt[:]), reads=[pmt], writes=[MTn])
                        if lev < 6:
                            Mn = sl["M"].next()
                            S.op("act", ACT(Mn[:], pm_[:], AF.Copy), reads=[pm_], writes=[Mn])
                            M = Mn
                        MT = MTn
                        yield
                    pu = pxo.next()
                    for hl in range(4):
                        S.op("pe", MM(pu[:, hl * 64:(hl + 1) * 64], MT[:, hl * 128:(hl + 1) * 128], Xb[:, hl * 64:(hl + 1) * 64]),
                             reads=[MT, Xb], writes=[pu], inc=(hl == 3))
                    S.op("act", ACT(Uall[:, g * 256:(g + 1) * 256], pu[:, 0:256], AF.Copy), reads=[pu], writes=[Uall])
                    yield
                    if with_out:
                        po = pxo.next()
                        for hl, H in enumerate(heads):
                            hc = slice(H["h"] * 64, (H["h"] + 1) * 64)
                            oc = po[:, hl * 64:(hl + 1) * 64]
                            S.op("pe", MM(oc, H["rt"], Tbf[:, H["fc"], :], start=True, stop=False),
                                 reads=[H["o"]["RT" + H["sfx"]], Tbf], writes=[po], inc=False)
                            S.op("pe", MM(oc, sl["nArb"][:, hl * 128:(hl + 1) * 128], Uall[:, hc], start=False, stop=False),
                                 reads=[sl["nArb"], Uall], writes=[po], inc=False)
                            S.op("pe", MM(oc, sl["Ark"][:, hl * 128:(hl + 1) * 128], Vtm[:, hc], start=False, stop=True),
                                 reads=[sl["Ark"], Vtm], writes=[po], inc=(hl == 3))
                        if fwd:
                            S.op("dve", TT(oa[:, g * 256:(g + 1) * 256], po[:, 0:256], ctxd["obl"][:, g * 256:(g + 1) * 256], ALU.add),
                                 reads=[po, ctxd["obl"]], writes=[oa])
                        else:
                            S.op("act", ACT(ctxd["ost"][:, g * 256:(g + 1) * 256], po[:, 0:256], AF.Copy), reads=[po], writes=[ctxd["ost"]])
                    yield

                def rr(gens):
                    gens = list(gens)
                    while gens:
                        for gq in list(gens):
                            try:
                                next(gq)
                            except StopIteration:
                                gens.remove(gq)
                        yield

                def chain(src, c, with_out, ctxd):
                    t0 = c * 128
                    Uall = Uallr.next()
                    if (not fwd) and with_out:
                        ctxd["ost"] = ostr.next()
                    for pair in (((0, 1), (2, 3)) if NSL == 2 else ((0, 1, 2, 3),)):
                        yield from rr([chain_group(g_, slots[i_], with_out, ctxd, Uall) for i_, g_ in enumerate(pair)])
                    pS = pprep.next()
                    Vtm, Khtm, nBhtm, WCt = ctxd["Vtm"], ctxd["Khtm"], ctxd["nBhtm"], ctxd["WC"]
                    for fc in range(8):
                        for hb in range(2):
                            h = 2 * fc + hb
                            hc = slice(h * 64, (h + 1) * 64)
                            oc = pS[64 * hb:64 * hb + 64, fc * 64:(fc + 1) * 64]
                            S.op("pe", MM(oc, Khtm[:, hc], Vtm[:, hc], start=True, stop=False), reads=[Khtm, Vtm], writes=[pS], inc=False)
                            S.op("pe", MM(oc, nBhtm[:, hc], Uall[:, hc], start=False, stop=True), reads=[nBhtm, Uall], writes=[pS],
                                 inc=(fc == 7 and hb == 1))
                    for fc in range(8):
                        S.op("dve", STT(Tst[:, fc, :], Tst[:, fc, :], WCt[:, fc:fc + 1], pS[:, fc * 64:(fc + 1) * 64], ALU.mult, ALU.add),
                             reads=[Tst, WCt, pS], writes=[Tst])
                    S.op("act", ACT(Tbf[:], Tst[:], AF.Copy), reads=[Tst], writes=[Tbf])
                    yield
                    if (not fwd) and with_out:
                        store(ctxd["ost"], obwd_a[t0:t0 + 128, :], ctxd["ost"][:])
                    if fwd and with_out:
                        st_ = stt_
                        if dbg:
                            S.dma("pool", dbgsem, dbg_oa[t0:t0 + 128, :], oa[:], reads=[oa], writes=[])
                        o3 = oa[:].rearrange("p (h v) -> p h v", h=16)
                        S.op("dve", lambda e: e.tensor_reduce(out=st_["s1"][:], in_=o3, axis=AX.X, op=ALU.add), reads=[oa], writes=[st_["s1"]])
                        S.op("act", ACT(sqt[:], oa[:], AF.Square), reads=[oa], writes=[sqt])
                        S.op("dve", lambda e: e.tensor_reduce(out=st_["s2"][:], in_=sqt[:].rearrange("p (h v) -> p h v", h=16), axis=AX.X, op=ALU.add),
                             reads=[sqt], writes=[st_["s2"]])
                        S.op("pool", TS(st_["mean"][:], st_["s1"][:], 1.0 / 64, None, ALU.mult, None), reads=[st_["s1"]], writes=[st_["mean"]])
                        S.op("pool", TT(st_["msq"][:], st_["mean"][:], st_["mean"][:], ALU.mult), reads=[st_["mean"]], writes=[st_["msq"]])
                        S.op("dve", STT(st_["var"][:], st_["s2"][:], 1.0 / 64, st_["msq"][:], ALU.mult, ALU.subtract), reads=[st_["s2"], st_["msq"]], writes=[st_["var"]])
                        S.op("act", ACT(st_["rstd"][:], st_["var"][:], AF.Sqrt, bias=64e-5), reads=[st_["var"]], writes=[st_["rstd"]])
                        S.op("dve", lambda e: e.reciprocal(out=st_["rstd"][:], in_=st_["rstd"][:]), reads=[st_["rstd"]], writes=[st_["rstd"]])
                        S.op("dve", STT(st_["nb"][:], st_["mean"][:], -1.0, st_["rstd"][:], ALU.mult, ALU.mult), reads=[st_["mean"], st_["rstd"]], writes=[st_["nb"]])
                        yield
                        s3 = sqt[:].rearrange("p (h v) -> p h v", h=16)
                        S.op("dve", TT(s3, o3, bc3(st_["rstd"][:], 64), ALU.mult), reads=[oa, st_["rstd"]], writes=[sqt])
                        S.op("dve", TT(ynb[:].rearrange("p (h v) -> p h v", h=16), s3, bc3(st_["nb"][:], 64), ALU.add), reads=[sqt, st_["nb"]], writes=[ynb])
                        for fc in range(8):
                            S.op("pe", TR(ptr[:, fc * 128:(fc + 1) * 128], ynb[:, fc * 128:(fc + 1) * 128], identb[:]), reads=[ynb, identb], writes=[ptr], inc=(fc == 7))
                        for fc in range(8):
                            S.op("act", ACT(sqt[:, fc * 128:(fc + 1) * 128], ptr[:, fc * 128:(fc + 1) * 128], AF.Identity, scale=pcol("a_ln_w", fc), bias=pcol("a_ln_b", fc)),
                                 reads=[ptr, pc], writes=[sqt])
                        yield
                        ga, bbl = ctxd["ga"], ctxd["bbl"]
                        g3 = sqt[:].rearrange("p (c t) -> p c t", c=8)
                        sg3 = oa[:].rearrange("p (c t) -> p c t", c=8)
                        S.op("pool", TT(g3, g3, ctxd["bonf"][:], ALU.add), reads=[sqt, ctxd["bonf"]], writes=[sqt])
                        S.op("pool", TT(g3, g3, bbl[:], ALU.add), reads=[sqt, bbl], writes=[sqt])
                        S.op("act", ACT(sg3, ga[:], AF.Tanh, scale=0.5), reads=[ga], writes=[oa])
                        S.op("pool", TS(oa[:], oa[:], 0.5, 0.5, ALU.mult, ALU.add), reads=[oa], writes=[oa])
                        S.op("pool", TT(sg3, sg3, ga[:], ALU.mult), reads=[oa, ga], writes=[oa])
                        GT = GTr.next()
                        S.op("dve", TT(GT[:], g3, sg3, ALU.mult), reads=[sqt, oa], writes=[GT])
                        store(GT, gall[0:D, t0:t0 + 128].rearrange("(c p) t -> p c t", p=128), GT[:])
                        yield

                order = [("ctx", 0, False), ("ctx", 1, False)] + [("lat", c, True) for c in range(NCH)]
                if not fwd:
                    order = [("ctx", 1, False), ("ctx", 0, False)] + [("lat", c, True) for c in range(NCH - 1, -1, -1)]
                ctxs = [dict() for _ in order]
                import os as _os
                budget = [int(_os.environ.get("KDBG_STEPS", "-1"))]

                def run(gq):
                    for _ in gq:
                        if budget[0] >= 0:
                            budget[0] -= 1
                            if budget[0] < 0:
                                return False
                    return True
                ok = budget[0] != 0 and run(prep(*order[0], ctxs[0]))
                for n in range(len(order)):
                    if not ok:
                        break
                    gens = [chain(*order[n], ctxs[n])]
                    if n + 1 < len(order):
                        gens.append(prep(*order[n + 1], ctxs[n + 1]))
                    if interleave and budget[0] < 0:
                        ok = run(rr(gens))
                    else:
                        for gq in gens:
                            ok = ok and run(gq)
                    ctxs[n].clear()
                S.barrier()
                S.replay()
                S.end_phase()

        def ret_sweep(d):
            fwd = (d == 0)
            mk = U_I if fwd else L_I
            with ExitStack() as ph:
                S.begin_phase()
                qTr = sbring(ph, "qTr", [128, 8, 128], BF16, 2, dma=True)
                kTr = sbring(ph, "kTr", [128, 8, 128], BF16, 2, dma=True)
                ktmr = sbring(ph, "ktmr", [128, D], BF16, 2, dma=True)
                vtmr = sbring(ph, "vtmr", [128, 2 * D], BF16, 2, dma=True)
                Mk = sb(ph, "Mk", [128, 512], F32)
                kdf = sb(ph, "kdf", [128, D], F32)
                Sst = [sb(ph, "Sst%d" % h, [128, 2, 512], F32) for h in range(4)]
                Sbf = [sb(ph, "Sbf%d" % h, [128, 2, 512], BF16) for h in range(4)]
                STr = sbring(ph, "STr", [128, 512], BF16, 2)
                Kdr = sbring(ph, "Kdr", [128, D], BF16, 2)
                if fwd:
                    grTr = sbring(ph, "grTr", [128, 16, 128], BF16, 2, dma=True)
                    oblr = sbring(ph, "roblr", [128, 2 * D], BF16, 2, dma=True)
                    orr = sb(ph, "orr", [128, 2 * D], F32)
                    sqr = sb(ph, "sqr", [128, 2 * D], F32)
                    ynr = sb(ph, "ynr", [128, 2 * D], BF16)
                    GrTr = sbring(ph, "GrTr", [128, 16, 128], BF16, 2, dma="st")
                    st_ = {n: sb(ph, "rst_" + n, [128, 4], F32) for n in ("s1", "s2", "mean", "msq", "var", "rstd", "nb")}
                    dbgsem = S.new_dma_sem("st") if dbg else None
                else:
                    ostr = sbring(ph, "rostr", [128, 2 * D], BF16, 2, dma="st")
                pst = Ring([psb(ph, "pst%d" % i, [128, 512], F32) for i in range(2)])
                pov = Ring([psb(ph, "pov%d" % i, [128, 512], F32) for i in range(2)])
                pss = Ring([psb(ph, "pss%d" % i, [128, 512], F32) for i in range(2)])
                ptr2 = [psb(ph, "ptrR%d" % i, [128, 1024], BF16) for i in range(2)]
                for h in range(4):
                    S.op("dve", TS(Mk[:, h * 128:(h + 1) * 128], masks[:, mk, 0:128], rtab[:, 0, 4 * d + h:4 * d + h + 1], None, ALU.mult, None),
                         reads=[masks, rtab], writes=[Mk])
                    for q2 in range(2):
                        S.op("act", ACT(kdf[:, h * 256 + q2 * 128:h * 256 + (q2 + 1) * 128], onesf[:], AF.Identity, scale=rtab[:, 2, 4 * d + h:4 * d + h + 1]),
                             reads=[onesf, rtab], writes=[kdf])
                    S.op("pool", MS(Sst[h][:], 0.0), writes=[Sst[h]])
                    S.op("pool", MS(Sbf[h][:], 0.0), writes=[Sbf[h]])

                def chunk(src, c, with_out):
                    t0 = c * 128
                    kT = kTr.next()
                    load(kT, kT[:], projT[src][RO["k"]:RO["k"] + D, t0:t0 + 128].rearrange("(c p) t -> p c t", p=128))
                    ktm = ktmr.next()
                    load(ktm, ktm[:], k_tm[src][t0:t0 + 128, :])
                    vtm = vtmr.next()
                    load(vtm, vtm[:], v_tm[src][t0:t0 + 128, :])
                    if with_out:
                        qT = qTr.next()
                        load(qT, qT[:], projT[src][RO["q"]:RO["q"] + D, t0:t0 + 128].rearrange("(c p) t -> p c t", p=128))
                        if fwd:
                            grT = grTr.next()
                            for part in range(2):
                                load(grT, grT[:, 8 * part:8 * part + 8, :],
                                     projT[src][RO["gr"] + D * part:RO["gr"] + D * (part + 1), t0:t0 + 128].rearrange("(c p) t -> p c t", p=128))
                            obl = oblr.next()
                            load(obl, obl[:], obwd_r[t0:t0 + 128, :])
                        else:
                            ost = ostr.next()
                        pS = pst.next()
                        for h in range(4):
                            for kc in range(2):
                                S.op("pe", MM(pS[:, h * 128:(h + 1) * 128], kT[:, 2 * h + kc, :], qT[:, 2 * h + kc, :], start=(kc == 0), stop=(kc == 1)),
                                     reads=[kT, qT], writes=[pS], inc=(h == 3 and kc == 1))
                        STm = STr.next()
                        S.op("dve", TT(STm[:], pS[:], Mk[:], ALU.mult), reads=[pS, Mk], writes=[STm])
                        for h in range(4):
                            po_ = pov.next()
                            S.op("pe", MM(po_[:], STm[:, h * 128:(h + 1) * 128], vtm[:, h * 512:(h + 1) * 512], start=True, stop=False),
                                 reads=[STm, vtm], writes=[po_], inc=False)
                            for kc in range(2):
                                S.op("pe", MM(po_[:], qT[:, 2 * h + kc, :], Sbf[h][:, kc, :], start=False, stop=(kc == 1)),
                                     reads=[qT, Sbf[h]], writes=[po_], inc=(kc == 1))
                            qd = rtab[:, 1, 4 * d + h:4 * d + h + 1]
                            if fwd:
                                S.op("dve", STT(orr[:, h * 512:(h + 1) * 512], po_[:], qd, obl[:, h * 512:(h + 1) * 512], ALU.mult, ALU.add),
                                     reads=[po_, rtab, obl], writes=[orr])
                            else:
                                S.op("act", ACT(ost[:, h * 512:(h + 1) * 512], po_[:], AF.Identity, scale=qd), reads=[po_, rtab], writes=[ost])
                        if not fwd:
                            store(ost, obwd_r[t0:t0 + 128, :], ost[:])
                    Kd = Kdr.next()
                    S.op("pool", TT(Kd[:], ktm[:], kdf[:], ALU.mult), reads=[ktm, kdf], writes=[Kd])
                    for h in range(4):
                        for kc in range(2):
                            ps_ = pss.next()
                            S.op("pe", MM(ps_[:], Kd[:, h * 256 + kc * 128:h * 256 + (kc + 1) * 128], vtm[:, h * 512:(h + 1) * 512]),
                                 reads=[Kd, vtm], writes=[ps_])
                            S.op("dve", STT(Sst[h][:, kc, :], Sst[h][:, kc, :], rtab[:, 3, 4 * d + h:4 * d + h + 1], ps_[:], ALU.mult, ALU.add),
                                 reads=[Sst[h], rtab, ps_], writes=[Sst[h]])
                        S.op("act", ACT(Sbf[h][:], Sst[h][:], AF.Copy), reads=[Sst[h]], writes=[Sbf[h]])
                    if fwd and with_out:
                        if dbg:
                            S.dma("pool", dbgsem, dbg_or[t0:t0 + 128, :], orr[:], reads=[orr], writes=[])
                        o3 = orr[:].rearrange("p (h v) -> p h v", h=4)
                        s3 = sqr[:].rearrange("p (h v) -> p h v", h=4)
                        S.op("dve", lambda e: e.tensor_reduce(out=st_["s1"][:], in_=o3, axis=AX.X, op=ALU.add), reads=[orr], writes=[st_["s1"]])
                        S.op("act", ACT(sqr[:], orr[:], AF.Square), reads=[orr], writes=[sqr])
                        S.op("dve", lambda e: e.tensor_reduce(out=st_["s2"][:], in_=s3, axis=AX.X, op=ALU.add), reads=[sqr], writes=[st_["s2"]])
                        S.op("pool", TS(st_["mean"][:], st_["s1"][:], 1.0 / 512, None, ALU.mult, None), reads=[st_["s1"]], writes=[st_["mean"]])
                        S.op("pool", TT(st_["msq"][:], st_["mean"][:], st_["mean"][:], ALU.mult), reads=[st_["mean"]], writes=[st_["msq"]])
                        S.op("dve", STT(st_["var"][:], st_["s2"][:], 1.0 / 512, st_["msq"][:], ALU.mult, ALU.subtract), reads=[st_["s2"], st_["msq"]], writes=[st_["var"]])
                        S.op("act", ACT(st_["rstd"][:], st_["var"][:], AF.Sqrt, bias=1e-5), reads=[st_["var"]], writes=[st_["rstd"]])
                        S.op("dve", lambda e: e.reciprocal(out=st_["rstd"][:], in_=st_["rstd"][:]), reads=[st_["rstd"]], writes=[st_["rstd"]])
                        S.op("dve", STT(st_["nb"][:], st_["mean"][:], -1.0, st_["rstd"][:], ALU.mult, ALU.mult), reads=[st_["mean"], st_["rstd"]], writes=[st_["nb"]])
                        S.op("dve", TT(s3, o3, bc3(st_["rstd"][:], 512), ALU.mult), reads=[orr, st_["rstd"]], writes=[sqr])
                        S.op("dve", TT(ynr[:].rearrange("p (h v) -> p h v", h=4), s3, bc3(st_["nb"][:], 512), ALU.add), reads=[sqr, st_["nb"]], writes=[ynr])
                        for fc in range(16):
                            pt = ptr2[fc // 8]
                            S.op("pe", TR(pt[:, (fc % 8) * 128:(fc % 8 + 1) * 128], ynr[:, fc * 128:(fc + 1) * 128], identb[:]), reads=[ynr, identb], writes=[pt],
                                 inc=(fc % 8 == 7))
                        for fc in range(16):
                            pt = ptr2[fc // 8]
                            S.op("act", ACT(sqr[:, fc * 128:(fc + 1) * 128], pt[:, (fc % 8) * 128:(fc % 8 + 1) * 128], AF.Identity,
                                            scale=pcol("r_ln_w", fc), bias=pcol("r_ln_b", fc)), reads=[pt, pc], writes=[sqr])
                        g3 = sqr[:].rearrange("p (c t) -> p c t", c=16)
                        sg3 = orr[:].rearrange("p (c t) -> p c t", c=16)
                        S.op("act", ACT(sg3, grT[:], AF.Tanh, scale=0.5), reads=[grT], writes=[orr])
                        S.op("pool", TS(orr[:], orr[:], 0.5, 0.5, ALU.mult, ALU.add), reads=[orr], writes=[orr])
                        S.op("pool", TT(sg3, sg3, grT[:], ALU.mult), reads=[orr, grT], writes=[orr])
                        GrT = GrTr.next()
                        S.op("dve", TT(GrT[:], g3, sg3, ALU.mult), reads=[sqr, orr], writes=[GrT])
                        for part in range(2):
                            store(GrT, gall[D * (1 + part):D * (2 + part), t0:t0 + 128].rearrange("(c p) t -> p c t", p=128), GrT[:, 8 * part:8 * part + 8, :])

                order = [("ctx", 0, False), ("ctx", 1, False)] + [("lat", c, True) for c in range(NCH)]
                if not fwd:
                    order = [("ctx", 1, False), ("ctx", 0, False)] + [("lat", c, True) for c in range(NCH - 1, -1, -1)]
                for o_ in order:
                    chunk(*o_)
                S.barrier()
                S.replay()
                S.end_phase()

        def phase6():
            TW = 256
            with ExitStack() as ph:
                S.begin_phase()
                fnw_row = sb(ph, "fnw_row", [128, D], F32, dma=True)
                load(fnw_row, fnw_row[:], fnw_d.partition_broadcast(128))
                awo = sb(ph, "awo", [128, 8, D], BF16)
                rwo = sb(ph, "rwo", [128, 16, D], BF16)
                wo = sb(ph, "wo", [128, 8, D], BF16)
                wf = sbring(ph, "wf6", [128, D], F32, 3, dma=True)
                n = 0
                for (src_d, dst, nk) in ((awo_d, awo, 8), (rwo_d, rwo, 16), (wo_d, wo, 8)):
                    for kc in range(nk):
                        f = wf.next()
                        load(f, f[:], src_d[kc * 128:(kc + 1) * 128, :])
                        e = ("dve", "act", "pool")[n % 3]
                        n += 1
                        if e == "act":
                            S.op("act", ACT(dst[:, kc, :], f[:], AF.Copy), reads=[f], writes=[dst])
                        else:
                            S.op(e, CP(dst[:, kc, :], f[:]), reads=[f], writes=[dst])
                gTr = sbring(ph, "gTr6", [128, 24, TW], BF16, 2, dma=True)
                mTr = sbring(ph, "mTr6", [128, 16, TW], BF16, 2, dma=True)
                xr = sbring(ph, "xr6", [128, D], F32, 3, dma=True)
                sm = sb(ph, "sm6", [128, 16, TW], F32)
                m1r = sbring(ph, "m1r", [128, TW], F32, 2)
                m2r = sbring(ph, "m2r", [128, TW], F32, 2)
                mgr = sbring(ph, "mgr", [128, 8, TW], BF16, 2)
                yor = sbring(ph, "yor", [128, D], F32, 2, dma="st")
                junk = sb(ph, "junk6", [128, D], BF16)
                ssr = sbring(ph, "ss6", [128, 1], F32, 2)
                pya = Ring([psb(ph, "pya%d" % i, [128, 512], F32) for i in range(2)])
                pyr = Ring([psb(ph, "pyr%d" % i, [128, 512], F32) for i in range(2)])
                pout = Ring([psb(ph, "pout%d" % i, [128, 512], F32) for i in range(3)])
                for t0 in range(0, NT, TW):
                    gT = gTr.next()
                    for part in range(3):
                        load(gT, gT[:, 8 * part:8 * part + 8, :], gall[D * part:D * (part + 1), t0:t0 + TW].rearrange("(c p) t -> p c t", p=128))
                    mT = mTr.next()
                    for part in range(2):
                        load(mT, mT[:, 8 * part:8 * part + 8, :],
                             projT["lat"][RO["ma"] + D * part:RO["ma"] + D * (part + 1), t0:t0 + TW].rearrange("(c p) t -> p c t", p=128))
                    S.op("act", ACT(sm[:], mT[:], AF.Tanh, scale=0.5), reads=[mT], writes=[sm])
                    S.op("pool", TS(sm[:], sm[:], 0.5, 0.5, ALU.mult, ALU.add), reads=[sm], writes=[sm])
                    mg = mgr.next()
                    for fo in range(8):
                        pa = pya.next()
                        for fc in range(8):
                            S.op("pe", MM(pa[:, 0:TW], awo[:, fc, fo * 128:(fo + 1) * 128], gT[:, fc, :], start=(fc == 0), stop=(fc == 7)),
                                 reads=[awo, gT], writes=[pa], inc=(fc == 7))
                        pr_ = pyr.next()
                        for fc in range(16):
                            S.op("pe", MM(pr_[:, 0:TW], rwo[:, fc, fo * 128:(fo + 1) * 128], gT[:, 8 + fc, :], start=(fc == 0), stop=(fc == 15)),
                                 reads=[rwo, gT], writes=[pr_], inc=(fc == 15))
                        m1 = m1r.next()
                        m2 = m2r.next()
                        S.op("dve", TT(m1[:], pa[:, 0:TW], sm[:, fo, :], ALU.mult), reads=[pa, sm], writes=[m1])
                        S.op("dve", TT(m2[:], pr_[:, 0:TW], sm[:, 8 + fo, :], ALU.mult), reads=[pr_, sm], writes=[m2])
                        S.op("pool", TT(mg[:, fo, :], m1[:], m2[:], ALU.add), reads=[m1, m2], writes=[mg])
                    for s_ in range(TW // 128):
                        tk = t0 + s_ * 128
                        xb = xr.next()
                        load(xb, xb[:], x_d[tk:tk + 128, :])
                        yo = yor.next()
                        for cb in range(2):
                            po_ = pout.next()
                            for fc in range(8):
                                S.op("pe", MM(po_[:], mg[:, fc, s_ * 128:(s_ + 1) * 128], wo[:, fc, cb * 512:(cb + 1) * 512], start=(fc == 0), stop=(fc == 7)),
                                     reads=[mg, wo], writes=[po_], inc=(fc == 7))
                            S.op("dve", TT(yo[:, cb * 512:(cb + 1) * 512], po_[:], gate_row[:, cb * 512:(cb + 1) * 512], ALU.mult),
                                 reads=[po_, gate_row], writes=[yo])
                        S.op("pool", TT(yo[:], yo[:], xb[:], ALU.add), reads=[yo, xb], writes=[yo])
                        ss = ssr.next()
                        S.op("act", ACT(junk[:], yo[:], AF.Square, accum=ss[:]), reads=[yo], writes=[junk, ss])
                        S.op("act", ACT(ss[:], ss[:], AF.Sqrt, scale=1.0 / D, bias=1e-6), reads=[ss], writes=[ss])
                        S.op("dve", lambda e, ss=ss: e.reciprocal(out=ss[:], in_=ss[:]), reads=[ss], writes=[ss])
                        S.op("dve", STT(yo[:], yo[:], ss[:], fnw_row[:], ALU.mult, ALU.mult), reads=[yo, ss, fnw_row], writes=[yo])
                        store(yo, y_d[tk:tk + 128, :], yo[:])
                S.barrier()
                S.replay()
                S.end_phase()

        import os as _os2
        phase0()
        if stop_after >= 1 and not _os2.environ.get("KDBG_SKIP_P1"):
            phase1()
        if stop_after >= 2:
            rwkv_sweep(1)
        if stop_after >= 3:
            rwkv_sweep(0)
        if stop_after >= 4:
            ret_sweep(1)
        if stop_after >= 5:
            ret_sweep(0)
        if stop_after >= 6:
            phase6()
        if stop_after < 6:
            with ExitStack() as ph:
                S.begin_phase()
                z = sb(ph, "zz", [128, D], F32, dma="st")
                S.op("pool", MS(z[:], 0.0), writes=[z])
                for i in range(NT // 128):
                    store(z, y_d[i * 128:(i + 1) * 128, :], z[:])
                S.barrier()
                S.replay()
                S.end_phase()
    return nc


def _cols(v):
    v = np.asarray(v, np.float32).reshape(-1)
    return np.ascontiguousarray(v.reshape(-1, 128).T)


def host_consts(NT):
    idx = np.arange(128)
    p = idx[:, None]
    f = idx[None, :]
    base = [(f > p), (f >= p), (f < p), (f <= p)]
    m = np.zeros((128, 8, 512), np.float32)
    for i, b in enumerate(base):
        m[:, i, :] = np.tile(b.astype(np.float32), (1, 4))
        m[:, 4 + i, :] = -m[:, i, :]
    bones = np.zeros((128, 128), np.float32)
    bones[:64, :64] = 1
    bones[64:, 64:] = 1
    j = idx.astype(np.float32)
    iota = np.stack([j + 1, -(j + 1), 127 - j, 128 - j, -(128 - j), j, 0 * j, 0 * j], 1).astype(np.float32)
    t = np.arange(NT)
    rows = (t // 64).astype(np.float64)
    cols = (t % 64).astype(np.float64)
    fr = 10000.0 ** (-np.arange(64, dtype=np.float64) / 64)
    cos = np.zeros((NT, 2, 2, 64), np.float32)
    sin = np.zeros((NT, 2, 2, 64), np.float32)
    for ty, pos in enumerate((rows, cols)):
        ang = (pos.astype(np.float32)[:, None] * fr.astype(np.float32)[None, :]).astype(np.float32)
        cos[:, ty, 0] = np.cos(ang)
        cos[:, ty, 1] = np.cos(ang)
        sin[:, ty, 0] = -np.sin(ang)
        sin[:, ty, 1] = np.sin(ang)
    lm = np.zeros((2, 128, 7, 512), np.float32)
    for j in range(7):
        bsz = 2 ** j
        q = ((p // (2 * bsz) == f // (2 * bsz)) & (p % (2 * bsz) >= bsz) & (f % (2 * bsz) < bsz)).astype(np.float32)
        lm[0, :, j, :] = np.tile(q, (1, 4))
        lm[1, :, j, :] = np.tile(q.T, (1, 4))
    return dict(masks=m.astype(ml_dtypes.bfloat16), identb=np.eye(128).astype(ml_dtypes.bfloat16),
                lmasks=lm.astype(ml_dtypes.bfloat16), ident4=np.tile(np.eye(128), (1, 4)).astype(ml_dtypes.bfloat16),
                blockones=bones.astype(ml_dtypes.bfloat16), iota=iota,
                rope_cos=cos.reshape(NT, 256), rope_sin=sin.reshape(NT, 256))


def make_in_maps(inputs, NT):
    f32 = lambda a: np.ascontiguousarray(np.asarray(a, np.float32))
    x = f32(inputs["x"])[:, :NT]
    B = x.shape[0]
    hc = host_consts(NT)
    shared = dict(
        ada_w=f32(inputs["ada_w"][0]), ada_b_gate=f32(inputs["ada_b"][0][2048:3072]),
        w_in=f32(inputs["w_in"][0]), a_w_up=f32(inputs["a_w_up"][0]).reshape(128, D),
        a_a_up=f32(inputs["a_a_up"][0]).reshape(128, D), a_w_out=f32(inputs["a_w_out"][0]),
        r_w_out=f32(inputs["r_w_out"][0]), w_out=f32(inputs["w_out"][0]),
        r_decay=f32(inputs["r_decay"][0]).reshape(8), final_norm_w=f32(inputs["final_norm_w"]), **hc)
    conv = f32(inputs["a_conv"][0])
    pcs = [_cols(inputs["norm_w"][0]), _cols(inputs["ada_b"][0])]
    pcs += [_cols(conv[j]) for j in range(3)]
    pcs += [_cols(inputs["a_w0"][0][d]) for d in range(2)]
    pcs += [_cols(inputs["a_a0"][0][d]) for d in range(2)]
    pcs += [_cols(inputs["a_k_k"][0]), _cols(inputs["a_k_a"][0]), _cols(inputs["a_r_k"][0]),
            _cols(inputs["a_ln_w"][0]), _cols(inputs["a_ln_b"][0]), _cols(inputs["r_ln_w"][0]), _cols(inputs["r_ln_b"][0])]
    maps = []
    for b in range(B):
        pcb = np.concatenate(pcs + [_cols(inputs["c"][b]), _cols(inputs["c_ctx"])], axis=1)
        assert pcb.shape == (128, NPC), pcb.shape
        m = dict(shared)
        m.update(x=np.ascontiguousarray(x[b]), ctx=f32(inputs["ctx"][b]), pcols=np.ascontiguousarray(pcb))
        maps.append(m)
    return maps


_NC_CACHE = {}


def kernel(**inputs):
    NT = inputs["x"].shape[1]
    if NT not in _NC_CACHE:
        _NC_CACHE[NT] = build(NT)
    nc = _NC_CACHE[NT]
    maps = make_in_maps(inputs, NT)
    res = run_bass_kernel_spmd(nc, maps, core_ids=list(range(len(maps))))
    return np.stack([np.asarray(r["y"], np.float32) for r in res.results], axis=0)
```

```python
from contextlib import ExitStack
import math
import numpy as np
import ml_dtypes
import concourse.bass as bass
import concourse.mybir as mybir
from concourse.bass_utils import run_bass_kernel_spmd

F32 = mybir.dt.float32
BF16 = mybir.dt.bfloat16
ALU = mybir.AluOpType
AF = mybir.ActivationFunctionType
AX = mybir.AxisListType

D = 1024
NCTX = 256
C = 128
N_IN = 12544
KAPPA = math.exp(-0.5)
CO = dict(rkv=0, ga=3072, lw=4096, la=4224, q=4352, k=5376, v=6400, gr=8448, ma=10496, mb=11520)
RO = dict(rkv=0, ga=3072, lora=4096, q=4352, k=5376, gr=6400, ma=8448, mb=9472)
NROW = 10496
PCO = {}
_o = 0
for _n, _w in (("norm_w", 8), ("ada_b", 24), ("conv", 72), ("w0", 16), ("a0", 16), ("k_k", 8), ("k_a", 8),
               ("r_k", 8), ("a_ln_w", 8), ("a_ln_b", 8), ("r_ln_w", 16), ("r_ln_b", 16), ("c", 8), ("c_ctx", 8)):
    PCO[_n] = _o
    _o += _w
NPC = _o


class Buf:
    __slots__ = ("name", "w", "r", "t", "sem")

    def __init__(self, name, t=None, sem=None):
        self.name = name
        self.w = None
        self.r = {}
        self.t = t
        self.sem = sem

    def __getitem__(self, idx):
        return self.t[idx]


class Ring:
    def __init__(self, bufs):
        self.b = bufs
        self.i = 0

    def next(self):
        b = self.b[self.i % len(self.b)]
        self.i += 1
        return b


class Sched:
    CE = ("pe", "act", "dve", "pool")
    QE = ("pe", "act", "dve", "pool", "sp")

    def __init__(self, nc, stack):
        self.nc = nc
        self.q = {e: [] for e in self.QE}
        self.tick = {e: 0 for e in self.CE}
        self.known = {e: {} for e in self.QE}
        self.semh = {}
        for e in self.CE:
            self.semh[e] = stack.enter_context(nc.semaphore("s_" + e))
        self.dma_cnt = {}
        self.n_dma = 0
        self.stack = stack
        self.n_instr = 0
        self.free_sems = {"ld": [], "st": []}
        self.phase_sems = []

    def begin_phase(self):
        self.phase_sems = []

    def end_phase(self):
        for kind, k in self.phase_sems:
            self.free_sems[kind].append(k)
        self.phase_sems = []

    def new_dma_sem(self, kind="ld"):
        if self.free_sems[kind]:
            k = self.free_sems[kind].pop()
            self.phase_sems.append((kind, k))
            return k
        k = "d%d" % self.n_dma
        self.phase_sems.append((kind, k))
        self.n_dma += 1
        self.semh[k] = self.stack.enter_context(self.nc.semaphore("s_" + k))
        self.dma_cnt[k] = 0
        return k

    def _deps(self, reads, writes):
        d = {}
        for b in reads:
            if b.w is not None and d.get(b.w[0], -1) < b.w[1]:
                d[b.w[0]] = b.w[1]
        for b in writes:
            if b.w is not None and d.get(b.w[0], -1) < b.w[1]:
                d[b.w[0]] = b.w[1]
            for k, v in b.r.items():
                if d.get(k, -1) < v:
                    d[k] = v
        return d

    def _waits(self, E, d):
        kn = self.known[E]
        for k, v in d.items():
            if k == E:
                if E == "pe":
                    continue
            if kn.get(k, -1) >= v:
                continue
            kn[k] = v
            sem = self.semh[k]
            self.q[E].append(lambda eng, sem=sem, v=v: eng.wait_ge(sem, v))
            self.n_instr += 1

    def op(self, E, fn, reads=(), writes=(), inc=True):
        self._waits(E, self._deps(reads, writes))
        v = self.tick[E] + 1
        if inc:
            self.tick[E] = v
            sem = self.semh[E]
            self.q[E].append(lambda eng, fn=fn, sem=sem: fn(eng).then_inc(sem, 1))
        else:
            self.q[E].append(lambda eng, fn=fn: fn(eng))
        self.n_instr += 1
        for b in reads:
            b.r[E] = v
        for b in writes:
            b.w = (E, v)
            b.r = {}

    def dma(self, Q, semk, out_ap, in_ap, reads=(), writes=()):
        self._waits(Q, self._deps(reads, writes))
        self.dma_cnt[semk] += 16
        v = self.dma_cnt[semk]
        sem = self.semh[semk]
        self.q[Q].append(lambda eng, o=out_ap, i=in_ap, sem=sem:
                         eng.dma_start(out=o, in_=i).then_inc(sem, 16))
        self.n_instr += 1
        for b in reads:
            b.r[semk] = v
        for b in writes:
            b.w = (semk, v)
            b.r = {}

    def barrier(self):
        for E in self.QE:
            d = {e: self.tick[e] for e in self.CE if self.tick[e] > 0}
            for k, v in self.dma_cnt.items():
                if v > 0:
                    d[k] = v
            kn = self.known[E]
            for k, v in d.items():
                if k == E:
                    continue
                if kn.get(k, -1) >= v:
                    continue
                kn[k] = v
                sem = self.semh[k]
                self.q[E].append(lambda eng, sem=sem, v=v: eng.wait_ge(sem, v))

    def replay(self):
        nc = self.nc
        with nc.Block() as block:
            @block.tensor
            def _(e):
                for f in self.q["pe"]:
                    f(e)

            @block.scalar
            def _(e):
                for f in self.q["act"]:
                    f(e)

            @block.vector
            def _(e):
                for f in self.q["dve"]:
                    f(e)

            @block.gpsimd
            def _(e):
                for f in self.q["pool"]:
                    f(e)

            @block.sync
            def _(e):
                for f in self.q["sp"]:
                    f(e)
        self.q = {e: [] for e in self.QE}


def build(NT, stop_after=99, dbg=False):
    import os as _os0
    LIGHT = bool(_os0.environ.get("KDBG_LIGHT"))
    nc = bass.Bass("TRN2", target_bir_lowering=False)
    NCH = NT // C
    scratch_kind = "ExternalOutput" if dbg else "Internal"

    def din(name, shape, dt=F32):
        return nc.dram_tensor(name, list(shape), dt, kind="ExternalInput").ap()

    def dscr(name, shape, dt=BF16):
        return nc.dram_tensor(name, list(shape), dt, kind=scratch_kind).ap()

    x_d = din("x", [NT, D])
    ctx_d = din("ctx", [NCTX, D])
    pc_d = din("pcols", [128, NPC])
    adaw_d = din("ada_w", [D, 3 * D] if not LIGHT else [8, 8])
    adabg_d = din("ada_b_gate", [D])
    w_in_d = din("w_in", [D, N_IN] if not LIGHT else [8, 8])
    wup_d = din("a_w_up", [128, D])
    aup_d = din("a_a_up", [128, D])
    awo_d = din("a_w_out", [D, D] if not LIGHT else [8, 8])
    rwo_d = din("r_w_out", [2 * D, D] if not LIGHT else [8, 8])
    wo_d = din("w_out", [D, D] if not LIGHT else [8, 8])
    rdec_d = din("r_decay", [8])
    fnw_d = din("final_norm_w", [D])
    masks_d = din("masks", [128, 8, 512], BF16)
    identb_d = din("identb", [128, 128], BF16)
    ident4_d = din("ident4", [128, 512], BF16)
    lmasks_d = din("lmasks", [2, 128, 7, 512], BF16)
    bones_d = din("blockones", [128, 128], BF16)
    iota_d = din("iota", [128, 8])
    cos_d = din("rope_cos", [NT, 256])
    sin_d = din("rope_sin", [NT, 256])
    y_d = nc.dram_tensor("y", [NT, D], F32, kind="ExternalOutput").ap()

    winb_d = dscr("winb", [D, N_IN])
    projT = {"lat": dscr("projT", [NROW, NT]), "ctx": dscr("projTc", [NROW, NCTX])}
    k_tm = {"lat": dscr("k_tm", [NT, D]), "ctx": dscr("k_tmc", [NCTX, D])}
    v_tm = {"lat": dscr("v_tm", [NT, 2 * D]), "ctx": dscr("v_tmc", [NCTX, 2 * D])}
    obwd_a = dscr("obwd_a", [NT, D])
    bonb_d = dscr("bonb", [D, NT])
    obwd_r = dscr("obwd_r", [NT, 2 * D])
    gall = dscr("gall", [3 * D, NT])
    dbg_oa = dscr("dbg_oa", [NT, D], F32) if dbg else None
    dbg_or = dscr("dbg_or", [NT, 2 * D], F32) if dbg else None

    with ExitStack() as st:
        S = Sched(nc, st)

        uid = [0]

        def sb(stack, name, shape, dt, dma=False):
            uid[0] += 1
            t = stack.enter_context(nc.sbuf_tensor("sb%d_%s" % (uid[0], name), list(shape), dt))
            return Buf(name, t, S.new_dma_sem("st" if dma == "st" else "ld") if dma else None)

        def sbring(stack, name, shape, dt, n, dma=False):
            return Ring([sb(stack, "%s%d" % (name, i), shape, dt, dma) for i in range(n)])

        def psb(stack, name, shape, dt):
            uid[0] += 1
            return Buf(name, stack.enter_context(nc.psum_tensor("ps%d_%s" % (uid[0], name), list(shape), dt)))

        def load(buf, out_ap, in_ap, extra_reads=()):
            S.dma("sp", buf.sem, out_ap, in_ap, reads=list(extra_reads), writes=[buf])

        def store(buf, out_ap, in_ap):
            S.dma("pool", buf.sem, out_ap, in_ap, reads=[buf], writes=[])

        def MM(out, lhsT, rhs, start=True, stop=True, skip=False):
            if skip:
                return lambda e: e.matmul(out, lhsT=lhsT, rhs=rhs, start=start, stop=stop, skip_group_check=True)
            return lambda e: e.matmul(out, lhsT=lhsT, rhs=rhs, start=start, stop=stop)

        def TR(out, in_, ident):
            return lambda e: e.transpose(out, in_, ident)

        def ACT(out, in_, func, scale=1.0, bias=0.0, accum=None):
            if accum is None:
                return lambda e: e.activation(out=out, in_=in_, func=func, bias=bias, scale=scale)
            return lambda e: e.activation(out=out, in_=in_, func=func, bias=bias, scale=scale, accum_out=accum)

        def TT(out, a, b, op):
            return lambda e: e.tensor_tensor(out=out, in0=a, in1=b, op=op)

        def TS(out, a, s1, s2, op0, op1):
            if s2 is None:
                return lambda e: e.tensor_scalar(out=out, in0=a, scalar1=s1, scalar2=None, op0=op0)
            return lambda e: e.tensor_scalar(out=out, in0=a, scalar1=s1, scalar2=s2, op0=op0, op1=op1)

        def STT(out, a, s, b, op0, op1):
            return lambda e: e.scalar_tensor_tensor(out=out, in0=a, scalar=s, in1=b, op0=op0, op1=op1)

        def CP(out, in_):
            return lambda e: e.tensor_copy(out=out, in_=in_)

        def MS(out, val):
            return lambda e: e.memset(out, val)

        def bc3(ap2, n):
            return ap2.unsqueeze(2).to_broadcast([128, ap2.shape[1], n])

        pc = sb(st, "pc", [128, NPC], F32, dma=True)
        dcol = sb(st, "dcol", [128, 48], F32)
        modc = sb(st, "modc", [128, 24, 2], F32)
        masks = sb(st, "masks", [128, 8, 512], BF16, dma=True)
        identb = sb(st, "identb", [128, 128], BF16, dma=True)
        ident4 = sb(st, "ident4", [128, 512], BF16, dma=True)
        bones = sb(st, "bones", [128, 128], BF16, dma=True)
        iota = sb(st, "iota", [128, 8], F32, dma=True)
        rtab = sb(st, "rtab", [128, 8, 8], F32)
        wupb = sb(st, "wupb", [128, D], BF16)
        aupb = sb(st, "aupb", [128, D], BF16)
        gate_row = sb(st, "gate_row", [128, D], F32)
        onesf = sb(st, "onesf", [128, 128], F32)
        S.op("pool", MS(onesf[:], 1.0), writes=[onesf])
        U_S, U_I, L_S, L_I, NU_S, NU_I, NL_S, NL_I = range(8)

        def pcol(name, i):
            o = PCO[name] + i
            return pc[:, o:o + 1]

        def phase0():
            with ExitStack() as ph:
                S.begin_phase()
                load(pc, pc[:], pc_d)
                load(masks, masks[:], masks_d)
                load(identb, identb[:], identb_d)
                load(ident4, ident4[:], ident4_d)
                load(bones, bones[:], bones_d)
                load(iota, iota[:], iota_d)
                adabg = sb(ph, "adabg", [128, D], F32, dma=True)
                load(adabg, adabg[:], adabg_d.partition_broadcast(128))
                rdec = sb(ph, "rdec", [128, 8], F32, dma=True)
                load(rdec, rdec[:], rdec_d.partition_broadcast(128))
                adaw = sb(ph, "adaw", [128, 8, 3 * D], F32, dma=True)
                for kc in range(8):
                    if LIGHT:
                        S.op("pool", MS(adaw[:, kc, :], 0.0), writes=[adaw])
                        continue
                    load(adaw, adaw[:, kc, :], adaw_d[kc * 128:(kc + 1) * 128, :])
                lt = sbring(ph, "lt", [128, D], F32, 2, dma=True)
                for src, dst, eng in ((wup_d, wupb, "dve"), (aup_d, aupb, "pool")):
                    t = lt.next()
                    load(t, t[:], src)
                    S.op(eng, CP(dst[:], t[:]), reads=[t], writes=[dst])
                sc = sb(ph, "sc", [128, 8, 2], F32)
                th = sb(ph, "th0", [128, 16], F32)
                cc = pc[:, PCO["c"]:PCO["c"] + 16]
                S.op("act", ACT(th[:], cc, AF.Tanh, scale=0.5), reads=[pc], writes=[th])
                S.op("dve", TS(th[:], th[:], 0.5, 0.5, ALU.mult, ALU.add), reads=[th], writes=[th])
                S.op("dve", TT(sc[:].rearrange("p k t -> p t k"), th[:].rearrange("p (t k) -> p t k", t=2),
                               cc.rearrange("p (t k) -> p t k", t=2), ALU.mult), reads=[th, pc], writes=[sc])
                pmod = psb(ph, "pmod", [128, 24, 2], F32)
                for fb in range(24):
                    for kc in range(8):
                        S.op("pe", MM(pmod[:, fb, :], adaw[:, kc, fb * 128:(fb + 1) * 128], sc[:, kc, :],
                                      start=(kc == 0), stop=(kc == 7)), reads=[adaw, sc], writes=[pmod], inc=(kc == 7))
                S.op("dve", TT(modc[:], pmod[:], bc3(pc[:, PCO["ada_b"]:PCO["ada_b"] + 24], 2), ALU.add),
                     reads=[pmod, pc], writes=[modc])
                for t in range(2):
                    S.op("dve", TS(dcol[:, 24 + 8 * t:32 + 8 * t], modc[:, 8:16, t], 1.0, None, ALU.add, None),
                         reads=[modc], writes=[dcol])
                    S.op("dve", TT(dcol[:, 8 * t:8 * t + 8], dcol[:, 24 + 8 * t:32 + 8 * t],
                                   pc[:, PCO["norm_w"]:PCO["norm_w"] + 8], ALU.mult), reads=[dcol, pc], writes=[dcol])
                S.op("dve", TS(dcol[:, 16:24], pc[:, PCO["k_a"]:PCO["k_a"] + 8], -1.0, 1.0, ALU.mult, ALU.add),
                     reads=[pc], writes=[dcol])
                scb = sb(ph, "scb", [128, 8, 128], F32)
                for kc in range(8):
                    S.op("act", ACT(scb[:, kc, :], onesf[:], AF.Identity, scale=sc[:, kc, 0:1]),
                         reads=[onesf, sc], writes=[scb])
                pg = [psb(ph, "pg%d" % i, [128, 512], F32) for i in range(2)]
                for cb in range(2):
                    for kc in range(8):
                        S.op("pe", MM(pg[cb][:], scb[:, kc, :], adaw[:, kc, 2048 + cb * 512:2048 + (cb + 1) * 512],
                                      start=(kc == 0), stop=(kc == 7)), reads=[scb, adaw], writes=[pg[cb]], inc=(kc == 7))
                    S.op("dve", TT(gate_row[:, cb * 512:(cb + 1) * 512], pg[cb][:], adabg[:, cb * 512:(cb + 1) * 512], ALU.add),
                         reads=[pg[cb], adabg], writes=[gate_row])
                lgt = sb(ph, "lgt", [128, 8], F32)
                S.op("act", ACT(lgt[:], rdec[:], AF.Exp), reads=[rdec], writes=[lgt])
                S.op("dve", TS(lgt[:], lgt[:], -1.0, None, ALU.mult, None), reads=[lgt], writes=[lgt])
                for i, (cf, cbk) in enumerate(((1, 4), (0, 3), (2, 5))):
                    S.op("act", ACT(rtab[:, i, 0:4], lgt[:, 0:4], AF.Exp, scale=iota[:, cf:cf + 1]), reads=[lgt, iota], writes=[rtab])
                    S.op("act", ACT(rtab[:, i, 4:8], lgt[:, 4:8], AF.Exp, scale=iota[:, cbk:cbk + 1]), reads=[lgt, iota], writes=[rtab])
                S.op("act", ACT(rtab[:, 3, :], lgt[:], AF.Exp, scale=128.0), reads=[lgt], writes=[rtab])
                S.op("dve", TS(rtab[:, 0, :], rtab[:, 0, :], 1.0 / 16.0, None, ALU.mult, None), reads=[rtab], writes=[rtab])
                S.op("dve", TS(rtab[:, 2, :], rtab[:, 2, :], 1.0 / 16.0, None, ALU.mult, None), reads=[rtab], writes=[rtab])
                wf = sbring(ph, "wf", [128, 3136], F32, 2, dma=True)
                wb = sbring(ph, "wb", [128, 3136], BF16, 2, dma="st")
                engs = ("dve", "act", "pool", "dve")
                n = 0
                for kc in range(8 if not LIGHT else 0):
                    for cb in range(4):
                        f = wf.next()
                        b = wb.next()
                        load(f, f[:], w_in_d[kc * 128:(kc + 1) * 128, cb * 3136:(cb + 1) * 3136])
                        e = engs[n % 4]
                        n += 1
                        if e == "act":
                            S.op("act", ACT(b[:], f[:], AF.Copy), reads=[f], writes=[b])
                        else:
                            S.op(e, CP(b[:], f[:]), reads=[f], writes=[b])
                        store(b, winb_d[kc * 128:(kc + 1) * 128, cb * 3136:(cb + 1) * 3136], b[:])
                S.barrier()
                S.replay()
                S.end_phase()

        def phase1():
            with ExitStack() as ph:
                S.begin_phase()
                xs = sbring(ph, "xs", [128, D], F32, 4, dma=True)
                xn = sbring(ph, "xn", [128, D], BF16, 4)
                junk = sb(ph, "junk", [128, D], BF16)
                ss = sbring(ph, "ss", [128, 4], F32, 2)
                rstd = sbring(ph, "rstd", [128, 4], F32, 2)
                uT = sbring(ph, "uT", [128, 8, 512], BF16, 2)
                wp = sbring(ph, "wp", [128, 8, 512], BF16, 3, dma=True)
                stg = sbring(ph, "stg", [128, 512], BF16, 4, dma="st")
                cosr = sbring(ph, "cosr", [128, 512], F32, 4, dma=True)
                sinr = sbring(ph, "sinr", [128, 512], F32, 4, dma=True)
                t1r = sbring(ph, "t1r", [128, 512], F32, 2)
                t2r = sbring(ph, "t2r", [128, 512], F32, 2)
                qr = sbring(ph, "qr", [128, D], BF16, 4)
                kr = sbring(ph, "kr", [128, D], BF16, 4, dma="st")
                vr = sbring(ph, "vr", [128, 2 * D], BF16, 4, dma="st")
                qTs = sbring(ph, "qTs", [128, 8, 512], BF16, 1, dma="st")
                kTs = sbring(ph, "kTs", [128, 8, 512], BF16, 1, dma="st")
                pmm = Ring([psb(ph, "pmm%d" % i, [128, 512], F32) for i in range(4)])
                ptr = Ring([psb(ph, "ptr%d" % i, [128, 1024], BF16) for i in range(3)])
                winb_v = winb_d.rearrange("(kc p) n -> p kc n", p=128)
                evac_n = [0]

                def evac(out_ap, in_ap, reads, writes):
                    evac_n[0] += 1
                    if evac_n[0] % 2:
                        S.op("act", ACT(out_ap, in_ap, AF.Copy), reads=reads, writes=writes)
                    else:
                        S.op("dve", CP(out_ap, in_ap), reads=reads, writes=writes)

                tiles = [("ctx", 0, NCTX)] + [("lat", t0, 512) for t0 in range(0, NT, 512)]
                for (src, tok0, tw) in tiles:
                    nsub = tw // 128
                    xsrc = ctx_d if src == "ctx" else x_d
                    t = 1 if src == "ctx" else 0
                    s_ = ss.next()
                    r_ = rstd.next()
                    xt = []
                    for s in range(nsub):
                        xb = xs.next()
                        load(xb, xb[:], xsrc[tok0 + s * 128:tok0 + (s + 1) * 128, :])
                        S.op("act", ACT(junk[:], xb[:], AF.Square, accum=s_[:, s:s + 1]), reads=[xb], writes=[junk, s_])
                        xt.append(xb)
                    S.op("act", ACT(r_[:, 0:nsub], s_[:, 0:nsub], AF.Sqrt, scale=1.0 / D, bias=1e-6), reads=[s_], writes=[r_])
                    S.op("dve", lambda e, r_=r_, nsub=nsub: e.reciprocal(out=r_[:, 0:nsub], in_=r_[:, 0:nsub]), reads=[r_], writes=[r_])
                    xns = []
                    for s in range(nsub):
                        xnb = xn.next()
                        S.op("pool" if s % 2 else "dve", TS(xnb[:], xt[s][:], r_[:, s:s + 1], None, ALU.mult, None),
                             reads=[xt[s], r_], writes=[xnb])
                        xns.append(xnb)
                    u = uT.next()
                    for jp in range(4):
                        pt = ptr.next()
                        for jj in range(2):
                            j = jp * 2 + jj
                            for s in range(nsub):
                                S.op("pe", TR(pt[:, jj * 512 + s * 128:jj * 512 + (s + 1) * 128], xns[s][:, j * 128:(j + 1) * 128], identb[:]),
                                     reads=[xns[s], identb], writes=[pt], inc=(s == nsub - 1))
                            S.op("act", ACT(u[:, j, 0:tw], pt[:, jj * 512:jj * 512 + tw], AF.Identity,
                                            scale=dcol[:, 8 * t + j:8 * t + j + 1], bias=modc[:, j, t:t + 1]),
                                 reads=[pt, dcol, modc], writes=[u])
                    pieces = []
                    for i in range(6):
                        pieces.append(("fm", CO["rkv"] + i * 512, 512, RO["rkv"] + i * 512))
                    pieces.append(("fm", CO["lw"], 256, RO["lora"]))
                    pieces += [("k", CO["k"] + i * 512, 512, i) for i in range(2)]
                    pieces += [("v", CO["v"] + i * 512, 512, i) for i in range(4)]
                    if src == "lat":
                        pieces += [("q", CO["q"] + i * 512, 512, i) for i in range(2)]
                        pieces += [("fm", CO["ga"] + i * 512, 512, RO["ga"] + i * 512) for i in range(2)]
                        pieces += [("fm", CO["gr"] + i * 512, 512, RO["gr"] + i * 512) for i in range(4)]
                        pieces += [("fm", CO["ma"] + i * 512, 512, RO["ma"] + i * 512) for i in range(2)]
                        pieces += [("fm", CO["mb"] + i * 512, 512, RO["mb"] + i * 512) for i in range(2)]
                    tm_pieces = [p for p in pieces if p[0] != "fm"]
                    fm_pieces = [p for p in pieces if p[0] == "fm"]
                    for (kind, c0, ncol, r0) in fm_pieces:
                        w = wp.next()
                        load(w, w[:, :, 0:ncol], winb_v[:, :, c0:c0 + ncol])
                        for bi in range(ncol // 128):
                            pm = pmm.next()
                            for kc in range(8):
                                S.op("pe", MM(pm[:, 0:tw], w[:, kc, bi * 128:(bi + 1) * 128], u[:, kc, 0:tw],
                                              start=(kc == 0), stop=(kc == 7)), reads=[w, u], writes=[pm], inc=(kc == 7))
                            sg = stg.next()
                            evac(sg[:, 0:tw], pm[:, 0:tw], [pm], [sg])
                            store(sg, projT[src][r0 + bi * 128:r0 + (bi + 1) * 128, tok0:tok0 + tw], sg[:, 0:tw])
                    qT = qTs.next()
                    kT = kTs.next()
                    subs = []
                    for s in range(nsub):
                        tk0 = tok0 + s * 128
                        cs = cosr.next()
                        sn = sinr.next()
                        if src == "lat":
                            for hh in range(2):
                                load(cs, cs[:, hh * 256:(hh + 1) * 256], cos_d[tk0:tk0 + 128, :])
                                load(sn, sn[:, hh * 256:(hh + 1) * 256], sin_d[tk0:tk0 + 128, :])
                        subs.append((cs, sn, qr.next() if src == "lat" else None, kr.next(), vr.next()))
                    for (kind, c0, ncol, idx) in tm_pieces:
                        w = wp.next()
                        load(w, w[:], winb_v[:, :, c0:c0 + 512])
                        for s in range(nsub):
                            cs, sn, qb, kb, vb = subs[s]
                            pm = pmm.next()
                            for kc in range(8):
                                S.op("pe", MM(pm[:], u[:, kc, s * 128:(s + 1) * 128], w[:, kc, :],
                                              start=(kc == 0), stop=(kc == 7)), reads=[w, u], writes=[pm], inc=(kc == 7))
                            if kind == "v":
                                evac(vb[:, idx * 512:(idx + 1) * 512], pm[:], [pm], [vb])
                            elif src == "ctx":
                                evac(kb[:, idx * 512:(idx + 1) * 512], pm[:], [pm], [kb])
                            else:
                                dst = qb if kind == "q" else kb
                                t1 = t1r.next()
                                t2 = t2r.next()
                                S.op("dve", TT(t1[:], pm[:], cs[:], ALU.mult), reads=[pm, cs], writes=[t1])
                                p4 = pm[:].rearrange("p (g h i) -> p g h i", g=4, h=2)
                                s4 = sn[:].rearrange("p (g h i) -> p g h i", g=4, h=2)
                                t4 = t2[:].rearrange("p (g h i) -> p g h i", g=4, h=2)
                                S.op("dve", TT(t4[:, :, 0, :], p4[:, :, 1, :], s4[:, :, 0, :], ALU.mult), reads=[pm, sn], writes=[t2])
                                S.op("dve", TT(t4[:, :, 1, :], p4[:, :, 0, :], s4[:, :, 1, :], ALU.mult), reads=[pm, sn], writes=[t2])
                                S.op("pool", TT(dst[:, idx * 512:(idx + 1) * 512], t1[:], t2[:], ALU.add), reads=[t1, t2], writes=[dst])
                    for s in range(nsub):
                        tk0 = tok0 + s * 128
                        cs, sn, qb, kb, vb = subs[s]
                        store(vb, v_tm[src][tk0:tk0 + 128, :], vb[:])
                        store(kb, k_tm[src][tk0:tk0 + 128, :], kb[:])
                        for (srcb, dstT) in ((qb, qT), (kb, kT)):
                            if srcb is None:
                                continue
                            pt = ptr.next()
                            for j in range(8):
                                S.op("pe", TR(pt[:, j * 128:(j + 1) * 128], srcb[:, j * 128:(j + 1) * 128], identb[:]),
                                     reads=[srcb, identb], writes=[pt], inc=(j == 7))
                            evac(dstT[:, :, s * 128:(s + 1) * 128], pt[:].rearrange("p (j t) -> p j t", j=8), [pt], [dstT])
                    if src == "lat":
                        store(qT, projT[src][RO["q"]:RO["q"] + D, tok0:tok0 + tw].rearrange("(j p) t -> p j t", p=128), qT[:, :, 0:tw])
                    store(kT, projT[src][RO["k"]:RO["k"] + D, tok0:tok0 + tw].rearrange("(j p) t -> p j t", p=128), kT[:, :, 0:tw])
                S.barrier()
                S.replay()
                S.end_phase()

        def rwkv_sweep(d, interleave=True):
            fwd = (d == 0)
            pb = 64 * d
            if fwd:
                mQ, mP, mAk, mRb, mRk = NU_S, NL_S, U_S, NU_I, U_I
            else:
                mQ, mP, mAk, mRb, mRk = NL_S, NU_S, L_S, NL_I, L_I
            with ExitStack() as ph:
                S.begin_phase()
                cdiag = sb(ph, "cdiag", [128, 72, 128], BF16)
                rkvraw = sbring(ph, "rkvraw", [128, 24, 130], BF16, 1, dma=True)
                lora = sbring(ph, "lora", [128, 2, 128], BF16, 2, dma=True)
                thr = sbring(ph, "thr", [128, 128], BF16, 2)
                if fwd:
                    gaTr = sbring(ph, "gaTr", [128, 8, 128], BF16, 2, dma=True)
                    oblr = sbring(ph, "oblr", [128, D], BF16, 2, dma=True)
                    bblr = sbring(ph, "bblr", [128, 8, 128], BF16, 2, dma=True)
                    bonfr = sbring(ph, "bonfr", [128, 8, 128], BF16, 2)
                    oa = sb(ph, "oa", [128, D], F32)
                    sqt = sb(ph, "sqt", [128, D], F32)
                    ynb = sb(ph, "ynb", [128, D], BF16)
                    GTr = sbring(ph, "GTr", [128, 8, 128], BF16, 1, dma="st")
                    stt_ = {n: sb(ph, "st_" + n, [128, 16], F32) for n in ("s1", "s2", "mean", "msq", "var", "rstd", "nb")}
                else:
                    ostr = sbring(ph, "ostr", [128, D], BF16, 1, dma="st")
                    bstr = sbring(ph, "bstr", [128, 8, 128], BF16, 1, dma="st")
                T = {n: sb(ph, "t_" + n, [128, 512], F32) for n in
                     ("rS", "kS", "kk", "rs", "sg", "a", "prefix", "Dm", "PT", "b", "kd", "PM", "DMm") + (() if fwd else ("DT",))}
                Er = sbring(ph, "Er", [128, 512], F32, 2)
                vSr = sbring(ph, "vSr", [128, 512], BF16, 2)
                sq = sb(ph, "sq", [128, 512], BF16)
                pr = sb(ph, "pr", [128, 512], BF16)
                WCr = sbring(ph, "WCr", [128, 8], F32, 2)
                outs = {n: [sbring(ph, "%s%d_" % (n, hf), [128, 512], BF16, 2) for hf in range(2)]
                        for n in ("ATe", "ATo", "ATse", "ATso", "BT", "KT", "RTe", "RTo", "RTse", "RTso", "KH", "BH")}
                for n in ("ATe", "ATo", "ATse", "ATso", "RTe", "RTo", "RTse", "RTso"):
                    for hf in range(2):
                        for b_ in outs[n][hf].b:
                            S.op("pool", MS(b_[:], 0.0), writes=[b_])
                Vtmr = sbring(ph, "Vtmr", [128, D], BF16, 2)
                Khtmr = sbring(ph, "Khtmr", [128, D], BF16, 2)
                nBhtmr = sbring(ph, "nBhtmr", [128, D], BF16, 2)
                Uallr = sbring(ph, "Uallr", [128, D], BF16, 1)
                Tst = sb(ph, "Tst", [128, 8, 64], F32)
                Tbf = sb(ph, "Tbf", [128, 8, 64], BF16)
                dbgsem = S.new_dma_sem("st") if dbg else None
                LM = sb(ph, "LM", [128, 7, 512], BF16, dma=True)
                load(LM, LM[:], lmasks_d[d])
                NSL = 2 if fwd else 4
                slots = []
                for sl in range(NSL):
                    slots.append(dict(
                        nN=sb(ph, "cnN%d" % sl, [128, 512], BF16), nB=sbring(ph, "cnB%d_" % sl, [128, 512], BF16, 2),
                        IT=sbring(ph, "cIT%d_" % sl, [128, 512], BF16, 2),
                        M=sbring(ph, "cM%d_" % sl, [128, 512], BF16, 2), MT=sbring(ph, "cMT%d_" % sl, [128, 512], BF16, 2),
                        Aak=sb(ph, "cAak%d" % sl, [128, 512], BF16), nArb=sb(ph, "cnArb%d" % sl, [128, 512], BF16),
                        Ark=sb(ph, "cArk%d" % sl, [128, 512], BF16), Xbf=sb(ph, "cX%d" % sl, [128, 256], BF16)))
                pprep = Ring([psb(ph, "pprep%d" % i, [128, 512], F32) for i in range(2)])
                ptr = psb(ph, "ptrA", [128, 1024], BF16)
                pxo = Ring([psb(ph, "pxo%d" % i, [128, 512], F32) for i in range(2)])
                pch = Ring([psb(ph, "pch%d" % i, [128, 512], F32) for i in range(3)])

                for i in range(72):
                    S.op("pool" if i % 2 else "dve", TS(cdiag[:, i, :], identb[:], pcol("conv", i), None, ALU.mult, None),
                         reads=[identb, pc], writes=[cdiag])
                S.op("pool", MS(Tst[:], 0.0), writes=[Tst])
                S.op("pool", MS(Tbf[:], 0.0), writes=[Tbf])

                def v4(ap):
                    return ap.rearrange("p (i t) -> p i t", i=4)

                def prep(src, c, with_out, ctxd):
                    ntok = NCTX if src == "ctx" else NT
                    t0 = c * 128
                    rk = rkvraw.next()
                    lo, hi = max(t0 - 1, 0), min(t0 + 129, ntok)
                    a0 = lo - (t0 - 1)
                    for part in range(3):
                        load(rk, rk[:, 8 * part:8 * part + 8, a0:a0 + hi - lo],
                             projT[src][1024 * part:1024 * (part + 1), lo:hi].rearrange("(c p) t -> p c t", p=128))
                    if t0 == 0:
                        S.op("dve", MS(rk[:, :, 0:1], 0.0), writes=[rk])
                    if t0 + 128 == ntok:
                        S.op("dve", MS(rk[:, :, 129:130], 0.0), writes=[rk])
                    lr = lora.next()
                    load(lr, lr[:], projT[src][RO["lora"]:RO["lora"] + 256, t0:t0 + 128].rearrange("(c p) t -> p c t", p=128))
                    if fwd and with_out:
                        ga = gaTr.next()
                        load(ga, ga[:], projT[src][RO["ga"]:RO["ga"] + D, t0:t0 + 128].rearrange("(c p) t -> p c t", p=128))
                        obl = oblr.next()
                        load(obl, obl[:], obwd_a[t0:t0 + 128, :])
                        bbl = bblr.next()
                        load(bbl, bbl[:], bonb_d[:, t0:t0 + 128].rearrange("(c p) t -> p c t", p=128))
                        ctxd.update(ga=ga, obl=obl, bbl=bbl, bonf=bonfr.next())
                    if (not fwd) and with_out:
                        ctxd["bst"] = bstr.next()
                    thb = thr.next()
                    S.op("act", ACT(thb[:], lr[:, 0, :], AF.Tanh), reads=[lr], writes=[thb])
                    WCt = WCr.next()
                    ctxd["WC"] = WCt
                    ctxd["vS"] = []
                    yield
                    for hf in range(2):
                        fcs = [4 * hf + i for i in range(4)]
                        o = {n: outs[n][hf].next() for n in outs}
                        ctxd.setdefault("o", []).append(o)
                        vS = vSr.next()
                        ctxd["vS"].append(vS)
                        rS, kS, kk, rs, sg, a_, prefix, Dm, PT, b_, kd = (T[n] for n in
                            ("rS", "kS", "kk", "rs", "sg", "a", "prefix", "Dm", "PT", "b", "kd"))
                        DT = T.get("DT")
                        for (nm, base) in (("r", 0), ("k", 8), ("v", 16)):
                            if nm == "r" and not with_out:
                                continue
                            pcv = pprep.next()
                            for i, fc in enumerate(fcs):
                                blk = base + fc
                                for j in range(3):
                                    S.op("pe", MM(pcv[:, i * 128:(i + 1) * 128], cdiag[:, j * 24 + blk, :], rk[:, blk, j:j + 128],
                                                  start=(j == 0), stop=(j == 2)), reads=[cdiag, rk], writes=[pcv], inc=(j == 2 and i == 3))
                            if nm == "r":
                                S.op("act", ACT(rS[:], pcv[:], AF.Copy), reads=[pcv], writes=[rS])
                            elif nm == "k":
                                S.op("dve", CP(kS[:], pcv[:]), reads=[pcv], writes=[kS])
                            else:
                                S.op("act", ACT(vS[:], pcv[:], AF.Copy), reads=[pcv], writes=[vS])
                        yield
                        S.op("dve", TT(v4(kk[:]), v4(kS[:]), bc3(pc[:, PCO["k_k"] + 4 * hf:PCO["k_k"] + 4 * hf + 4], 128), ALU.mult),
                             reads=[kS, pc], writes=[kk])
                        S.op("act", ACT(sq[:], kk[:], AF.Square), reads=[kk], writes=[sq])
                        pss = pprep.next()
                        for i in range(4):
                            S.op("pe", MM(pss[:, i * 128:(i + 1) * 128], bones[:], sq[:, i * 128:(i + 1) * 128]), reads=[bones, sq], writes=[pss], inc=(i == 3))
                        S.op("act", ACT(rs[:], pss[:], AF.Sqrt, bias=1e-12), reads=[pss], writes=[rs])
                        S.op("dve", lambda e, rs=rs: e.reciprocal(out=rs[:], in_=rs[:]), reads=[rs], writes=[rs])
                        S.op("pool", TT(kk[:], kk[:], rs[:], ALU.mult), reads=[kk, rs], writes=[kk])
                        yield
                        pz = pprep.next()
                        for i, fc in enumerate(fcs):
                            S.op("pe", MM(pz[:, i * 128:(i + 1) * 128], wupb[pb:pb + 64, fc * 128:(fc + 1) * 128], thb[pb:pb + 64, :]),
                                 reads=[wupb, thb], writes=[pz], inc=(i == 3))
                        paz = pprep.next()
                        for i, fc in enumerate(fcs):
                            S.op("pe", MM(paz[:, i * 128:(i + 1) * 128], aupb[pb:pb + 64, fc * 128:(fc + 1) * 128], lr[pb:pb + 64, 1, :]),
                                 reads=[aupb, lr], writes=[paz], inc=(i == 3))
                        w0o = PCO["w0"] + d * 8 + 4 * hf
                        a0o = PCO["a0"] + d * 8 + 4 * hf
                        S.op("dve", TT(v4(sg[:]), v4(pz[:]), bc3(pc[:, w0o:w0o + 4], 128), ALU.add), reads=[pz, pc], writes=[sg])
                        S.op("act", ACT(sg[:], sg[:], AF.Tanh, scale=0.5), reads=[sg], writes=[sg])
                        S.op("dve", TS(sg[:], sg[:], 0.5, 0.5, ALU.mult, ALU.add), reads=[sg], writes=[sg])
                        S.op("dve", TT(v4(a_[:]), v4(paz[:]), bc3(pc[:, a0o:a0o + 4], 128), ALU.add), reads=[paz, pc], writes=[a_])
                        S.op("act", ACT(a_[:], a_[:], AF.Tanh, scale=0.5), reads=[a_], writes=[a_])
                        S.op("pool", TS(a_[:], a_[:], 0.5, 0.5, ALU.mult, ALU.add), reads=[a_], writes=[a_])
                        yield
                        for i in range(4):
                            S.op("dve", lambda e, i=i: e.tensor_tensor_scan(out=prefix[:, i * 128:(i + 1) * 128], data0=onesf[:], data1=sg[:, i * 128:(i + 1) * 128],
                                                                             initial=0.0, op0=ALU.mult, op1=ALU.add), reads=[onesf, sg], writes=[prefix])
                        S.op("dve", TT(Dm[:], sg[:], prefix[:], ALU.subtract), reads=[sg, prefix], writes=[Dm])
                        totb = v4(prefix[:])[:, :, 127:128].to_broadcast([128, 4, 128])
                        S.op("dve", TT(v4(PT[:]), v4(prefix[:]), totb, ALU.subtract), reads=[prefix], writes=[PT])
                        if not fwd:
                            S.op("dve", TT(v4(DT[:]), v4(Dm[:]), totb, ALU.add), reads=[Dm, prefix], writes=[DT])
                        S.op("act", ACT(WCt[:, 4 * hf:4 * hf + 4], v4(prefix[:])[:, :, 127], AF.Exp, scale=-KAPPA), reads=[prefix], writes=[WCt])
                        S.op("dve", TT(b_[:], kk[:], a_[:], ALU.mult), reads=[kk, a_], writes=[b_])
                        S.op("dve", TT(v4(kd[:]), v4(a_[:]), bc3(pc[:, PCO["k_a"] + 4 * hf:PCO["k_a"] + 4 * hf + 4], 128), ALU.mult), reads=[a_, pc], writes=[kd])
                        S.op("dve", TT(v4(kd[:]), v4(kd[:]), bc3(dcol[:, 16 + 4 * hf:20 + 4 * hf], 128), ALU.add), reads=[kd, dcol], writes=[kd])
                        S.op("pool", TT(kd[:], kS[:], kd[:], ALU.mult), reads=[kS, kd], writes=[kd])
                        yield
                        PM, DMm = T["PM"], T["DMm"]
                        if fwd:
                            midb = v4(prefix[:])[:, :, 63:64].to_broadcast([128, 4, 128])
                            S.op("dve", TT(v4(PM[:]), v4(prefix[:]), midb, ALU.subtract), reads=[prefix], writes=[PM])
                            S.op("dve", TT(v4(DMm[:]), v4(Dm[:]), midb, ALU.add), reads=[Dm, prefix], writes=[DMm])
                            e1, e3, e4 = (prefix, -KAPPA), (Dm, KAPPA), (PT, KAPPA)
                        else:
                            midb = v4(DT[:])[:, :, 64:65].to_broadcast([128, 4, 128])
                            S.op("dve", TT(v4(PM[:]), v4(DT[:]), midb, ALU.subtract), reads=[DT], writes=[PM])
                            S.op("dve", TT(v4(DMm[:]), v4(PT[:]), midb, ALU.add), reads=[PT, DT], writes=[DMm])
                            e1, e3, e4 = (DT, -KAPPA), (PT, KAPPA), (Dm, KAPPA)
                        if with_out:
                            E = Er.next()
                            S.op("act", ACT(E[:], e1[0][:], AF.Exp, scale=e1[1]), reads=[e1[0]], writes=[E])
                            S.op("pool", TT(o["RTe"][0:64, :], rS[0:64, :], E[0:64, :], ALU.mult), reads=[rS, E], writes=[o["RTe"]])
                            S.op("dve", TT(o["RTo"][64:128, :], rS[64:128, :], E[64:128, :], ALU.mult), reads=[rS, E], writes=[o["RTo"]])
                            E = Er.next()
                            S.op("act", ACT(E[:], PM[:], AF.Exp, scale=-KAPPA), reads=[PM], writes=[E])
                            S.op("pool", TT(o["RTse"][0:64, :], rS[0:64, :], E[0:64, :], ALU.mult), reads=[rS, E], writes=[o["RTse"]])
                            S.op("dve", TT(o["RTso"][64:128, :], rS[64:128, :], E[64:128, :], ALU.mult), reads=[rS, E], writes=[o["RTso"]])
                        E = Er.next()
                        S.op("act", ACT(E[:], PM[:], AF.Exp, scale=KAPPA), reads=[PM], writes=[E])
                        S.op("pool", TT(o["BT"][:], b_[:], E[:], ALU.mult), reads=[b_, E], writes=[o["BT"]])
                        S.op("pool", TT(o["KT"][:], kd[:], E[:], ALU.mult), reads=[kd, E], writes=[o["KT"]])
                        E = Er.next()
                        S.op("act", ACT(E[:], e3[0][:], AF.Exp, scale=e3[1]), reads=[e3[0]], writes=[E])
                        S.op("pool", TT(o["ATe"][0:64, :], kk[0:64, :], E[0:64, :], ALU.mult), reads=[kk, E], writes=[o["ATe"]])
                        S.op("dve", TT(o["ATo"][64:128, :], kk[64:128, :], E[64:128, :], ALU.mult), reads=[kk, E], writes=[o["ATo"]])
                        E = Er.next()
                        S.op("act", ACT(E[:], DMm[:], AF.Exp, scale=KAPPA), reads=[DMm], writes=[E])
                        S.op("pool", TT(o["ATse"][0:64, :], kk[0:64, :], E[0:64, :], ALU.mult), reads=[kk, E], writes=[o["ATse"]])
                        S.op("dve", TT(o["ATso"][64:128, :], kk[64:128, :], E[64:128, :], ALU.mult), reads=[kk, E], writes=[o["ATso"]])
                        E = Er.next()
                        S.op("act", ACT(E[:], e4[0][:], AF.Exp, scale=e4[1]), reads=[e4[0]], writes=[E])
                        S.op("pool", TT(o["KH"][:], kd[:], E[:], ALU.mult), reads=[kd, E], writes=[o["KH"]])
                        S.op("pool", TT(o["BH"][:], b_[:], E[:], ALU.mult), reads=[b_, E], writes=[o["BH"]])
                        yield
                        if with_out:
                            S.op("dve", TT(v4(rs[:]), v4(rS[:]), bc3(pc[:, PCO["r_k"] + 4 * hf:PCO["r_k"] + 4 * hf + 4], 128), ALU.mult), reads=[rS, pc], writes=[rs])
                            S.op("pool", TT(pr[:], rs[:], kd[:], ALU.mult), reads=[rs, kd], writes=[pr])
                            pbs = pprep.next()
                            for i in range(4):
                                S.op("pe", MM(pbs[:, i * 128:(i + 1) * 128], bones[:], pr[:, i * 128:(i + 1) * 128]), reads=[bones, pr], writes=[pbs], inc=(i == 3))
                            if fwd:
                                S.op("dve", TT(ctxd["bonf"][:, 4 * hf:4 * hf + 4, :], v4(pbs[:]), v4(vS[:]), ALU.mult), reads=[pbs, vS], writes=[ctxd["bonf"]])
                            else:
                                S.op("dve", TT(ctxd["bst"][:, 4 * hf:4 * hf + 4, :], v4(pbs[:]), v4(vS[:]), ALU.mult), reads=[pbs, vS], writes=[ctxd["bst"]])
                            yield
                    Vtm, Khtm, nBhtm = Vtmr.next(), Khtmr.next(), nBhtmr.next()
                    ctxd.update(Vtm=Vtm, Khtm=Khtm, nBhtm=nBhtm)
                    for which in range(3):
                        for fc in range(8):
                            hf, i = fc // 4, fc % 4
                            srcb = ctxd["vS"][hf] if which == 0 else ctxd["o"][hf]["KH" if which == 1 else "BH"]
                            S.op("pe", TR(ptr[:, fc * 128:(fc + 1) * 128], srcb[:, i * 128:(i + 1) * 128], identb[:]),
                                 reads=[srcb, identb], writes=[ptr], inc=(fc == 7))
                        if which == 0:
                            S.op("act", ACT(Vtm[:], ptr[:], AF.Copy), reads=[ptr], writes=[Vtm])
                        elif which == 1:
                            S.op("dve", CP(Khtm[:], ptr[:]), reads=[ptr], writes=[Khtm])
                        else:
                            S.op("act", ACT(nBhtm[:], ptr[:], AF.Copy, scale=-1.0), reads=[ptr], writes=[nBhtm])
                        yield
                    if (not fwd) and with_out:
                        store(ctxd["bst"], bonb_d[:, t0:t0 + 128].rearrange("(c p) t -> p c t", p=128), ctxd["bst"][:])

                def chain_group(g, sl, with_out, ctxd, Uall):
                    Vtm, Khtm, nBhtm = ctxd["Vtm"], ctxd["Khtm"], ctxd["nBhtm"]
                    heads = []
                    for hl in range(4):
                        fc = 2 * g + hl // 2
                        hf, i, p0 = fc // 4, fc % 4, 64 * (hl % 2)
                        o = ctxd["o"][hf]
                        sfx = "e" if p0 == 0 else "o"
                        cs_ = slice(i * 128, (i + 1) * 128)
                        heads.append(dict(fc=fc, p0=p0, h=4 * g + hl, o=o, sfx=sfx,
                                          at=o["AT" + sfx][:, cs_], ats=o["ATs" + sfx][:, cs_], rt=o["RT" + sfx][:, cs_], rts=o["RTs" + sfx][:, cs_],
                                          bt=o["BT"][:, cs_], kt=o["KT"][:, cs_]))

                    def prod(lname, rname, mask, dst, eng="dve"):
                        pp = pch.next()
                        for hl, H in enumerate(heads):
                            S.op("pe", MM(pp[:, hl * 128:(hl + 1) * 128], H[lname], H[rname]),
                                 reads=[H["o"]["ATs" + H["sfx"]], H["o"]["BT"], H["o"]["KT"]] + ([H["o"]["RTs" + H["sfx"]]] if with_out else []),
                                 writes=[pp], inc=(hl == 3))
                        S.op("dve", TT(dst[:], pp[:], masks[:, mask, :], ALU.mult), reads=[pp, masks], writes=[dst])

                    nN = sl["nN"]
                    prod("ats", "bt", mP, nN)
                    prod("kt", "ats", mAk, sl["Aak"])
                    if with_out:
                        prod("bt", "rts", mRb, sl["nArb"])
                        prod("kt", "rts", mRk, sl["Ark"])
                    yield
                    px = pxo.next()
                    for hl, H in enumerate(heads):
                        S.op("pe", MM(px[:, hl * 64:(hl + 1) * 64], H["at"], Tbf[:, H["fc"], :], start=True, stop=False),
                             reads=[H["o"]["AT" + H["sfx"]], Tbf], writes=[px], inc=False)
                        S.op("pe", MM(px[:, hl * 64:(hl + 1) * 64], sl["Aak"][:, hl * 128:(hl + 1) * 128], Vtm[:, H["h"] * 64:(H["h"] + 1) * 64], start=False, stop=True),
                             reads=[sl["Aak"], Vtm], writes=[px], inc=(hl == 3))
                    Xb = sl["Xbf"]
                    S.op("act", ACT(Xb[:], px[:, 0:256], AF.Copy), reads=[px], writes=[Xb])
                    yield
                    M, MT = ident4, ident4
                    for lev in range(7):
                        nB = sl["nB"].next()
                        S.op("pool", TT(nB[:], nN[:], LM[:, lev, :], ALU.mult), reads=[nN, LM], writes=[nB])
                        pb_ = pch.next()
                        for hl in range(4):
                            cs_ = slice(hl * 128, (hl + 1) * 128)
                            S.op("pe", MM(pb_[:, cs_], nB[:, cs_], MT[:, cs_]), reads=[nB, MT], writes=[pb_], inc=(hl == 3))
                        IT = sl["IT"].next()
                        S.op("dve", TT(IT[:], pb_[:], ident4[:], ALU.add), reads=[pb_, ident4], writes=[IT])
                        if lev == 0:
                            Mn = sl["M"].next()
                            S.op("pool", TT(Mn[:], nB[:], ident4[:], ALU.add), reads=[nB, ident4], writes=[Mn])
                            M, MT = Mn, IT
                            yield
                            continue
                        pmt = pch.next()
                        for hl in range(4):
                            cs_ = slice(hl * 128, (hl + 1) * 128)
                            S.op("pe", MM(pmt[:, cs_], M[:, cs_], IT[:, cs_]), reads=[M, IT], writes=[pmt], inc=(hl == 3))
                        if lev < 6:
                            pm_ = pch.next()
                            for hl in range(4):
                                cs_ = slice(hl * 128, (hl + 1) * 128)
                                S.op("pe", MM(pm_[:, cs_], IT[:, cs_], M[:, cs_]), reads=[M, IT], writes=[pm_], inc=(hl == 3))
                        MTn = sl["MT"].next()
                        S.op("dve", CP(MTn[:], pmt[:]), reads=[pmt], writes=[MTn])
                        if lev < 6:
                            Mn = sl["M"].next()
                            S.op("act", ACT(Mn[:], pm_[:], AF.Copy), reads=[pm_], writes=[Mn])
                            M = Mn
                        MT = MTn
                        yield
                    pu = pxo.next()
                    for hl in range(4):
                        S.op("pe", MM(pu[:, hl * 64:(hl + 1) * 64], MT[:, hl * 128:(hl + 1) * 128], Xb[:, hl * 64:(hl + 1) * 64]),
                             reads=[MT, Xb], writes=[pu], inc=(hl == 3))
                    S.op("act", ACT(Uall[:, g * 256:(g + 1) * 256], pu[:, 0:256], AF.Copy), reads=[pu], writes=[Uall])
                    yield
                    if with_out:
                        po = pxo.next()
                        for hl, H in enumerate(heads):
                            hc = slice(H["h"] * 64, (H["h"] + 1) * 64)
                            oc = po[:, hl * 64:(hl + 1) * 64]
                            S.op("pe", MM(oc, H["rt"], Tbf[:, H["fc"], :], start=True, stop=False),
                                 reads=[H["o"]["RT" + H["sfx"]], Tbf], writes=[po], inc=False)
                            S.op("pe", MM(oc, sl["nArb"][:, hl * 128:(hl + 1) * 128], Uall[:, hc], start=False, stop=False),
                                 reads=[sl["nArb"], Uall], writes=[po], inc=False)
                            S.op("pe", MM(oc, sl["Ark"][:, hl * 128:(hl + 1) * 128], Vtm[:, hc], start=False, stop=True),
                                 reads=[sl["Ark"], Vtm], writes=[po], inc=(hl == 3))
                        if fwd:
                            S.op("dve", TT(oa[:, g * 256:(g + 1) * 256], po[:, 0:256], ctxd["obl"][:, g * 256:(g + 1) * 256], ALU.add),
                                 reads=[po, ctxd["obl"]], writes=[oa])
                        else:
                            S.op("act", ACT(ctxd["ost"][:, g * 256:(g + 1) * 256], po[:, 0:256], AF.Copy), reads=[po], writes=[ctxd["ost"]])
                    yield

                def rr(gens):
                    gens = list(gens)
                    while gens:
                        for gq in list(gens):
                            try:
                                next(gq)
                            except StopIteration:
                                gens.remove(gq)
                        yield

                def chain(src, c, with_out, ctxd):
                    t0 = c * 128
                    Uall = Uallr.next()
                    if (not fwd) and with_out:
                        ctxd["ost"] = ostr.next()
                    for pair in (((0, 1), (2, 3)) if NSL == 2 else ((0, 1, 2, 3),)):
                        yield from rr([chain_group(g_, slots[i_], with_out, ctxd, Uall) for i_, g_ in enumerate(pair)])
                    pS = pprep.next()
                    Vtm, Khtm, nBhtm, WCt = ctxd["Vtm"], ctxd["Khtm"], ctxd["nBhtm"], ctxd["WC"]
                    for fc in range(8):
                        for hb in range(2):
                            h = 2 * fc + hb
                            hc = slice(h * 64, (h + 1) * 64)
                            oc = pS[64 * hb:64 * hb + 64, fc * 64:(fc + 1) * 64]
                            S.op("pe", MM(oc, Khtm[:, hc], Vtm[:, hc], start=True, stop=False), reads=[Khtm, Vtm], writes=[pS], inc=False)
                            S.op("pe", MM(oc, nBhtm[:, hc], Uall[:, hc], start=False, stop=True), reads=[nBhtm, Uall], writes=[pS],
                                 inc=(fc == 7 and hb == 1))
                    for fc in range(8):
                        S.op("dve", STT(Tst[:, fc, :], Tst[:, fc, :], WCt[:, fc:fc + 1], pS[:, fc * 64:(fc + 1) * 64], ALU.mult, ALU.add),
                             reads=[Tst, WCt, pS], writes=[Tst])
                    S.op("act", ACT(Tbf[:], Tst[:], AF.Copy), reads=[Tst], writes=[Tbf])
                    yield
                    if (not fwd) and with_out:
                        store(ctxd["ost"], obwd_a[t0:t0 + 128, :], ctxd["ost"][:])
                    if fwd and with_out:
                        st_ = stt_
                        if dbg:
                            S.dma("pool", dbgsem, dbg_oa[t0:t0 + 128, :], oa[:], reads=[oa], writes=[])
                        o3 = oa[:].rearrange("p (h v) -> p h v", h=16)
                        S.op("dve", lambda e: e.tensor_reduce(out=st_["s1"][:], in_=o3, axis=AX.X, op=ALU.add), reads=[oa], writes=[st_["s1"]])
                        S.op("act", ACT(sqt[:], oa[:], AF.Square), reads=[oa], writes=[sqt])
                        S.op("dve", lambda e: e.tensor_reduce(out=st_["s2"][:], in_=sqt[:].rearrange("p (h v) -> p h v", h=16), axis=AX.X, op=ALU.add),
                             reads=[sqt], writes=[st_["s2"]])
                        S.op("pool", TS(st_["mean"][:], st_["s1"][:], 1.0 / 64, None, ALU.mult, None), reads=[st_["s1"]], writes=[st_["mean"]])
                        S.op("pool", TT(st_["msq"][:], st_["mean"][:], st_["mean"][:], ALU.mult), reads=[st_["mean"]], writes=[st_["msq"]])
                        S.op("dve", STT(st_["var"][:], st_["s2"][:], 1.0 / 64, st_["msq"][:], ALU.mult, ALU.subtract), reads=[st_["s2"], st_["msq"]], writes=[st_["var"]])
                        S.op("act", ACT(st_["rstd"][:], st_["var"][:], AF.Sqrt, bias=64e-5), reads=[st_["var"]], writes=[st_["rstd"]])
                        S.op("dve", lambda e: e.reciprocal(out=st_["rstd"][:], in_=st_["rstd"][:]), reads=[st_["rstd"]], writes=[st_["rstd"]])
                        S.op("dve", STT(st_["nb"][:], st_["mean"][:], -1.0, st_["rstd"][:], ALU.mult, ALU.mult), reads=[st_["mean"], st_["rstd"]], writes=[st_["nb"]])
                        yield
                        s3 = sqt[:].rearrange("p (h v) -> p h v", h=16)
                        S.op("dve", TT(s3, o3, bc3(st_["rstd"][:], 64), ALU.mult), reads=[oa, st_["rstd"]], writes=[sqt])
                        S.op("dve", TT(ynb[:].rearrange("p (h v) -> p h v", h=16), s3, bc3(st_["nb"][:], 64), ALU.add), reads=[sqt, st_["nb"]], writes=[ynb])
                        for fc in range(8):
                            S.op("pe", TR(ptr[:, fc * 128:(fc + 1) * 128], ynb[:, fc * 128:(fc + 1) * 128], identb[:]), reads=[ynb, identb], writes=[ptr], inc=(fc == 7))
                        for fc in range(8):
                            S.op("act", ACT(sqt[:, fc * 128:(fc + 1) * 128], ptr[:, fc * 128:(fc + 1) * 128], AF.Identity, scale=pcol("a_ln_w", fc), bias=pcol("a_ln_b", fc)),
                                 reads=[ptr, pc], writes=[sqt])
                        yield
                        ga, bbl = ctxd["ga"], ctxd["bbl"]
                        g3 = sqt[:].rearrange("p (c t) -> p c t", c=8)
                        sg3 = oa[:].rearrange("p (c t) -> p c t", c=8)
                        S.op("pool", TT(g3, g3, ctxd["bonf"][:], ALU.add), reads=[sqt, ctxd["bonf"]], writes=[sqt])
                        S.op("pool", TT(g3, g3, bbl[:], ALU.add), reads=[sqt, bbl], writes=[sqt])
                        S.op("act", ACT(sg3, ga[:], AF.Tanh, scale=0.5), reads=[ga], writes=[oa])
                        S.op("pool", TS(oa[:], oa[:], 0.5, 0.5, ALU.mult, ALU.add), reads=[oa], writes=[oa])
                        S.op("pool", TT(sg3, sg3, ga[:], ALU.mult), reads=[oa, ga], writes=[oa])
                        GT = GTr.next()
                        S.op("dve", TT(GT[:], g3, sg3, ALU.mult), reads=[sqt, oa], writes=[GT])
                        store(GT, gall[0:D, t0:t0 + 128].rearrange("(c p) t -> p c t", p=128), GT[:])
                        yield

                order = [("ctx", 0, False), ("ctx", 1, False)] + [("lat", c, True) for c in range(NCH)]
                if not fwd:
                    order = [("ctx", 1, False), ("ctx", 0, False)] + [("lat", c, True) for c in range(NCH - 1, -1, -1)]
                ctxs = [dict() for _ in order]
                import os as _os
                budget = [int(_os.environ.get("KDBG_STEPS", "-1"))]

                def run(gq):
                    for _ in gq:
                        if budget[0] >= 0:
                            budget[0] -= 1
                            if budget[0] < 0:
                                return False
                    return True
                ok = budget[0] != 0 and run(prep(*order[0], ctxs[0]))
                for n in range(len(order)):
                    if not ok:
                        break
                    gens = [chain(*order[n], ctxs[n])]
                    if n + 1 < len(order):
                        gens.append(prep(*order[n + 1], ctxs[n + 1]))
                    if interleave and budget[0] < 0:
                        ok = run(rr(gens))
                    else:
                        for gq in gens:
                            ok = ok and run(gq)
                    ctxs[n].clear()
                S.barrier()
                S.replay()
                S.end_phase()

        def ret_sweep(d):
            fwd = (d == 0)
            mk = U_I if fwd else L_I
            with ExitStack() as ph:
                S.begin_phase()
                qTr = sbring(ph, "qTr", [128, 8, 128], BF16, 2, dma=True)
                kTr = sbring(ph, "kTr", [128, 8, 128], BF16, 2, dma=True)
                ktmr = sbring(ph, "ktmr", [128, D], BF16, 2, dma=True)
                vtmr = sbring(ph, "vtmr", [128, 2 * D], BF16, 2, dma=True)
                Mk = sb(ph, "Mk", [128, 512], F32)
                kdf = sb(ph, "kdf", [128, D], F32)
                Sst = [sb(ph, "Sst%d" % h, [128, 2, 512], F32) for h in range(4)]
                Sbf = [sb(ph, "Sbf%d" % h, [128, 2, 512], BF16) for h in range(4)]
                STr = sbring(ph, "STr", [128, 512], BF16, 2)
                Kdr = sbring(ph, "Kdr", [128, D], BF16, 2)
                if fwd:
                    grTr = sbring(ph, "grTr", [128, 16, 128], BF16, 2, dma=True)
                    oblr = sbring(ph, "roblr", [128, 2 * D], BF16, 2, dma=True)
                    orr = sb(ph, "orr", [128, 2 * D], F32)
                    sqr = sb(ph, "sqr", [128, 2 * D], F32)
                    ynr = sb(ph, "ynr", [128, 2 * D], BF16)
                    GrTr = sbring(ph, "GrTr", [128, 16, 128], BF16, 2, dma="st")
                    st_ = {n: sb(ph, "rst_" + n, [128, 4], F32) for n in ("s1", "s2", "mean", "msq", "var", "rstd", "nb")}
                    dbgsem = S.new_dma_sem("st") if dbg else None
                else:
                    ostr = sbring(ph, "rostr", [128, 2 * D], BF16, 2, dma="st")
                pst = Ring([psb(ph, "pst%d" % i, [128, 512], F32) for i in range(2)])
                pov = Ring([psb(ph, "pov%d" % i, [128, 512], F32) for i in range(2)])
                pss = Ring([psb(ph, "pss%d" % i, [128, 512], F32) for i in range(2)])
                ptr2 = [psb(ph, "ptrR%d" % i, [128, 1024], BF16) for i in range(2)]
                for h in range(4):
                    S.op("dve", TS(Mk[:, h * 128:(h + 1) * 128], masks[:, mk, 0:128], rtab[:, 0, 4 * d + h:4 * d + h + 1], None, ALU.mult, None),
                         reads=[masks, rtab], writes=[Mk])
                    for q2 in range(2):
                        S.op("act", ACT(kdf[:, h * 256 + q2 * 128:h * 256 + (q2 + 1) * 128], onesf[:], AF.Identity, scale=rtab[:, 2, 4 * d + h:4 * d + h + 1]),
                             reads=[onesf, rtab], writes=[kdf])
                    S.op("pool", MS(Sst[h][:], 0.0), writes=[Sst[h]])
                    S.op("pool", MS(Sbf[h][:], 0.0), writes=[Sbf[h]])

                def chunk(src, c, with_out):
                    t0 = c * 128
                    kT = kTr.next()
                    load(kT, kT[:], projT[src][RO["k"]:RO["k"] + D, t0:t0 + 128].rearrange("(c p) t -> p c t", p=128))
                    ktm = ktmr.next()
                    load(ktm, ktm[:], k_tm[src][t0:t0 + 128, :])
                    vtm = vtmr.next()
                    load(vtm, vtm[:], v_tm[src][t0:t0 + 128, :])
                    if with_out:
                        qT = qTr.next()
                        load(qT, qT[:], projT[src][RO["q"]:RO["q"] + D, t0:t0 + 128].rearrange("(c p) t -> p c t", p=128))
                        if fwd:
                            grT = grTr.next()
                            for part in range(2):
                                load(grT, grT[:, 8 * part:8 * part + 8, :],
                                     projT[src][RO["gr"] + D * part:RO["gr"] + D * (part + 1), t0:t0 + 128].rearrange("(c p) t -> p c t", p=128))
                            obl = oblr.next()
                            load(obl, obl[:], obwd_r[t0:t0 + 128, :])
                        else:
                            ost = ostr.next()
                        pS = pst.next()
                        for h in range(4):
                            for kc in range(2):
                                S.op("pe", MM(pS[:, h * 128:(h + 1) * 128], kT[:, 2 * h + kc, :], qT[:, 2 * h + kc, :], start=(kc == 0), stop=(kc == 1)),
                                     reads=[kT, qT], writes=[pS], inc=(h == 3 and kc == 1))
                        STm = STr.next()
                        S.op("dve", TT(STm[:], pS[:], Mk[:], ALU.mult), reads=[pS, Mk], writes=[STm])
                        for h in range(4):
                            po_ = pov.next()
                            S.op("pe", MM(po_[:], STm[:, h * 128:(h + 1) * 128], vtm[:, h * 512:(h + 1) * 512], start=True, stop=False),
                                 reads=[STm, vtm], writes=[po_], inc=False)
                            for kc in range(2):
                                S.op("pe", MM(po_[:], qT[:, 2 * h + kc, :], Sbf[h][:, kc, :], start=False, stop=(kc == 1)),
                                     reads=[qT, Sbf[h]], writes=[po_], inc=(kc == 1))
                            qd = rtab[:, 1, 4 * d + h:4 * d + h + 1]
                            if fwd:
                                S.op("dve", STT(orr[:, h * 512:(h + 1) * 512], po_[:], qd, obl[:, h * 512:(h + 1) * 512], ALU.mult, ALU.add),
                                     reads=[po_, rtab, obl], writes=[orr])
                            else:
                                S.op("act", ACT(ost[:, h * 512:(h + 1) * 512], po_[:], AF.Identity, scale=qd), reads=[po_, rtab], writes=[ost])
                        if not fwd:
                            store(ost, obwd_r[t0:t0 + 128, :], ost[:])
                    Kd = Kdr.next()
                    S.op("pool", TT(Kd[:], ktm[:], kdf[:], ALU.mult), reads=[ktm, kdf], writes=[Kd])
                    for h in range(4):
                        for kc in range(2):
                            ps_ = pss.next()
                            S.op("pe", MM(ps_[:], Kd[:, h * 256 + kc * 128:h * 256 + (kc + 1) * 128], vtm[:, h * 512:(h + 1) * 512]),
                                 reads=[Kd, vtm], writes=[ps_])
                            S.op("dve", STT(Sst[h][:, kc, :], Sst[h][:, kc, :], rtab[:, 3, 4 * d + h:4 * d + h + 1], ps_[:], ALU.mult, ALU.add),
                                 reads=[Sst[h], rtab, ps_], writes=[Sst[h]])
                        S.op("act", ACT(Sbf[h][:], Sst[h][:], AF.Copy), reads=[Sst[h]], writes=[Sbf[h]])
                    if fwd and with_out:
                        if dbg:
                            S.dma("pool", dbgsem, dbg_or[t0:t0 + 128, :], orr[:], reads=[orr], writes=[])
                        o3 = orr[:].rearrange("p (h v) -> p h v", h=4)
                        s3 = sqr[:].rearrange("p (h v) -> p h v", h=4)
                        S.op("dve", lambda e: e.tensor_reduce(out=st_["s1"][:], in_=o3, axis=AX.X, op=ALU.add), reads=[orr], writes=[st_["s1"]])
                        S.op("act", ACT(sqr[:], orr[:], AF.Square), reads=[orr], writes=[sqr])
                        S.op("dve", lambda e: e.tensor_reduce(out=st_["s2"][:], in_=s3, axis=AX.X, op=ALU.add), reads=[sqr], writes=[st_["s2"]])
                        S.op("pool", TS(st_["mean"][:], st_["s1"][:], 1.0 / 512, None, ALU.mult, None), reads=[st_["s1"]], writes=[st_["mean"]])
                        S.op("pool", TT(st_["msq"][:], st_["mean"][:], st_["mean"][:], ALU.mult), reads=[st_["mean"]], writes=[st_["msq"]])
                        S.op("dve", STT(st_["var"][:], st_["s2"][:], 1.0 / 512, st_["msq"][:], ALU.mult, ALU.subtract), reads=[st_["s2"], st_["msq"]], writes=[st_["var"]])
                        S.op("act", ACT(st_["rstd"][:], st_["var"][:], AF.Sqrt, bias=1e-5), reads=[st_["var"]], writes=[st_["rstd"]])
                        S.op("dve", lambda e: e.reciprocal(out=st_["rstd"][:], in_=st_["rstd"][:]), reads=[st_["rstd"]], writes=[st_["rstd"]])
                        S.op("dve", STT(st_["nb"][:], st_["mean"][:], -1.0, st_["rstd"][:], ALU.mult, ALU.mult), reads=[st_["mean"], st_["rstd"]], writes=[st_["nb"]])
                        S.op("dve", TT(s3, o3, bc3(st_["rstd"][:], 512), ALU.mult), reads=[orr, st_["rstd"]], writes=[sqr])
                        S.op("dve", TT(ynr[:].rearrange("p (h v) -> p h v", h=4), s3, bc3(st_["nb"][:], 512), ALU.add), reads=[sqr, st_["nb"]], writes=[ynr])
                        for fc in range(16):
                            pt = ptr2[fc // 8]
                            S.op("pe", TR(pt[:, (fc % 8) * 128:(fc % 8 + 1) * 128], ynr[:, fc * 128:(fc + 1) * 128], identb[:]), reads=[ynr, identb], writes=[pt],
                                 inc=(fc % 8 == 7))
                        for fc in range(16):
                            pt = ptr2[fc // 8]
                            S.op("act", ACT(sqr[:, fc * 128:(fc + 1) * 128], pt[:, (fc % 8) * 128:(fc % 8 + 1) * 128], AF.Identity,
                                            scale=pcol("r_ln_w", fc), bias=pcol("r_ln_b", fc)), reads=[pt, pc], writes=[sqr])
                        g3 = sqr[:].rearrange("p (c t) -> p c t", c=16)
                        sg3 = orr[:].rearrange("p (c t) -> p c t", c=16)
                        S.op("act", ACT(sg3, grT[:], AF.Tanh, scale=0.5), reads=[grT], writes=[orr])
                        S.op("pool", TS(orr[:], orr[:], 0.5, 0.5, ALU.mult, ALU.add), reads=[orr], writes=[orr])
                        S.op("pool", TT(sg3, sg3, grT[:], ALU.mult), reads=[orr, grT], writes=[orr])
                        GrT = GrTr.next()
                        S.op("dve", TT(GrT[:], g3, sg3, ALU.mult), reads=[sqr, orr], writes=[GrT])
                        for part in range(2):
                            store(GrT, gall[D * (1 + part):D * (2 + part), t0:t0 + 128].rearrange("(c p) t -> p c t", p=128), GrT[:, 8 * part:8 * part + 8, :])

                order = [("ctx", 0, False), ("ctx", 1, False)] + [("lat", c, True) for c in range(NCH)]
                if not fwd:
                    order = [("ctx", 1, False), ("ctx", 0, False)] + [("lat", c, True) for c in range(NCH - 1, -1, -1)]
                for o_ in order:
                    chunk(*o_)
                S.barrier()
                S.replay()
                S.end_phase()

        def phase6():
            TW = 256
            with ExitStack() as ph:
                S.begin_phase()
                fnw_row = sb(ph, "fnw_row", [128, D], F32, dma=True)
                load(fnw_row, fnw_row[:], fnw_d.partition_broadcast(128))
                awo = sb(ph, "awo", [128, 8, D], BF16)
                rwo = sb(ph, "rwo", [128, 16, D], BF16)
                wo = sb(ph, "wo", [128, 8, D], BF16)
                wf = sbring(ph, "wf6", [128, D], F32, 3, dma=True)
                n = 0
                for (src_d, dst, nk) in ((awo_d, awo, 8), (rwo_d, rwo, 16), (wo_d, wo, 8)):
                    for kc in range(nk):
                        f = wf.next()
                        load(f, f[:], src_d[kc * 128:(kc + 1) * 128, :])
                        e = ("dve", "act", "pool")[n % 3]
                        n += 1
                        if e == "act":
                            S.op("act", ACT(dst[:, kc, :], f[:], AF.Copy), reads=[f], writes=[dst])
                        else:
                            S.op(e, CP(dst[:, kc, :], f[:]), reads=[f], writes=[dst])
                gTr = sbring(ph, "gTr6", [128, 24, TW], BF16, 2, dma=True)
                mTr = sbring(ph, "mTr6", [128, 16, TW], BF16, 2, dma=True)
                xr = sbring(ph, "xr6", [128, D], F32, 3, dma=True)
                sm = sb(ph, "sm6", [128, 16, TW], F32)
                m1r = sbring(ph, "m1r", [128, TW], F32, 2)
                m2r = sbring(ph, "m2r", [128, TW], F32, 2)
                mgr = sbring(ph, "mgr", [128, 8, TW], BF16, 2)
                yor = sbring(ph, "yor", [128, D], F32, 2, dma="st")
                junk = sb(ph, "junk6", [128, D], BF16)
                ssr = sbring(ph, "ss6", [128, 1], F32, 2)
                pya = Ring([psb(ph, "pya%d" % i, [128, 512], F32) for i in range(2)])
                pyr = Ring([psb(ph, "pyr%d" % i, [128, 512], F32) for i in range(2)])
                pout = Ring([psb(ph, "pout%d" % i, [128, 512], F32) for i in range(3)])
                for t0 in range(0, NT, TW):
                    gT = gTr.next()
                    for part in range(3):
                        load(gT, gT[:, 8 * part:8 * part + 8, :], gall[D * part:D * (part + 1), t0:t0 + TW].rearrange("(c p) t -> p c t", p=128))
                    mT = mTr.next()
                    for part in range(2):
                        load(mT, mT[:, 8 * part:8 * part + 8, :],
                             projT["lat"][RO["ma"] + D * part:RO["ma"] + D * (part + 1), t0:t0 + TW].rearrange("(c p) t -> p c t", p=128))
                    S.op("act", ACT(sm[:], mT[:], AF.Tanh, scale=0.5), reads=[mT], writes=[sm])
                    S.op("pool", TS(sm[:], sm[:], 0.5, 0.5, ALU.mult, ALU.add), reads=[sm], writes=[sm])
                    mg = mgr.next()
                    for fo in range(8):
                        pa = pya.next()
                        for fc in range(8):
                            S.op("pe", MM(pa[:, 0:TW], awo[:, fc, fo * 128:(fo + 1) * 128], gT[:, fc, :], start=(fc == 0), stop=(fc == 7)),
                                 reads=[awo, gT], writes=[pa], inc=(fc == 7))
                        pr_ = pyr.next()
                        for fc in range(16):
                            S.op("pe", MM(pr_[:, 0:TW], rwo[:, fc, fo * 128:(fo + 1) * 128], gT[:, 8 + fc, :], start=(fc == 0), stop=(fc == 15)),
                                 reads=[rwo, gT], writes=[pr_], inc=(fc == 15))
                        m1 = m1r.next()
                        m2 = m2r.next()
                        S.op("dve", TT(m1[:], pa[:, 0:TW], sm[:, fo, :], ALU.mult), reads=[pa, sm], writes=[m1])
                        S.op("dve", TT(m2[:], pr_[:, 0:TW], sm[:, 8 + fo, :], ALU.mult), reads=[pr_, sm], writes=[m2])
                        S.op("pool", TT(mg[:, fo, :], m1[:], m2[:], ALU.add), reads=[m1, m2], writes=[mg])
                    for s_ in range(TW // 128):
                        tk = t0 + s_ * 128
                        xb = xr.next()
                        load(xb, xb[:], x_d[tk:tk + 128, :])
                        yo = yor.next()
                        for cb in range(2):
                            po_ = pout.next()
                            for fc in range(8):
                                S.op("pe", MM(po_[:], mg[:, fc, s_ * 128:(s_ + 1) * 128], wo[:, fc, cb * 512:(cb + 1) * 512], start=(fc == 0), stop=(fc == 7)),
                                     reads=[mg, wo], writes=[po_], inc=(fc == 7))
                            S.op("dve", TT(yo[:, cb * 512:(cb + 1) * 512], po_[:], gate_row[:, cb * 512:(cb + 1) * 512], ALU.mult),
                                 reads=[po_, gate_row], writes=[yo])
                        S.op("pool", TT(yo[:], yo[:], xb[:], ALU.add), reads=[yo, xb], writes=[yo])
                        ss = ssr.next()
                        S.op("act", ACT(junk[:], yo[:], AF.Square, accum=ss[:]), reads=[yo], writes=[junk, ss])
                        S.op("act", ACT(ss[:], ss[:], AF.Sqrt, scale=1.0 / D, bias=1e-6), reads=[ss], writes=[ss])
                        S.op("dve", lambda e, ss=ss: e.reciprocal(out=ss[:], in_=ss[:]), reads=[ss], writes=[ss])
                        S.op("dve", STT(yo[:], yo[:], ss[:], fnw_row[:], ALU.mult, ALU.mult), reads=[yo, ss, fnw_row], writes=[yo])
                        store(yo, y_d[tk:tk + 128, :], yo[:])
                S.barrier()
                S.replay()
                S.end_phase()

        import os as _os2
        phase0()
        if stop_after >= 1 and not _os2.environ.get("KDBG_SKIP_P1"):
            phase1()
        if stop_after >= 2:
            rwkv_sweep(1)
        if stop_after >= 3:
            rwkv_sweep(0)
        if stop_after >= 4:
            ret_sweep(1)
        if stop_after >= 5:
            ret_sweep(0)
        if stop_after >= 6:
            phase6()
        if stop_after < 6:
            with ExitStack() as ph:
                S.begin_phase()
                z = sb(ph, "zz", [128, D], F32, dma="st")
                S.op("pool", MS(z[:], 0.0), writes=[z])
                for i in range(NT // 128):
                    store(z, y_d[i * 128:(i + 1) * 128, :], z[:])
                S.barrier()
                S.replay()
                S.end_phase()
    return nc


def _cols(v):
    v = np.asarray(v, np.float32).reshape(-1)
    return np.ascontiguousarray(v.reshape(-1, 128).T)


def host_consts(NT):
    idx = np.arange(128)
    p = idx[:, None]
    f = idx[None, :]
    base = [(f > p), (f >= p), (f < p), (f <= p)]
    m = np.zeros((128, 8, 512), np.float32)
    for i, b in enumerate(base):
        m[:, i, :] = np.tile(b.astype(np.float32), (1, 4))
        m[:, 4 + i, :] = -m[:, i, :]
    bones = np.zeros((128, 128), np.float32)
    bones[:64, :64] = 1
    bones[64:, 64:] = 1
    j = idx.astype(np.float32)
    iota = np.stack([j + 1, -(j + 1), 127 - j, 128 - j, -(128 - j), j, 0 * j, 0 * j], 1).astype(np.float32)
    t = np.arange(NT)
    rows = (t // 64).astype(np.float64)
    cols = (t % 64).astype(np.float64)
    fr = 10000.0 ** (-np.arange(64, dtype=np.float64) / 64)
    cos = np.zeros((NT, 2, 2, 64), np.float32)
    sin = np.zeros((NT, 2, 2, 64), np.float32)
    for ty, pos in enumerate((rows, cols)):
        ang = (pos.astype(np.float32)[:, None] * fr.astype(np.float32)[None, :]).astype(np.float32)
        cos[:, ty, 0] = np.cos(ang)
        cos[:, ty, 1] = np.cos(ang)
        sin[:, ty, 0] = -np.sin(ang)
        sin[:, ty, 1] = np.sin(ang)
    lm = np.zeros((2, 128, 7, 512), np.float32)
    for j in range(7):
        bsz = 2 ** j
        q = ((p // (2 * bsz) == f // (2 * bsz)) & (p % (2 * bsz) >= bsz) & (f % (2 * bsz) < bsz)).astype(np.float32)
        lm[0, :, j, :] = np.tile(q, (1, 4))
        lm[1, :, j, :] = np.tile(q.T, (1, 4))
    return dict(masks=m.astype(ml_dtypes.bfloat16), identb=np.eye(128).astype(ml_dtypes.bfloat16),
                lmasks=lm.astype(ml_dtypes.bfloat16), ident4=np.tile(np.eye(128), (1, 4)).astype(ml_dtypes.bfloat16),
                blockones=bones.astype(ml_dtypes.bfloat16), iota=iota,
                rope_cos=cos.reshape(NT, 256), rope_sin=sin.reshape(NT, 256))


def make_in_maps(inputs, NT):
    f32 = lambda a: np.ascontiguousarray(np.asarray(a, np.float32))
    x = f32(inputs["x"])[:, :NT]
    B = x.shape[0]
    hc = host_consts(NT)
    shared = dict(
        ada_w=f32(inputs["ada_w"][0]), ada_b_gate=f32(inputs["ada_b"][0][2048:3072]),
        w_in=f32(inputs["w_in"][0]), a_w_up=f32(inputs["a_w_up"][0]).reshape(128, D),
        a_a_up=f32(inputs["a_a_up"][0]).reshape(128, D), a_w_out=f32(inputs["a_w_out"][0]),
        r_w_out=f32(inputs["r_w_out"][0]), w_out=f32(inputs["w_out"][0]),
        r_decay=f32(inputs["r_decay"][0]).reshape(8), final_norm_w=f32(inputs["final_norm_w"]), **hc)
    conv = f32(inputs["a_conv"][0])
    pcs = [_cols(inputs["norm_w"][0]), _cols(inputs["ada_b"][0])]
    pcs += [_cols(conv[j]) for j in range(3)]
    pcs += [_cols(inputs["a_w0"][0][d]) for d in range(2)]
    pcs += [_cols(inputs["a_a0"][0][d]) for d in range(2)]
    pcs += [_cols(inputs["a_k_k"][0]), _cols(inputs["a_k_a"][0]), _cols(inputs["a_r_k"][0]),
            _cols(inputs["a_ln_w"][0]), _cols(inputs["a_ln_b"][0]), _cols(inputs["r_ln_w"][0]), _cols(inputs["r_ln_b"][0])]
    maps = []
    for b in range(B):
        pcb = np.concatenate(pcs + [_cols(inputs["c"][b]), _cols(inputs["c_ctx"])], axis=1)
        assert pcb.shape == (128, NPC), pcb.shape
        m = dict(shared)
        m.update(x=np.ascontiguousarray(x[b]), ctx=f32(inputs["ctx"][b]), pcols=np.ascontiguousarray(pcb))
        maps.append(m)
    return maps


_NC_CACHE = {}


def kernel(**inputs):
    NT = inputs["x"].shape[1]
    if NT not in _NC_CACHE:
        _NC_CACHE[NT] = build(NT)
    nc = _NC_CACHE[NT]
    maps = make_in_maps(inputs, NT)
    res = run_bass_kernel_spmd(nc, maps, core_ids=list(range(len(maps))))
    return np.stack([np.asarray(r["y"], np.float32) for r in res.results], axis=0)
```
